# Optimizing a Trainium2 kernel written in Bass

```python
import jax
import jax.numpy as jnp
from jax import lax
import numpy as np

D_MODEL = 1024
BATCH = 4
SEQ = 8192
DEPTH = 4

CHUNK = 64
D_MIX = D_MODEL
HEAD_DIM = 64
GROUP_WIDTH = D_MIX // 4
N_GROUP_HEADS = GROUP_WIDTH // HEAD_DIM
ATTN_LEFT_CHUNKS = 8
ATTN_BAND = ATTN_LEFT_CHUNKS + 1
ATTN_MAX_REL = 128
LRU_CONV = 4
LRU_C = 8.0
RWKV_DECAY_RANK = 32
RWKV_ICLR_RANK = 32
RWKV_GATE_RANK = 64
RWKV_LN_EPS = 64e-5
SB_BLOCK = 128
D_FF = 2816
FFN_CONV = 3
NORM_EPS = 1e-6

P_ATTN = 3 * GROUP_WIDTH
P_LRU = 2 * GROUP_WIDTH
P_RWKV = 3 * GROUP_WIDTH + RWKV_DECAY_RANK + RWKV_ICLR_RANK + RWKV_GATE_RANK
P_SB = 3 * GROUP_WIDTH
P_TOTAL = P_ATTN + P_LRU + P_RWKV + P_SB

kernel_name = "hybrid_chunk_causal_encoder_trunk"


def rms_norm(x, g, eps=NORM_EPS):
    x32 = x.astype(jnp.float32)
    y = x32 * lax.rsqrt(jnp.mean(x32 * x32, axis=-1, keepdims=True) + eps)
    return (y * g.astype(jnp.float32)).astype(x.dtype)


def split_heads(t):
    return t.reshape(t.shape[:-1] + (-1, HEAD_DIM))


def causal_depthwise_conv(x, w, b):
    k = w.shape[0]
    y = lax.conv_general_dilated(
        x, w[:, None, :].astype(x.dtype), window_strides=(1,), padding=[(k - 1, 0)],
        dimension_numbers=("NWC", "WIO", "NWC"), feature_group_count=x.shape[-1])
    return y + b


def chunk_attention(q, k, v, q_gain, k_gain, rel_bias):
    bsz, seq, nh, dh = q.shape
    nc = seq // CHUNK
    band = ATTN_BAND * CHUNK
    q = rms_norm(q, q_gain).reshape(bsz, nc, CHUNK, nh, dh)
    k = rms_norm(k, k_gain)
    left = ((0, 0), (ATTN_LEFT_CHUNKS * CHUNK, 0), (0, 0), (0, 0))
    kp = jnp.pad(k, left).reshape(bsz, nc + ATTN_LEFT_CHUNKS, CHUNK, nh, dh)
    vp = jnp.pad(v, left).reshape(bsz, nc + ATTN_LEFT_CHUNKS, CHUNK, nh, dh)
    kb = jnp.concatenate([kp[:, i:i + nc] for i in range(ATTN_BAND)], axis=2)
    vb = jnp.concatenate([vp[:, i:i + nc] for i in range(ATTN_BAND)], axis=2)
    s = jnp.einsum("bnqhd,bnkhd->bnhqk", q.astype(jnp.float32), kb.astype(jnp.float32)) * (dh ** -0.5)
    dist = ATTN_LEFT_CHUNKS * CHUNK + jnp.arange(CHUNK)[:, None] - jnp.arange(band)[None, :]
    rel = jnp.clip(dist, -ATTN_MAX_REL, ATTN_MAX_REL) + ATTN_MAX_REL
    s = s + rel_bias.astype(jnp.float32)[:, rel]
    valid = ((jnp.arange(nc)[:, None] - ATTN_LEFT_CHUNKS) * CHUNK + jnp.arange(band)[None, :]) >= 0
    s = jnp.where(valid[None, :, None, None, :], s, -jnp.inf)
    p = jax.nn.softmax(s, axis=-1).astype(v.dtype)
    o = jnp.einsum("bnhqk,bnkhd->bnqhd", p, vb)
    return o.reshape(bsz, seq, nh * dh)


def _linear_combine(left, right):
    a_l, b_l = left
    a_r, b_r = right
    return a_l * a_r, a_r * b_l + b_r


def rglru_mixer(xb, gate, conv_w, conv_b, ra_w, ra_b, ri_w, ri_b, lam):
    bsz, seq, width = xb.shape
    xb = causal_depthwise_conv(xb, conv_w, conv_b)
    xh = xb.reshape(bsz, seq, N_GROUP_HEADS, HEAD_DIM)
    r_gate = jax.nn.sigmoid(jnp.einsum("bshi,hij->bshj", xh, ra_w).reshape(bsz, seq, width) + ra_b)
    i_gate = jax.nn.sigmoid(jnp.einsum("bshi,hij->bshj", xh, ri_w).reshape(bsz, seq, width) + ri_b)
    log_a = (-LRU_C * r_gate.astype(jnp.float32)) * jax.nn.softplus(-lam.astype(jnp.float32))
    a = jnp.exp(log_a)
    u = jnp.sqrt(-jnp.expm1(2.0 * log_a)) * (i_gate * xb).astype(jnp.float32)
    _, h = lax.associative_scan(_linear_combine, (a, u), axis=1)
    return h.astype(xb.dtype) * jax.nn.gelu(gate)


def rwkv7_mixer(p, mu, w0, w2, a0, a2, g2, k_k, k_a, r_k, ln_w, ln_b):
    bsz, seq, _ = p.shape
    gw, nh, n = GROUP_WIDTH, N_GROUP_HEADS, HEAD_DIM
    p_prev = jnp.pad(p, ((0, 0), (1, 0), (0, 0)))[:, :-1]
    p = p + (p_prev - p) * mu
    r, k, v = p[..., :gw], p[..., gw:2 * gw], p[..., 2 * gw:3 * gw]
    o1 = 3 * gw
    o2 = o1 + RWKV_DECAY_RANK
    o3 = o2 + RWKV_ICLR_RANK
    xw, xa, xg = p[..., o1:o2], p[..., o2:o3], p[..., o3:]
    w = -jax.nn.softplus(-(w0 + jnp.tanh(xw) @ w2)) - 0.5
    decay = jnp.exp(-jnp.exp(w.astype(jnp.float32)))
    a = jax.nn.sigmoid(a0 + xa @ a2)
    g = jax.nn.sigmoid(xg) @ g2
    kk = split_heads(k * k_k).astype(jnp.float32)
    kk = kk / jnp.maximum(jnp.linalg.norm(kk, axis=-1, keepdims=True), 1e-12)
    k = k * (1.0 + (a - 1.0) * k_a)
    rh = split_heads(r).astype(jnp.float32)
    kh = split_heads(k).astype(jnp.float32)
    vh = split_heads(v).astype(jnp.float32)
    ah = split_heads(a).astype(jnp.float32)
    dech = split_heads(decay)

    def time_major(t):
        return jnp.moveaxis(t, 1, 0)

    def step(state, inp):
        r_t, d_t, k_t, v_t, kk_t, a_t = inp
        sa = jnp.einsum("bhvk,bhk->bhv", state, -kk_t)
        state = (state * d_t[:, :, None, :] + sa[..., None] * (kk_t * a_t)[:, :, None, :]
                 + v_t[..., None] * k_t[:, :, None, :])
        return state, jnp.einsum("bhvk,bhk->bhv", state, r_t)

    state0 = jnp.zeros((bsz, nh, n, n), jnp.float32)
    _, y = lax.scan(step, state0, (time_major(rh), time_major(dech), time_major(kh),
                                   time_major(vh), time_major(kk), time_major(ah)))
    y = jnp.moveaxis(y, 0, 1)
    mean = jnp.mean(y, axis=-1, keepdims=True)
    var = jnp.mean(jnp.square(y - mean), axis=-1, keepdims=True)
    y = ((y - mean) * lax.rsqrt(var + RWKV_LN_EPS) * ln_w.astype(jnp.float32).reshape(nh, n)
         + ln_b.astype(jnp.float32).reshape(nh, n))
    bonus = jnp.sum(rh * kh * r_k.astype(jnp.float32).reshape(nh, n), axis=-1, keepdims=True) * vh
    y = (y + bonus).reshape(bsz, seq, gw).astype(p.dtype)
    return y * g


def stick_breaking(q, k, v):
    bsz, seq, nh, dh = q.shape
    nb = seq // SB_BLOCK
    k32 = k.astype(jnp.float32)
    v32 = v.astype(jnp.float32)
    kpos = jnp.arange(seq)
    qb = jnp.moveaxis(q.reshape(bsz, nb, SB_BLOCK, nh, dh), 1, 0)

    def block(args):
        q_blk, bi = args
        z = jnp.einsum("bqhd,bkhd->bhqk", q_blk.astype(jnp.float32), k32) * (dh ** -0.5)
        qpos = bi * SB_BLOCK + jnp.arange(SB_BLOCK)
        causal = kpos[None, :] < qpos[:, None]
        log_keep = jnp.where(causal, jax.nn.log_sigmoid(-z), 0.0)
        between = lax.cumsum(log_keep, axis=3, reverse=True) - log_keep
        weight = jnp.where(causal, jnp.exp(jax.nn.log_sigmoid(z) + between), 0.0)
        return jnp.einsum("bhqk,bkhd->bqhd", weight, v32)

    out = lax.map(block, (qb, jnp.arange(nb)))
    return jnp.moveaxis(out, 0, 1).reshape(bsz, seq, nh * dh).astype(q.dtype)


def conv_ffn(h, w_up, conv_w, conv_b, w_down):
    u = causal_depthwise_conv(h @ w_up, conv_w, conv_b)
    val, gate = jnp.split(u, 2, axis=-1)
    return (val * jax.nn.gelu(gate)) @ w_down


def setup_inputs(seed: int = 0) -> dict:
    key = jax.random.key(seed)
    keys = iter(jax.random.split(key, 48))
    f32 = jnp.float32
    L, D, G, H, N, F = DEPTH, D_MODEL, GROUP_WIDTH, N_GROUP_HEADS, HEAD_DIM, D_FF

    def normal(shape, scale):
        return jax.random.normal(next(keys), shape, f32) * scale

    def gain(shape, center=1.0):
        return center + 0.02 * jax.random.normal(next(keys), shape, f32)

    lru_a = jax.random.uniform(next(keys), (L, G), f32, 0.9, 0.999) ** (1.0 / LRU_C)
    return {
        "x": normal((BATCH, SEQ, D), 1.0),
        "c": normal((BATCH, D), 1.0),
        "ada_w": normal((L, D, 6 * D), 0.5 * D ** -0.5),
        "ada_b": normal((L, 6 * D), 0.02),
        "norm1_g": gain((L, D)),
        "norm2_g": gain((L, D)),
        "w_in": normal((L, D, P_TOTAL), D ** -0.5),
        "w_out": normal((L, D_MIX, D), D_MIX ** -0.5),
        "attn_q_gain": gain((L, N)),
        "attn_k_gain": gain((L, N)),
        "attn_rel_bias": normal((L, H, 2 * ATTN_MAX_REL + 1), 0.5),
        "lru_conv_w": normal((L, LRU_CONV, G), LRU_CONV ** -0.5),
        "lru_conv_b": normal((L, G), 0.02),
        "lru_ra_w": normal((L, H, N, N), N ** -0.5),
        "lru_ra_b": normal((L, G), 0.02),
        "lru_ri_w": normal((L, H, N, N), N ** -0.5),
        "lru_ri_b": normal((L, G), 0.02),
        "lru_lambda": jnp.log(lru_a) - jnp.log1p(-lru_a),
        "rwkv_mu": jax.random.uniform(next(keys), (L, P_RWKV), f32),
        "rwkv_w0": jax.random.uniform(next(keys), (L, G), f32, -6.0, -1.0),
        "rwkv_w2": normal((L, RWKV_DECAY_RANK, G), 0.5 * RWKV_DECAY_RANK ** -0.5),
        "rwkv_a0": normal((L, G), 0.1),
        "rwkv_a2": normal((L, RWKV_ICLR_RANK, G), RWKV_ICLR_RANK ** -0.5),
        "rwkv_g2": normal((L, RWKV_GATE_RANK, G), RWKV_GATE_RANK ** -0.5),
        "rwkv_k_k": gain((L, G), 0.85),
        "rwkv_k_a": gain((L, G)),
        "rwkv_r_k": normal((L, G), 0.1),
        "rwkv_ln_w": gain((L, G)),
        "rwkv_ln_b": normal((L, G), 0.02),
        "ffn_up": normal((L, D, 2 * F), D ** -0.5),
        "ffn_conv_w": normal((L, FFN_CONV, 2 * F), FFN_CONV ** -0.5),
        "ffn_conv_b": normal((L, 2 * F), 0.02),
        "ffn_down": normal((L, F, D), F ** -0.5),
    }


def reference(x, c, ada_w, ada_b, norm1_g, norm2_g, w_in, w_out,
              attn_q_gain, attn_k_gain, attn_rel_bias,
              lru_conv_w, lru_conv_b, lru_ra_w, lru_ra_b, lru_ri_w, lru_ri_b, lru_lambda,
              rwkv_mu, rwkv_w0, rwkv_w2, rwkv_a0, rwkv_a2, rwkv_g2, rwkv_k_k, rwkv_k_a,
              rwkv_r_k, rwkv_ln_w, rwkv_ln_b,
              ffn_up, ffn_conv_w, ffn_conv_b, ffn_down):
    cond = jax.nn.silu(c)
    cuts = [P_ATTN, P_ATTN + P_LRU, P_ATTN + P_LRU + P_RWKV]
    for l in range(DEPTH):
        mod = (cond @ ada_w[l] + ada_b[l])[:, None, :]
        sh_m, sc_m, gt_m, sh_f, sc_f, gt_f = jnp.split(mod, 6, axis=-1)
        h = rms_norm(x, norm1_g[l]) * (1.0 + sc_m) + sh_m
        proj = h @ w_in[l]
        pa, pb, pc, pd = jnp.split(proj, cuts, axis=-1)
        qa, ka, va = jnp.split(pa, 3, axis=-1)
        y_a = chunk_attention(split_heads(qa), split_heads(ka), split_heads(va),
                              attn_q_gain[l], attn_k_gain[l], attn_rel_bias[l])
        xb, gb = jnp.split(pb, 2, axis=-1)
        y_b = rglru_mixer(xb, gb, lru_conv_w[l], lru_conv_b[l], lru_ra_w[l], lru_ra_b[l],
                          lru_ri_w[l], lru_ri_b[l], lru_lambda[l])
        y_c = rwkv7_mixer(pc, rwkv_mu[l], rwkv_w0[l], rwkv_w2[l], rwkv_a0[l], rwkv_a2[l],
                          rwkv_g2[l], rwkv_k_k[l], rwkv_k_a[l], rwkv_r_k[l],
                          rwkv_ln_w[l], rwkv_ln_b[l])
        qd, kd, vd = jnp.split(pd, 3, axis=-1)
        y_d = stick_breaking(split_heads(qd), split_heads(kd), split_heads(vd))
        mixed = jnp.concatenate([y_a, y_b, y_c, y_d], axis=-1) @ w_out[l]
        x = x + gt_m * mixed
        h = rms_norm(x, norm2_g[l]) * (1.0 + sc_f) + sh_f
        x = x + gt_f * conv_ffn(h, ffn_up[l], ffn_conv_w[l], ffn_conv_b[l], ffn_down[l])
    return x
```

```python
import contextlib
import numpy as np
import concourse.bass as bass
import concourse.mybir as mybir
from concourse.bass_utils import run_bass_kernel_spmd

F32 = mybir.dt.float32
BF16 = mybir.dt.bfloat16
AF = mybir.ActivationFunctionType
ALU = mybir.AluOpType
AX = mybir.AxisListType

ENGS = ['pe', 'act', 'dve', 'pool', 'sp']

D_MODEL = 1024
DEPTH = 4
P_TOTAL = 2944
D_FF = 2816
NORM_EPS = 1e-6
RWKV_LN_EPS = 64e-5


class Buf:
    __slots__ = ('w', 'r', 'name')

    def __init__(self, name=''):
        self.w = None
        self.r = {}
        self.name = name


class Prog:
    def __init__(self, nc, n_ch=24):
        self.nc = nc
        self.ops = {e: [] for e in ENGS}
        self.cnt = {e: 0 for e in ENGS}
        self.seen = {e: {} for e in ENGS}
        self.n_ch = n_ch
        self.ch_val = [0] * n_ch
        self.ch_next = 0
        self.stack = contextlib.ExitStack()
        self.nbuf = 0

    @contextlib.contextmanager
    def scope(self):
        old = self.stack
        st = contextlib.ExitStack()
        self.stack = st
        try:
            yield
        finally:
            self.barrier()
            st.close()
            self.stack = old

    def sb(self, shape, dt=F32):
        self.nbuf += 1
        return self.stack.enter_context(self.nc.sbuf_tensor('t%d' % self.nbuf, list(shape), dt))

    def ps(self, shape, dt=F32):
        self.nbuf += 1
        return self.stack.enter_context(self.nc.psum_tensor('p%d' % self.nbuf, list(shape), dt))

    def _deps(self, eng, reads, writes):
        need = {}
        for b in reads:
            if b.w is not None:
                k, v = b.w
                if need.get(k, 0) < v:
                    need[k] = v
        for b in writes:
            if b.w is not None:
                k, v = b.w
                if need.get(k, 0) < v:
                    need[k] = v
            for k, v in b.r.items():
                if need.get(k, 0) < v:
                    need[k] = v
        waits = []
        sn = self.seen[eng]
        for k, v in need.items():
            if k == eng and eng == 'pe':
                continue
            if sn.get(k, 0) < v:
                waits.append((k, v))
                sn[k] = v
        return waits

    def _update(self, tok, reads, writes):
        for b in reads:
            if b.r.get(tok[0], 0) < tok[1]:
                b.r[tok[0]] = tok[1]
        for b in writes:
            b.w = tok
            b.r = {}

    def op(self, eng, fn, reads=(), writes=()):
        waits = self._deps(eng, reads, writes)
        self.cnt[eng] += 1
        tok = (eng, self.cnt[eng])
        self.ops[eng].append((waits, fn, (eng, 1)))
        self._update(tok, reads, writes)

    def dma(self, out_ap, in_ap, reads=(), writes=(), q='sp'):
        ch = self.ch_next
        self.ch_next = (ch + 1) % self.n_ch
        key = 'ch%d' % ch
        waits = self._deps(q, reads, writes)
        pv = self.ch_val[ch]
        if pv > 0 and self.seen[q].get(key, 0) < pv:
            waits.append((key, pv))
            self.seen[q][key] = pv
        self.ch_val[ch] += 16
        tok = (key, self.ch_val[ch])

        def fn(e, out_ap=out_ap, in_ap=in_ap):
            return e.dma_start(out=out_ap, in_=in_ap)
        self.ops[q].append((waits, fn, (key, 16)))
        self._update(tok, reads, writes)

    def barrier(self):
        allv = [(e, self.cnt[e]) for e in ENGS if self.cnt[e] > 0]
        allv += [('ch%d' % c, self.ch_val[c]) for c in range(self.n_ch) if self.ch_val[c] > 0]
        for e in ENGS:
            waits = []
            for k, v in allv:
                if k == e:
                    continue
                if self.seen[e].get(k, 0) < v:
                    waits.append((k, v))
                    self.seen[e][k] = v
            if waits:
                self.ops[e].append((waits, None, None))

    def emit(self):
        nc = self.nc
        self.barrier()
        keys = list(ENGS) + ['ch%d' % c for c in range(self.n_ch)]
        sems = {}
        for k in keys:
            sems[k] = self.stack.enter_context(nc.semaphore('s_' + k))
        ops = self.ops

        def run(e, name):
            for waits, fn, inc in ops[name]:
                for k, v in waits:
                    e.wait_ge(sems[k], v)
                if fn is not None:
                    fn(e).then_inc(sems[inc[0]], inc[1])

        with nc.Block() as block:
            @block.sync
            def _(e):
                run(e, 'sp')

            @block.tensor
            def _(e):
                run(e, 'pe')

            @block.scalar
            def _(e):
                run(e, 'act')

            @block.vector
            def _(e):
                run(e, 'dve')

            @block.gpsimd
            def _(e):
                run(e, 'pool')
        self.stack.close()


class KB:
    def __init__(self, nc, S, L):
        self.nc = nc
        self.P = Prog(nc)
        self.S = S
        self.L = L
        self.d = {}

    def din(self, name, shape, dt=F32):
        self.d[name] = self.nc.dram_tensor(name, list(shape), dt, kind="ExternalInput").ap()
        return self.d[name]

    def dout(self, name, shape, dt=F32):
        self.d[name] = self.nc.dram_tensor(name, list(shape), dt, kind="ExternalOutput").ap()
        return self.d[name]

    def dscr(self, name, shape, dt=F32):
        self.d[name] = self.nc.dram_tensor(name, list(shape), dt, kind="Internal").ap()
        return self.d[name]

    def act(self, out, in_, func, r, w, bias=None, scale=None, accum=None, eng='act'):
        kw = {}
        if bias is not None:
            kw['bias'] = bias
        if scale is not None:
            kw['scale'] = scale
        if accum is not None:
            kw['accum_out'] = accum
        self.P.op('act', lambda e: e.activation(out=out, in_=in_, func=func, **kw), reads=r, writes=w)

    def tt(self, out, in0, in1, op, r, w, eng='dve'):
        self.P.op(eng, lambda e: e.tensor_tensor(out=out, in0=in0, in1=in1, op=op), reads=r, writes=w)

    def ts(self, out, in0, s1, s2, op0, op1, r, w, eng='dve'):
        if op1 is None:
            self.P.op(eng, lambda e: e.tensor_scalar(out=out, in0=in0, scalar1=s1, scalar2=None, op0=op0), reads=r, writes=w)
        else:
            self.P.op(eng, lambda e: e.tensor_scalar(out=out, in0=in0, scalar1=s1, scalar2=s2, op0=op0, op1=op1), reads=r, writes=w)

    def stt(self, out, in0, scalar, in1, op0, op1, r, w):
        self.P.op('dve', lambda e: e.scalar_tensor_tensor(out=out, in0=in0, scalar=scalar, in1=in1, op0=op0, op1=op1), reads=r, writes=w)

    def cp(self, out, in_, r, w, eng='dve'):
        self.P.op(eng, lambda e: e.tensor_copy(out=out, in_=in_), reads=r, writes=w)

    def memset(self, ap, val, w, eng='dve'):
        self.P.op(eng, lambda e: e.memset(ap, val), writes=w)

    def recip(self, out, in_, r, w):
        self.P.op('dve', lambda e: e.reciprocal(out=out, in_=in_), reads=r, writes=w)

    def mm(self, out, lhsT, rhs, r, w, start=True, stop=True):
        self.P.op('pe', lambda e: e.matmul(out, lhsT=lhsT, rhs=rhs, start=start, stop=stop), reads=r, writes=w)

    def tr(self, out, in_, ident, r, w):
        self.P.op('pe', lambda e: e.transpose(out, in_, ident), reads=r, writes=w)

    def load(self, out, in_, w, r=(), q='sp'):
        self.P.dma(out, in_, reads=r, writes=w, q=q)

    def store(self, out, in_, r, w=(), q='sp'):
        self.P.dma(out, in_, reads=r, writes=w, q=q)


def load_consts(kb):
    P = kb.P
    c = {}
    cd = kb.d['consts']
    ct = P.sb([128, cd.shape[1]])
    b = Buf('consts')
    kb.load(ct[:], cd[:, :], [b])
    c['buf'] = b
    c['ident'] = ct[:, 0:128]
    c['bd'] = ct[:, 128:256]
    c['ones_col'] = ct[:, 256:257]
    c['m320'] = ct[0:64, 320:640]
    c['mdiag'] = ct[:, 640:768]
    c['ident64'] = ct[0:64, 0:64]
    c['mdiag_inv'] = ct[:, 768:896]
    cb = P.sb([128, 768], BF16)
    bb = Buf('constsb')
    kb.cp(cb[:], ct[:, 0:768], [b], [bb])
    c['bufb'] = bb
    c['ident_b'] = cb[:, 0:128]
    c['bd_b'] = cb[:, 128:256]
    c['mdiag_b'] = cb[:, 640:768]
    onesb = P.sb([128, 128], BF16)
    kb.memset(onesb[:], 1.0, [bb])
    c['ones_b'] = onesb
    kb.c = c


def make_consts():
    c = np.zeros((128, 896), np.float32)
    c[:, 0:128] = np.eye(128, dtype=np.float32)
    c[0:64, 128:192] = 1.0
    c[64:128, 192:256] = 1.0
    c[:, 256] = 1.0
    i = np.arange(64)[:, None]
    t = np.arange(64)[None, :]
    strict = (i < t).astype(np.float32)
    incl = (i <= t).astype(np.float32)
    c[0:64, 320:384] = strict
    c[0:64, 384:448] = incl
    c[0:64, 448:512] = strict
    c[0:64, 512:576] = incl
    c[0:64, 576:640] = (t < i).astype(np.float32)
    q = np.arange(128)[:, None]
    k = np.arange(128)[None, :]
    c[:, 640:768] = (k < q).astype(np.float32)
    c[:, 768:896] = (k >= q).astype(np.float32)
    return c


def phase_ada(kb, modT_dram):
    P = kb.P
    L = kb.L
    with P.scope():
        cT = P.sb([128, 8])
        bc = Buf()
        kb.load(cT[:], kb.d['cT'][:, :], [bc])
        sc2 = P.sb([128, 8, 2])
        bs = Buf()
        kb.act(sc2[:, :, 0], cT[:], AF.Silu, [bc], [bs])
        kb.act(sc2[:, :, 1], cT[:], AF.Silu, [bc], [bs])
        abT = P.sb([128, L, 48])
        bab = Buf()
        kb.load(abT[:], kb.d['ada_bT'].rearrange("l p j -> p l j"), [bab])
        wst = [P.sb([128, 8, 1024]) for _ in range(2)]
        bw = [Buf() for _ in range(2)]
        pm = P.ps([128, 128])
        bpm = Buf()
        mo = P.sb([128, L, 48])
        bmo = Buf()
        it = 0
        for l in range(L):
            for m in range(6):
                w = wst[it % 2]
                b = bw[it % 2]
                it += 1
                kb.load(w[:], kb.d['ada_w'][l, :, m * 1024:(m + 1) * 1024].rearrange("(k p) n -> p k n", p=128), [b])
                for j in range(8):
                    col = (m * 8 + j) * 2
                    for k in range(8):
                        kb.mm(pm[:, col:col + 2], w[:, k, j * 128:(j + 1) * 128], sc2[:, k, :], [b, bs], [bpm],
                              start=(k == 0), stop=(k == 7))
            pv = pm[:, 0:96].rearrange("p (j t) -> p j t", t=2)[:, :, 0]
            kb.tt(mo[:, l, :], pv, abT[:, l, :], ALU.add, [bpm, bab], [bmo])
        kb.store(modT_dram.rearrange("l p j -> p l j"), mo[:], [bmo])


def emit_norm(kb, xt, bx, n, gs, sh, bmod, sq, bsq, pss, bpss, rstd, brstd, tmp, btmp, h, bh):
    c = kb.c
    kb.act(sq[:, :, 0:n], xt[:, :, 0:n], AF.Square, [bx], [bsq])
    for k in range(8):
        kb.mm(pss[:, 0:n], c['ones_b'][:], sq[:, k, 0:n], [bsq, c['bufb']], [bpss], start=(k == 0), stop=(k == 7))
    kb.ts(rstd[:, 0:n], pss[:, 0:n], 1.0 / D_MODEL, NORM_EPS, ALU.mult, ALU.add, [bpss], [brstd])
    kb.act(rstd[:, 0:n], rstd[:, 0:n], AF.Ln, [brstd], [brstd])
    kb.act(rstd[:, 0:n], rstd[:, 0:n], AF.Exp, [brstd], [brstd], scale=-0.5)
    for k in range(8):
        kb.stt(tmp[:, k, 0:n], xt[:, k, 0:n], gs[:, k:k + 1], rstd[:, 0:n], ALU.mult, ALU.mult, [bx, bmod, brstd], [btmp[k % 2]])
        kb.act(h[:, k, 0:n], tmp[:, k, 0:n], AF.Identity, [btmp[k % 2], bmod], [bh], bias=sh[:, k:k + 1])


def load_cast_weight(kb, dst, bdst, src_rows, ncols, nk, stages, bst, it0=0):
    engs = ['act', 'pool', 'dve']
    it = it0
    for k in range(nk):
        src = src_rows(k)
        PW = stages[0].shape[1]
        for c0 in range(0, ncols, PW):
            c1 = min(ncols, c0 + PW)
            s = stages[it % len(stages)]
            b = bst[it % len(stages)]
            kb.load(s[:, 0:c1 - c0], src[:, c0:c1], [b])
            e = engs[it % 3]
            it += 1
            if e == 'act':
                kb.act(dst[:, k, c0:c1], s[:, 0:c1 - c0], AF.Copy, [b], [bdst])
            else:
                kb.cp(dst[:, k, c0:c1], s[:, 0:c1 - c0], [b], [bdst], eng=e)


def phase_T1(kb, l, XT, PT, modT):
    P = kb.P
    S = kb.S
    TN = 512
    with P.scope():
        Wb = P.sb([128, 8, P_TOTAL], BF16)
        bW = Buf()
        stages = [P.sb([128, 2048]) for _ in range(3)]
        bst = [Buf() for _ in range(3)]
        load_cast_weight(kb, Wb, bW, lambda k: kb.d['w_in'][l, k * 128:(k + 1) * 128, :], P_TOTAL, 8, stages, bst)
        modt = P.sb([128, 48])
        g1 = P.sb([128, 8])
        gs = P.sb([128, 8])
        bmod = Buf()
        kb.load(modt[:], modT[l], [bmod])
        kb.load(g1[:], kb.d['norm1_gT'][l], [bmod])
        kb.stt(gs[:], modt[:, 8:16], 1.0, g1[:], ALU.add, ALU.mult, [bmod], [bmod])
        sh = modt[:, 0:8]
        xts = [P.sb([128, 8, TN]) for _ in range(2)]
        bxs = [Buf() for _ in range(2)]
        sq = P.sb([128, 8, TN], BF16)
        bsq = Buf()
        pss = P.ps([128, TN])
        bpss = Buf()
        rstd = P.sb([128, TN])
        brstd = Buf()
        tmp = P.sb([128, 8, TN])
        btmp = [Buf(), Buf()]
        hs = [P.sb([128, 8, TN], BF16) for _ in range(2)]
        bhs = [Buf() for _ in range(2)]
        pps = [P.ps([128, TN]) for _ in range(4)]
        bpp = [Buf() for _ in range(4)]
        outs = [P.sb([128, TN]) for _ in range(4)]
        bo = [Buf() for _ in range(4)]
        io = 0
        for ti in range(S // TN):
            xt = xts[ti % 2]
            bx = bxs[ti % 2]
            h = hs[ti % 2]
            bh = bhs[ti % 2]
            kb.load(xt[:], XT[:, :, ti * TN:(ti + 1) * TN], [bx])
            emit_norm(kb, xt, bx, TN, gs, sh, bmod, sq, bsq, pss, bpss, rstd, brstd, tmp, btmp, h, bh)
            for c in range(23):
                pp = pps[io % 4]
                bp = bpp[io % 4]
                o = outs[io % 4]
                bob = bo[io % 4]
                for k in range(8):
                    kb.mm(pp[:], Wb[:, k, c * 128:(c + 1) * 128], h[:, k, :], [bW, bh], [bp], start=(k == 0), stop=(k == 7))
                if io % 2 == 0:
                    kb.cp(o[:], pp[:], [bp], [bob])
                else:
                    kb.act(o[:], pp[:], AF.Copy, [bp], [bob])
                kb.store(PT[c, :, ti * TN:(ti + 1) * TN], o[:], [bob], q='pool')
                io += 1


def phase_A(kb, l, PT, YT):
    P = kb.P
    S = kb.S
    c = kb.c
    NCH = S // 64
    with P.scope():
        pv = P.sb([128, 4])
        bpv = Buf()
        kb.load(pv[:, 0:2], kb.d['pvA'][l], [bpv])
        kb.ts(pv[:, 2:3], pv[:, 0:1], 0.125, None, ALU.mult, None, [bpv], [bpv])
        biasT = P.sb([64, 4, 9, 64])
        bbias = Buf()
        kb.load(biasT[:], kb.d['biasA'][l].rearrange("h k j q -> k h j q"), [bbias])
        qf = P.sb([128, S])
        kf = P.sb([128, S])
        vf = P.sb([128, S])
        bq, bk, bv = Buf(), Buf(), Buf()
        qn = P.sb([128, S], BF16)
        kn = P.sb([128, S], BF16)
        bqn, bkn = Buf(), Buf()
        vtm = P.sb([64, NCH, 128], BF16)
        bvtm = Buf()
        ya = P.sb([128, S], BF16)
        bya = Buf()
        sq = P.sb([128, 512], BF16)
        bsq = Buf()
        rs = P.sb([128, 512])
        brs = Buf()
        pA = [P.ps([128, 512]) for _ in range(2)]
        bpA = [Buf(), Buf()]
        pS = [P.ps([64, 16, 64]) for _ in range(2)]
        bpS = [Buf(), Buf()]
        pO = [P.ps([128, 128]) for _ in range(2)]
        bpO = [Buf(), Buf()]
        ssb = [P.sb([64, 9, 64]) for _ in range(2)]
        bss = [Buf(), Buf()]
        eb = [P.sb([64, 9, 64], BF16) for _ in range(4)]
        beb = [Buf() for _ in range(4)]
        rd = [P.sb([128, 64]) for _ in range(2)]
        brd = [Buf(), Buf()]
        for hp in range(2):
            kb.load(qf[:], PT[0 + hp], [bq])
            kb.load(kf[:], PT[2 + hp], [bk])
            kb.load(vf[:], PT[4 + hp], [bv])
            it = 0
            for (src, bs_, dst, bd_, gcol) in ((qf, bq, qn, bqn, 2), (kf, bk, kn, bkn, 1)):
                for ti in range(S // 512):
                    sl = slice(ti * 512, (ti + 1) * 512)
                    pa = pA[it % 2]
                    bpa = bpA[it % 2]
                    it += 1
                    kb.act(sq[:], src[:, sl], AF.Square, [bs_], [bsq])
                    kb.mm(pa[:], c['bd_b'], sq[:], [bsq, c['bufb']], [bpa])
                    kb.ts(rs[:], pa[:], 1.0 / 64, NORM_EPS, ALU.mult, ALU.add, [bpa], [brs])
                    kb.act(rs[:], rs[:], AF.Ln, [brs], [brs])
                    kb.act(rs[:], rs[:], AF.Exp, [brs], [brs], scale=-0.5)
                    kb.stt(dst[:, sl], src[:, sl], pv[:, gcol:gcol + 1], rs[:], ALU.mult, ALU.mult, [bs_, bpv, brs], [bd_])
            for g in range(NCH // 4):
                pa = pA[it % 2]
                bpa = bpA[it % 2]
                it += 1
                for j in range(4):
                    n = g * 4 + j
                    kb.tr(pa[0:64, j * 128:(j + 1) * 128], vf[:, n * 64:(n + 1) * 64], c['ident'], [bv, c['buf']], [bpa])
                kb.cp(vtm[:, g * 4:(g + 1) * 4, :], pa[0:64, :].rearrange("p (j c) -> p j c", c=128), [bpa], [bvtm])
            unitsA = [(h, n) for h in range(2) for n in range(NCH)]
            NU = len(unitsA)

            def A1(i):
                h, n = unitsA[i]
                hb = 64 * h
                hg = hp * 2 + h
                j0 = 9 - (min(n, 8) + 1)
                ps_, bps = pS[i % 2], bpS[i % 2]
                s_, bs_ = ssb[i % 2], bss[i % 2]
                e_, be = eb[i % 4], beb[i % 4]
                for jj in range(j0, 9):
                    j = n - (8 - jj)
                    kb.mm(ps_[:, jj, :], kn[hb:hb + 64, j * 64:(j + 1) * 64], qn[hb:hb + 64, n * 64:(n + 1) * 64],
                          [bkn, bqn], [bps])
                kb.tt(s_[:, j0:9, :], ps_[:, j0:9, :], biasT[:, hg, j0:9, :], ALU.add, [bps, bbias], [bs_])
                kb.act(e_[:, j0:9, :], s_[:, j0:9, :], AF.Exp, [bs_], [be])

            def A2(i):
                h, n = unitsA[i]
                hb = 64 * h
                j0 = 9 - (min(n, 8) + 1)
                e_, be = eb[i % 4], beb[i % 4]
                po, bpo = pO[i % 2], bpO[i % 2]
                for jj in range(j0, 9):
                    j = n - (8 - jj)
                    kb.mm(po[hb:hb + 64, 0:64], vtm[:, j, hb:hb + 64], e_[:, jj, :], [bvtm, be], [bpo],
                          start=(jj == j0), stop=(jj == 8))
                for jj in range(j0, 9):
                    kb.mm(po[hb:hb + 64, 64:128], c['ones_b'][0:64, 0:64], e_[:, jj, :], [be, c['bufb']], [bpo],
                          start=(jj == j0), stop=(jj == 8))

            def A3(i):
                h, n = unitsA[i]
                hb = 64 * h
                po, bpo = pO[i % 2], bpO[i % 2]
                r_, br = rd[i % 2], brd[i % 2]
                kb.recip(r_[hb:hb + 64, :], po[hb:hb + 64, 64:128], [bpo], [br])
                kb.tt(ya[hb:hb + 64, n * 64:(n + 1) * 64], po[hb:hb + 64, 0:64], r_[hb:hb + 64, :], ALU.mult, [bpo, br], [bya])

            for t in range(NU + 3):
                if t < NU:
                    A1(t)
                if 0 <= t - 2 < NU:
                    A2(t - 2)
                if 0 <= t - 3 < NU:
                    A3(t - 3)
            kb.store(YT[0 + hp], ya[:], [bya], q='pool')


def phase_B(kb, l, PT, YT):
    P = kb.P
    S = kb.S
    c = kb.c
    with P.scope():
        pv = P.sb([128, 2, 12])
        bpv = Buf()
        kb.load(pv[:, :, 0:8], kb.d['pvB'][l].rearrange("h p n -> p h n"), [bpv])
        wm = P.sb([128, 2, 2, 128])
        bwm = Buf()
        kb.load(wm[:], kb.d['wB'][l].rearrange("h g p n -> p h g n"), [bwm])
        wmb = P.sb([128, 2, 2, 128], BF16)
        kb.cp(wmb[:], wm[:], [bwm], [bwm])
        for hp in range(2):
            kb.act(pv[:, hp, 8:9], pv[:, hp, 7:8], AF.Exp, [bpv], [bpv], scale=-1.0)
            kb.ts(pv[:, hp, 8:9], pv[:, hp, 8:9], 1.0, None, ALU.add, None, [bpv], [bpv])
            kb.act(pv[:, hp, 8:9], pv[:, hp, 8:9], AF.Ln, [bpv], [bpv])
            kb.ts(pv[:, hp, 9:10], pv[:, hp, 8:9], -8.0, None, ALU.mult, None, [bpv], [bpv])
        xb = P.sb([128, S])
        bxb = Buf()
        xc = P.sb([128, S])
        bxc = Buf()
        uu = P.sb([128, S])
        buu = Buf()
        xcb = P.sb([128, 512], BF16)
        bxcb = Buf()
        pr = [P.ps([128, 512]) for _ in range(2)]
        bpr = [Buf(), Buf()]
        pi = [P.ps([128, 512]) for _ in range(2)]
        bpi = [Buf(), Buf()]
        rg = P.sb([128, 512])
        brg = Buf()
        ig = P.sb([128, 512])
        big = Buf()
        t1 = P.sb([128, 512])
        bt1 = Buf()
        yb = P.sb([128, S], BF16)
        byb = Buf()
        for hp in range(2):
            kb.load(xb[:], PT[6 + hp], [bxb])
            kb.act(xc[:], xb[:], AF.Identity, [bxb, bpv], [bxc], bias=pv[:, hp, 4:5], scale=pv[:, hp, 3:4])
            for i in range(3):
                sft = 3 - i
                kb.stt(xc[:, sft:S], xb[:, 0:S - sft], pv[:, hp, i:i + 1], xc[:, sft:S], ALU.mult, ALU.add, [bxb, bpv, bxc], [bxc])
            for ti in range(S // 512):
                sl = slice(ti * 512, (ti + 1) * 512)
                kb.cp(xcb[:], xc[:, sl], [bxc], [bxcb])
                kb.mm(pr[ti % 2][:], wmb[:, hp, 0, :], xcb[:], [bwm, bxcb], [bpr[ti % 2]])
                kb.mm(pi[ti % 2][:], wmb[:, hp, 1, :], xcb[:], [bwm, bxcb], [bpi[ti % 2]])
                kb.act(rg[:], pr[ti % 2][:], AF.Sigmoid, [bpr[ti % 2], bpv], [brg], bias=pv[:, hp, 5:6])
                kb.act(ig[:], pi[ti % 2][:], AF.Sigmoid, [bpi[ti % 2], bpv], [big], bias=pv[:, hp, 6:7])
                kb.act(xb[:, sl], rg[:], AF.Exp, [brg, bpv, bxc], [bxb], scale=pv[:, hp, 9:10])
                kb.tt(t1[:], xb[:, sl], xb[:, sl], ALU.mult, [bxb], [bt1])
                kb.ts(t1[:], t1[:], -1.0, 1.0, ALU.mult, ALU.add, [bt1], [bt1])
                kb.act(t1[:], t1[:], AF.Sqrt, [bt1], [bt1])
                kb.tt(ig[:], ig[:], xc[:, sl], ALU.mult, [big, bxc], [big])
                kb.tt(uu[:, sl], t1[:], ig[:], ALU.mult, [bt1, big], [buu])
            kb.P.op('dve', lambda e: e.tensor_tensor_scan(out=xc[:], data0=xb[:], data1=uu[:], initial=0.0,
                                                          op0=ALU.mult, op1=ALU.add), reads=[bxb, buu], writes=[bxc])
            kb.load(uu[:], PT[8 + hp], [buu])
            kb.act(uu[:], uu[:], AF.Gelu, [buu], [buu])
            kb.tt(yb[:], xc[:], uu[:], ALU.mult, [bxc, buu], [byb])
            kb.store(YT[2 + hp], yb[:], [byb], q='pool')


def phase_D(kb, l, PT, YT):
    P = kb.P
    S = kb.S
    c = kb.c
    NQ = S // 128
    WMAX = 1024
    NB = 3
    PC = min(2048, S)
    with P.scope():
        stg = P.sb([128, PC])
        bstg = Buf()
        qb = P.sb([128, S], BF16)
        kbf = P.sb([128, S], BF16)
        bqb, bkb = Buf(), Buf()
        vtm = P.sb([128, NQ, 128], BF16)
        bvtm = Buf()
        yd = P.sb([128, S], BF16)
        byd = Buf()
        pz = [P.ps([128, WMAX]) for _ in range(2)]
        bpz = [Buf(), Buf()]
        pt = [P.ps([128, WMAX], BF16) for _ in range(2)]
        bpt = [Buf(), Buf()]
        po = [P.ps([128, 512]) for _ in range(2)]
        bpo = [Buf(), Buf()]
        kp = [P.sb([128, WMAX]) for _ in range(NB)]
        bkp = [Buf() for _ in range(NB)]
        Pb = [P.sb([128, WMAX + 1]) for _ in range(NB)]
        bPb = [Buf() for _ in range(NB)]
        wt = [P.sb([128, WMAX], BF16) for _ in range(NB)]
        bwt = [Buf() for _ in range(NB)]
        wT = [P.sb([128, WMAX], BF16) for _ in range(3)]
        bwT = [Buf() for _ in range(3)]
        for hp in range(2):
            for t0 in range(0, S, PC):
                sl = slice(t0, t0 + PC)
                kb.load(stg[:], PT[17 + hp][:, sl], [bstg])
                kb.act(qb[:, sl], stg[:], AF.Copy, [bstg], [bqb], scale=0.125)
                kb.load(stg[:], PT[19 + hp][:, sl], [bstg])
                kb.cp(kbf[:, sl], stg[:], [bstg], [bkb])
            it = 0
            for t0 in range(0, S, PC):
                kb.load(stg[:], PT[21 + hp][:, t0:t0 + PC], [bstg])
                for g in range(PC // 512):
                    pa = pz[it % 2]
                    bpa = bpz[it % 2]
                    it += 1
                    for j in range(4):
                        kb.tr(pa[:, j * 128:(j + 1) * 128], stg[:, (g * 4 + j) * 128:(g * 4 + j + 1) * 128], c['ident'], [bstg, c['buf']], [bpa])
                    n0 = t0 // 128 + g * 4
                    kb.cp(vtm[:, n0:n0 + 4, :], pa[:, 0:512].rearrange("p (j c) -> p j c", c=128), [bpa], [bvtm])
            units = []
            io = 0
            for h in range(2):
                for qi in range(NQ):
                    kend = (qi + 1) * 128
                    nsb = (kend + WMAX - 1) // WMAX
                    for s in range(nsb - 1, -1, -1):
                        k0 = s * WMAX
                        k1 = min(kend, k0 + WMAX)
                        units.append(dict(h=h, qi=qi, k0=k0, W=k1 - k0, diag=(s == nsb - 1), first=(s == nsb - 1), last=(s == 0), io=io))
                    io += 1
            NU = len(units)

            def S1(i):
                u = units[i]
                hb = 64 * u['h']
                W = u['W']
                z, bz = pz[i % 2], bpz[i % 2]
                k_, bk_ = kp[i % NB], bkp[i % NB]
                for b0 in range(0, W, 512):
                    b1 = min(W, b0 + 512)
                    kb.mm(z[:, b0:b1], qb[hb:hb + 64, u['qi'] * 128:(u['qi'] + 1) * 128], kbf[hb:hb + 64, u['k0'] + b0:u['k0'] + b1],
                          [bqb, bkb], [bz])
                kb.act(k_[:, 0:W], z[:, 0:W], AF.Sigmoid, [bz], [bk_], scale=-1.0)
                if u['diag']:
                    kb.tt(k_[:, W - 128:W], k_[:, W - 128:W], c['mdiag_inv'], ALU.max, [bk_, c['buf']], [bk_])

            def S2(i):
                u = units[i]
                W = u['W']
                k_, bk_ = kp[i % NB], bkp[i % NB]
                p_, bp_ = Pb[i % NB], bPb[i % NB]
                if u['first']:
                    init = 1.0
                    rds = [bk_]
                    kb.memset(p_[:, W:W + 1], 1.0, [bp_], eng='pool')
                else:
                    pp_, bpp_ = Pb[(i - 1) % NB], bPb[(i - 1) % NB]
                    init = pp_[:, 0:1]
                    rds = [bk_, bpp_]
                    kb.act(p_[:, W:W + 1], pp_[:, 0:1], AF.Copy, [bpp_], [bp_])
                kb.P.op('dve', lambda e, p_=p_, k_=k_, W=W, init=init: e.tensor_tensor_scan(
                    out=p_[:, 0:W][:, ::-1], data0=k_[:, 0:W][:, ::-1], data1=c['ones_col'].broadcast_to([128, W]),
                    initial=init, op0=ALU.mult, op1=ALU.mult), reads=rds + [c['buf']], writes=[bp_])

            def S2b(i):
                u = units[i]
                W = u['W']
                p_, bp_ = Pb[i % NB], bPb[i % NB]
                w_, bw_ = wt[i % NB], bwt[i % NB]
                hW = 128 if W >= 256 else 0
                if hW > 0:
                    kb.tt(w_[:, 0:hW], p_[:, 1:hW + 1], p_[:, 0:hW], ALU.subtract, [bp_], [bw_])
                kb.tt(w_[:, hW:W], p_[:, hW + 1:W + 1], p_[:, hW:W], ALU.subtract, [bp_], [bw_], eng='pool')

            def S3a(i):
                u = units[i]
                W = u['W']
                w_, bw_ = wt[i % NB], bwt[i % NB]
                T_, bT_ = pt[i % 2], bpt[i % 2]
                wT_, bwT_ = wT[i % 3], bwT[i % 3]
                nblk = W // 128
                for bi in range(nblk):
                    kb.tr(T_[:, bi * 128:(bi + 1) * 128], w_[:, bi * 128:(bi + 1) * 128], c['ident_b'], [bw_, c['bufb']], [bT_])
                kb.act(wT_[:, 0:W], T_[:, 0:W], AF.Copy, [bT_], [bwT_])

            def S3b(i):
                u = units[i]
                hb = 64 * u['h']
                W = u['W']
                wT_, bwT_ = wT[i % 3], bwT[i % 3]
                pot, bpot = po[u['io'] % 2], bpo[u['io'] % 2]
                nblk = W // 128
                for bi in range(nblk):
                    kb.mm(pot[hb:hb + 64, 0:128], vtm[:, u['k0'] // 128 + bi, hb:hb + 64], wT_[:, bi * 128:(bi + 1) * 128],
                          [bvtm, bwT_], [bpot], start=(u['first'] and bi == 0), stop=(u['last'] and bi == nblk - 1))
                if u['last']:
                    kb.act(yd[hb:hb + 64, u['qi'] * 128:(u['qi'] + 1) * 128], pot[hb:hb + 64, 0:128], AF.Copy, [bpot], [byd])

            for t in range(NU + 4):
                if 0 <= t - 1 < NU:
                    S2(t - 1)
                if t < NU:
                    S1(t)
                if 0 <= t - 2 < NU:
                    S2b(t - 2)
                if 0 <= t - 3 < NU:
                    S3a(t - 3)
                if 0 <= t - 4 < NU:
                    S3b(t - 4)
            kb.store(YT[6 + hp], yd[:], [byd], q='pool')


def phase_C(kb, l, PT, YT):
    P = kb.P
    S = kb.S
    c = kb.c
    SEG = min(512, S)
    NC_ = SEG // 64
    NCH = NC_ * 2
    with P.scope():
        pv = P.sb([128, 2, 16])
        bpv = Buf()
        kb.load(pv[:, :, 0:11], kb.d['pvC'][l].rearrange("h p n -> p h n"), [bpv])
        for hp in range(2):
            kb.ts(pv[:, hp, 11:15], pv[:, hp, 0:4], -1.0, 1.0, ALU.mult, ALU.add, [bpv], [bpv])
            kb.ts(pv[:, hp, 15:16], pv[:, hp, 7:8], -1.0, 1.0, ALU.mult, ALU.add, [bpv], [bpv])
        wag = P.sb([128, 2, 128])
        bwag = Buf()
        kb.load(wag[:], kb.d['wC'][l].rearrange("h p n -> p h n"), [bwag])
        rmask = P.sb([128, SEG])
        brm = Buf()
        kb.memset(rmask[:], 1.0, [brm])
        kb.memset(rmask[:].rearrange("p (n c) -> p n c", c=64)[:, :, 0:1], 0.0, [brm])

        def T():
            return P.sb([128, SEG]), Buf()
        raw = [(P.sb([128, SEG + 1]), Buf()) for _ in range(4)]
        sh = [T() for _ in range(4)]
        (ld, bld), (al, bal), (gf, bgf), (kk, bkk), (km, bkm), (bon, bbon) = T(), T(), T(), T(), T(), T()
        (Lc, bLc), (t1, bt1), (t2, bt2), (bv_, bbv) = T(), T(), T(), T()
        AR = P.sb([128, NC_, 128])
        bAR = Buf()
        (btl, bbt), (ktl, bkt), (bh, bbh), (kh, bkh) = T(), T(), T(), T()
        BHt = P.sb([64, NC_, 128], BF16)
        KHt = P.sb([64, NC_, 128], BF16)
        Vt = P.sb([64, NC_, 128], BF16)
        ARb = P.sb([128, NC_, 128], BF16)
        bARb = Buf()
        AMb = P.sb([64, NCH, 192], BF16)
        bAMb = [Buf() for _ in range(NCH // 8)]
        Mtb = P.sb([64, NCH, 64], BF16)
        bMtb = [Buf() for _ in range(NCH // 8)]
        Tb = [P.sb([64, 2, 64], BF16) for _ in range(2)]
        bTb = [Buf(), Buf()]
        bBHt, bKHt, bVt = Buf(), Buf(), Buf()
        AM = P.sb([64, NCH, 320])
        bAM = [Buf() for _ in range(NCH)]
        Bm = [P.sb([64, NCH, 64]) for _ in range(2)]
        Btm = [P.sb([64, NCH, 64]) for _ in range(2)]
        Mt = [P.sb([64, NCH, 64]) for _ in range(2)]
        bBm = [[Buf() for _ in range(NCH // 8)] for _ in range(2)]
        bBtm = [[Buf() for _ in range(NCH // 8)] for _ in range(2)]
        bMt = [[Buf() for _ in range(NCH // 8)] for _ in range(2)]
        TS = [P.sb([64, 2, 64]) for _ in range(2)]
        AR1 = P.sb([64, NC_, 128], BF16)
        bAR1 = Buf()
        GCc = P.sb([128, NC_])
        bGCc = Buf()
        G1 = P.sb([64, NC_])
        bG1 = Buf()
        bTS = [Buf(), Buf()]
        Wsb = P.sb([64, 128], BF16)
        bWsb = Buf()
        Usb = P.sb([64, 128], BF16)
        bUsb = Buf()
        ysb = P.sb([64, 8, 64])
        bysb = Buf()
        ysq = P.sb([64, 8, 64])
        bysq = Buf()
        st1 = P.sb([64, 8])
        st2 = P.sb([64, 8])
        bst1, bst2 = Buf(), Buf()
        yc = P.sb([128, SEG], BF16)
        byc = Buf()
        yfm = P.sb([128, 256])
        byfm = Buf()
        pg = [P.ps([128, 512]) for _ in range(2)]
        bpg = [Buf(), Buf()]
        pinv = [P.ps([64, 512]) for _ in range(3)]
        bpinv = [Buf() for _ in range(3)]
        pW = P.ps([64, 128])
        bpW = Buf()
        pY = P.ps([64, 512])
        bpY = Buf()
        pT = P.ps([64, 128])
        bpT = Buf()
        ig = [0]

        def gbank():
            i = ig[0] % 2
            ig[0] += 1
            return pg[i], bpg[i]

        for hp in range(2):
            chans = [10 + hp, 12 + hp, 14 + hp, 16]
            kb.memset(TS[0][:], 0.0, [bTS[0]])
            kb.memset(Tb[0][:], 0.0, [bTb[0]])
            tsi = 0
            for sg in range(S // SEG):
                t0 = sg * SEG
                for i in range(4):
                    rt_, br_ = raw[i]
                    if t0 == 0:
                        kb.memset(rt_[:, 0:1], 0.0, [br_])
                        kb.load(rt_[:, 1:SEG + 1], PT[chans[i]][:, 0:SEG], [br_])
                    else:
                        kb.load(rt_[:, :], PT[chans[i]][:, t0 - 1:t0 + SEG], [br_])
                    s_, bs_ = sh[i]
                    mcol = 3 if i == 3 else i
                    kb.ts(s_[:], rt_[:, 1:SEG + 1], pv[:, hp, 11 + mcol:12 + mcol], None, ALU.mult, None, [br_, bpv], [bs_])
                    kb.stt(s_[:], rt_[:, 0:SEG], pv[:, hp, mcol:mcol + 1], s_[:], ALU.mult, ALU.add, [br_, bpv, bs_], [bs_])
                (rs, brs), (ks, bks), (vs, bvs), (xs, bxs) = sh
                kb.act(t1[0:32, :], xs[0:32, :], AF.Tanh, [bxs], [bt1])
                kb.act(t2[64:128, :], xs[64:128, :], AF.Sigmoid, [bxs], [bt2])
                for ti in range(SEG // 512):
                    sl = slice(ti * 512, (ti + 1) * 512)
                    p_, bp_ = gbank()
                    kb.mm(p_[:], wag[0:32, hp, :], t1[0:32, sl], [bwag, bt1], [bp_])
                    kb.act(ld[:, sl], p_[:], AF.Sigmoid, [bp_, bpv], [bld], bias=pv[:, hp, 4:5])
                    p_, bp_ = gbank()
                    kb.mm(p_[:], wag[32:64, hp, :], xs[32:64, sl], [bwag, bxs], [bp_])
                    kb.act(al[:, sl], p_[:], AF.Sigmoid, [bp_, bpv], [bal], bias=pv[:, hp, 5:6])
                    p_, bp_ = gbank()
                    kb.mm(p_[:], wag[64:128, hp, :], t2[64:128, sl], [bwag, bt2], [bp_])
                    kb.cp(gf[:, sl], p_[:], [bp_], [bgf])
                kb.ts(ld[:], ld[:], -0.6065306597126334, None, ALU.mult, None, [bld], [bld])
                kb.ts(kk[:], ks[:], pv[:, hp, 6:7], None, ALU.mult, None, [bks, bpv], [bkk])
                kb.tt(t1[:], kk[:], kk[:], ALU.mult, [bkk], [bt1])
                for ti in range(SEG // 512):
                    sl = slice(ti * 512, (ti + 1) * 512)
                    p_, bp_ = gbank()
                    kb.mm(p_[:], c['bd'], t1[:, sl], [c['buf'], bt1], [bp_])
                    kb.act(t2[:, sl], p_[:], AF.Sqrt, [bp_], [bt2])
                kb.ts(t2[:], t2[:], 1e-12, None, ALU.max, None, [bt2], [bt2])
                kb.recip(t2[:], t2[:], [bt2], [bt2])
                kb.tt(kk[:], kk[:], t2[:], ALU.mult, [bkk, bt2], [bkk])
                kb.ts(km[:], al[:], pv[:, hp, 7:8], pv[:, hp, 15:16], ALU.mult, ALU.add, [bal, bpv], [bkm])
                kb.tt(km[:], km[:], ks[:], ALU.mult, [bkm, bks], [bkm])
                kb.stt(t1[:], rs[:], pv[:, hp, 8:9], km[:], ALU.mult, ALU.mult, [brs, bpv, bkm], [bt1])
                for ti in range(SEG // 512):
                    sl = slice(ti * 512, (ti + 1) * 512)
                    p_, bp_ = gbank()
                    kb.mm(p_[:], c['bd'], t1[:, sl], [c['buf'], bt1], [bp_])
                    kb.tt(bon[:, sl], p_[:], vs[:, sl], ALU.mult, [bp_, bvs], [bbon])
                kb.P.op('dve', lambda e: e.tensor_tensor_scan(out=Lc[:], data0=rmask[:], data1=ld[:], initial=0.0,
                                                              op0=ALU.mult, op1=ALU.add), reads=[brm, bld], writes=[bLc])
                AR4 = AR[:].rearrange("p n (two c) -> p n two c", two=2)
                v3 = lambda t: t[:].rearrange("p (n c) -> p n c", c=64)
                kb.act(t1[:], Lc[:], AF.Exp, [bLc], [bt1])
                kb.tt(AR4[:, :, 1, :], v3(rs), v3(t1), ALU.mult, [brs, bt1], [bAR])
                kb.tt(t2[:], Lc[:], ld[:], ALU.subtract, [bLc, bld], [bt2])
                kb.act(t2[:], t2[:], AF.Exp, [bt2], [bt2])
                kb.stt(AR4[:, :, 0, :], v3(kk), -1.0, v3(t2), ALU.mult, ALU.mult, [bkk, bt2], [bAR])
                kb.tt(bv_[:], kk[:], al[:], ALU.mult, [bkk, bal], [bbv])
                kb.act(t2[:], Lc[:], AF.Exp, [bLc], [bt2], scale=-1.0)
                kb.tt(btl[:], bv_[:], t2[:], ALU.mult, [bbv, bt2], [bbt])
                kb.tt(ktl[:], km[:], t2[:], ALU.mult, [bkm, bt2], [bkt])
                LCb = v3(Lc)[:, :, 63:64].broadcast_to([128, NC_, 64])
                kb.tt(v3(t2), LCb, v3(Lc), ALU.subtract, [bLc], [bt2])
                kb.act(t2[:], t2[:], AF.Exp, [bt2], [bt2])
                kb.tt(bh[:], bv_[:], t2[:], ALU.mult, [bbv, bt2], [bbh])
                kb.tt(kh[:], km[:], t2[:], ALU.mult, [bkm, bt2], [bkh])
                kb.cp(GCc[:], v3(t1)[:, :, 63], [bt1], [bGCc])
                kb.load(G1[:], GCc[64:128, :], [bG1], r=[bGCc])
                kb.cp(ARb[:], AR[:], [bAR], [bARb])
                kb.load(AR1[:], ARb[64:128, :, :], [bAR1], r=[bARb])
                ARh = [ARb, AR1]
                bARh = [bARb, bAR1]
                GCh = [GCc, G1]
                bGCh = [bGCc, bG1]
                for (src, bsrc, dst, bdst) in ((bh, bbh, BHt, bBHt), (kh, bkh, KHt, bKHt), (vs, bvs, Vt, bVt)):
                    for g in range(NC_ // 4):
                        p_, bp_ = gbank()
                        for j in range(4):
                            n = g * 4 + j
                            kb.tr(p_[0:64, j * 128:(j + 1) * 128], src[:, n * 64:(n + 1) * 64], c['ident'], [bsrc, c['buf']], [bp_])
                        kb.cp(dst[:, g * 4:(g + 1) * 4, :], p_[0:64, :].rearrange("p (j c) -> p j c", c=128), [bp_], [bdst])
                for n in range(NC_):
                    for h in range(2):
                        hb = 64 * h
                        ci = n * 2 + h
                        p_, bp_ = gbank()
                        csl = slice(n * 64, (n + 1) * 64)
                        kb.mm(p_[0:64, 0:128], btl[hb:hb + 64, csl], AR[hb:hb + 64, n, :], [bbt, bAR], [bp_])
                        kb.mm(p_[0:64, 128:256], ktl[hb:hb + 64, csl], AR[hb:hb + 64, n, :], [bkt, bAR], [bp_])
                        kb.mm(p_[0:64, 256:320], AR[hb:hb + 64, n, 0:64], btl[hb:hb + 64, csl], [bbt, bAR], [bp_])
                        kb.tt(AM[:, ci, :], p_[0:64, 0:320], c['m320'], ALU.mult, [bp_, c['buf']], [bAM[ci]])
                for g in range(NCH // 8):
                    gs_ = slice(g * 8, (g + 1) * 8)
                    rb = [bAM[ci] for ci in range(g * 8, (g + 1) * 8)]
                    kb.cp(Bm[0][:, gs_, :], AM[:, gs_, 256:320], rb, [bBm[0][g]])
                    kb.cp(Btm[0][:, gs_, :], AM[:, gs_, 0:64], rb, [bBtm[0][g]], eng='pool')
                    kb.tt(Mt[0][:, gs_, :], AM[:, gs_, 0:64], c['ident64'].unsqueeze(1).broadcast_to([64, 8, 64]), ALU.add,
                          rb + [c['buf']], [bMt[0][g]])
                cur = 0
                for step in range(5):
                    nxt = 1 - cur
                    for g in range(NCH // 8):
                        gs_ = slice(g * 8, (g + 1) * 8)
                        pB, pBt, pM = pinv
                        for j in range(8):
                            ci = g * 8 + j
                            kb.mm(pB[:, j * 64:(j + 1) * 64], Btm[cur][:, ci, :], Bm[cur][:, ci, :], [bBtm[cur][g], bBm[cur][g]], [bpinv[0]])
                        if step < 4:
                            for j in range(8):
                                ci = g * 8 + j
                                kb.mm(pBt[:, j * 64:(j + 1) * 64], Bm[cur][:, ci, :], Btm[cur][:, ci, :], [bBtm[cur][g], bBm[cur][g]], [bpinv[1]])
                        kb.cp(Bm[nxt][:, gs_, :], pB[:].rearrange("p (j c) -> p j c", c=64), [bpinv[0]], [bBm[nxt][g]])
                        if step < 4:
                            kb.act(Btm[nxt][:, gs_, :], pBt[:].rearrange("p (j c) -> p j c", c=64), AF.Copy, [bpinv[1]], [bBtm[nxt][g]])
                        for j in range(8):
                            ci = g * 8 + j
                            kb.mm(pM[:, j * 64:(j + 1) * 64], Bm[nxt][:, ci, :], Mt[cur][:, ci, :], [bBm[nxt][g], bMt[cur][g]], [bpinv[2]])
                        kb.tt(Mt[nxt][:, gs_, :], Mt[cur][:, gs_, :], pM[:].rearrange("p (j c) -> p j c", c=64), ALU.add,
                              [bpinv[2], bMt[cur][g]], [bMt[nxt][g]])
                    cur = nxt
                MtF = Mtb
                bMtF = bMtb
                for g in range(NCH // 8):
                    gs_ = slice(g * 8, (g + 1) * 8)
                    kb.cp(Mtb[:, gs_, :], Mt[cur][:, gs_, :], [bMt[cur][g]], [bMtb[g]])
                    kb.act(AMb[:, gs_, :], AM[:, gs_, 64:256], AF.Copy, [bAM[ci] for ci in range(g * 8, (g + 1) * 8)], [bAMb[g]])
                for n in range(NC_):
                    Tc = TS[tsi % 2]
                    bTc = bTS[tsi % 2]
                    Tn = TS[(tsi + 1) % 2]
                    bTn = bTS[(tsi + 1) % 2]
                    Tcb, bTcb = Tb[tsi % 2], bTb[tsi % 2]
                    Tnb, bTnb = Tb[(tsi + 1) % 2], bTb[(tsi + 1) % 2]
                    tsi += 1
                    for h in range(2):
                        hb = 64 * h
                        ci = n * 2 + h
                        kb.mm(pW[:, h * 64:(h + 1) * 64], ARh[h][0:64, n, 0:64], Tcb[:, h, :], [bARh[h], bTcb], [bpW], start=True, stop=False)
                        kb.mm(pW[:, h * 64:(h + 1) * 64], AMb[:, ci, 64:128], Vt[:, n, hb:hb + 64], [bAMb[ci // 8], bVt], [bpW], start=False, stop=True)
                    kb.cp(Wsb[:], pW[:], [bpW], [bWsb])
                    for h in range(2):
                        ci = n * 2 + h
                        kb.mm(pW[:, h * 64:(h + 1) * 64], MtF[:, ci, :], Wsb[:, h * 64:(h + 1) * 64], [bMtF[ci // 8], bWsb], [bpW])
                    kb.cp(Usb[:], pW[:], [bpW], [bUsb])
                    q4 = n % 4
                    for h in range(2):
                        hb = 64 * h
                        ci = n * 2 + h
                        ysl = slice((q4 * 2 + h) * 64, (q4 * 2 + h + 1) * 64)
                        kb.mm(pY[:, ysl], ARh[h][0:64, n, 64:128], Tcb[:, h, :], [bARh[h], bTcb], [bpY], start=True, stop=False)
                        kb.mm(pY[:, ysl], AMb[:, ci, 0:64], Usb[:, h * 64:(h + 1) * 64], [bAMb[ci // 8], bUsb], [bpY], start=False, stop=False)
                        kb.mm(pY[:, ysl], AMb[:, ci, 128:192], Vt[:, n, hb:hb + 64], [bAMb[ci // 8], bVt], [bpY], start=False, stop=True)
                    for h in range(2):
                        hb = 64 * h
                        kb.mm(pT[:, h * 64:(h + 1) * 64], BHt[:, n, hb:hb + 64], Usb[:, h * 64:(h + 1) * 64], [bBHt, bUsb], [bpT], start=True, stop=False)
                        kb.mm(pT[:, h * 64:(h + 1) * 64], KHt[:, n, hb:hb + 64], Vt[:, n, hb:hb + 64], [bKHt, bVt], [bpT], start=False, stop=True)
                    for h in range(2):
                        kb.stt(Tnb[:, h, :], Tc[:, h, :], GCh[h][0:64, n:n + 1], pT[:, h * 64:(h + 1) * 64], ALU.mult, ALU.add,
                               [bTc, bGCh[h], bpT], [bTnb])
                    for h in range(2):
                        kb.stt(Tn[:, h, :], Tc[:, h, :], GCh[h][0:64, n:n + 1], pT[:, h * 64:(h + 1) * 64], ALU.mult, ALU.add,
                               [bTc, bGCh[h], bpT], [bTn])
                    if q4 == 3:
                        kb.act(ysb[:], pY[:].rearrange("p (g v) -> p g v", v=64), AF.Copy, [bpY], [bysb])
                        kb.P.op('dve', lambda e: e.tensor_reduce(out=st1[:], in_=ysb[:], axis=AX.X, op=ALU.add), reads=[bysb], writes=[bst1])
                        kb.ts(st1[:], st1[:], 1.0 / 64, None, ALU.mult, None, [bst1], [bst1])
                        kb.tt(ysb[:], ysb[:], st1[:].unsqueeze(2).broadcast_to([64, 8, 64]), ALU.subtract, [bysb, bst1], [bysb])
                        kb.act(ysq[:], ysb[:], AF.Square, [bysb], [bysq])
                        kb.P.op('dve', lambda e: e.tensor_reduce(out=st2[:], in_=ysq[:], axis=AX.X, op=ALU.add), reads=[bysq], writes=[bst2])
                        kb.ts(st2[:], st2[:], 1.0 / 64, RWKV_LN_EPS, ALU.mult, ALU.add, [bst2], [bst2])
                        kb.act(st2[:], st2[:], AF.Sqrt, [bst2], [bst2])
                        kb.recip(st2[:], st2[:], [bst2], [bst2])
                        kb.tt(ysb[:], ysb[:], st2[:].unsqueeze(2).broadcast_to([64, 8, 64]), ALU.mult, [bysb, bst2], [bysb])
                        p_, bp_ = gbank()
                        for j in range(4):
                            kb.tr(p_[:, j * 64:(j + 1) * 64], ysb[:, 2 * j:2 * j + 2, :].rearrange("p g v -> p (g v)"), c['ident64'],
                                  [bysb, c['buf']], [bp_])
                        tsl = slice((n - 3) * 64, (n + 1) * 64)
                        kb.ts(yfm[:], p_[:, 0:256], pv[:, hp, 9:10], pv[:, hp, 10:11], ALU.mult, ALU.add, [bp_, bpv], [byfm])
                        kb.tt(yfm[:], yfm[:], bon[:, tsl], ALU.add, [byfm, bbon], [byfm])
                        kb.tt(yc[:, tsl], yfm[:], gf[:, tsl], ALU.mult, [byfm, bgf], [byc])
                kb.store(YT[4 + hp][:, t0:t0 + SEG], yc[:], [byc], q='pool')


def gen_D(kb, l, PT, YT):
    P = kb.P
    S = kb.S
    c = kb.c
    NQ = S // 128
    WMAX = 1024
    NB = 3
    PC = min(512, S)
    with contextlib.nullcontext():
        stg = P.sb([128, PC])
        bstg = Buf()
        qb = P.sb([128, S], BF16)
        kbf = P.sb([128, S], BF16)
        bqb, bkb = Buf(), Buf()
        vtm = P.sb([128, NQ, 128], BF16)
        bvtm = Buf()
        yd = P.sb([128, S], BF16)
        byd = Buf()
        pz = [P.ps([128, WMAX])]
        bpz = [Buf()]
        pt = [P.ps([128, WMAX], BF16)]
        bpt = [Buf()]
        po = [P.ps([128, 512])]
        bpo = [Buf()]
        kp = [P.sb([128, WMAX]) for _ in range(NB)]
        bkp = [Buf() for _ in range(NB)]
        Pb = [P.sb([128, WMAX + 1]) for _ in range(NB)]
        bPb = [Buf() for _ in range(NB)]
        wt = [P.sb([128, WMAX], BF16) for _ in range(NB)]
        bwt = [Buf() for _ in range(NB)]
        wT = [P.sb([128, WMAX], BF16) for _ in range(2)]
        bwT = [Buf(), Buf()]
        for hp in range(2):
            for t0 in range(0, S, PC):
                sl = slice(t0, t0 + PC)
                kb.load(stg[:], PT[17 + hp][:, sl], [bstg])
                kb.act(qb[:, sl], stg[:], AF.Copy, [bstg], [bqb], scale=0.125)
                kb.load(stg[:], PT[19 + hp][:, sl], [bstg])
                kb.cp(kbf[:, sl], stg[:], [bstg], [bkb])
            it = 0
            for t0 in range(0, S, PC):
                kb.load(stg[:], PT[21 + hp][:, t0:t0 + PC], [bstg])
                for g in range(PC // 512):
                    pa = pz[0]
                    bpa = bpz[0]
                    it += 1
                    for j in range(4):
                        kb.tr(pa[:, j * 128:(j + 1) * 128], stg[:, (g * 4 + j) * 128:(g * 4 + j + 1) * 128], c['ident'], [bstg, c['buf']], [bpa])
                    n0 = t0 // 128 + g * 4
                    kb.cp(vtm[:, n0:n0 + 4, :], pa[:, 0:512].rearrange("p (j c) -> p j c", c=128), [bpa], [bvtm])
            units = []
            io = 0
            for h in range(2):
                for qi in range(NQ):
                    kend = (qi + 1) * 128
                    nsb = (kend + WMAX - 1) // WMAX
                    for s in range(nsb - 1, -1, -1):
                        k0 = s * WMAX
                        k1 = min(kend, k0 + WMAX)
                        units.append(dict(h=h, qi=qi, k0=k0, W=k1 - k0, diag=(s == nsb - 1), first=(s == nsb - 1), last=(s == 0), io=io))
                    io += 1
            NU = len(units)

            def S1(i):
                u = units[i]
                hb = 64 * u['h']
                W = u['W']
                z, bz = pz[0], bpz[0]
                k_, bk_ = kp[i % NB], bkp[i % NB]
                for b0 in range(0, W, 512):
                    b1 = min(W, b0 + 512)
                    kb.mm(z[:, b0:b1], qb[hb:hb + 64, u['qi'] * 128:(u['qi'] + 1) * 128], kbf[hb:hb + 64, u['k0'] + b0:u['k0'] + b1],
                          [bqb, bkb], [bz])
                kb.act(k_[:, 0:W], z[:, 0:W], AF.Sigmoid, [bz], [bk_], scale=-1.0)
                if u['diag']:
                    kb.tt(k_[:, W - 128:W], k_[:, W - 128:W], c['mdiag_inv'], ALU.max, [bk_, c['buf']], [bk_])

            def S2(i):
                u = units[i]
                W = u['W']
                k_, bk_ = kp[i % NB], bkp[i % NB]
                p_, bp_ = Pb[i % NB], bPb[i % NB]
                if u['first']:
                    init = 1.0
                    rds = [bk_]
                    kb.memset(p_[:, W:W + 1], 1.0, [bp_], eng='pool')
                else:
                    pp_, bpp_ = Pb[(i - 1) % NB], bPb[(i - 1) % NB]
                    init = pp_[:, 0:1]
                    rds = [bk_, bpp_]
                    kb.act(p_[:, W:W + 1], pp_[:, 0:1], AF.Copy, [bpp_], [bp_])
                kb.P.op('dve', lambda e, p_=p_, k_=k_, W=W, init=init: e.tensor_tensor_scan(
                    out=p_[:, 0:W][:, ::-1], data0=k_[:, 0:W][:, ::-1], data1=c['ones_col'].broadcast_to([128, W]),
                    initial=init, op0=ALU.mult, op1=ALU.mult), reads=rds + [c['buf']], writes=[bp_])

            def S2b(i):
                u = units[i]
                W = u['W']
                p_, bp_ = Pb[i % NB], bPb[i % NB]
                w_, bw_ = wt[i % NB], bwt[i % NB]
                hW = 128 if W >= 256 else 0
                if hW > 0:
                    kb.tt(w_[:, 0:hW], p_[:, 1:hW + 1], p_[:, 0:hW], ALU.subtract, [bp_], [bw_])
                kb.tt(w_[:, hW:W], p_[:, hW + 1:W + 1], p_[:, hW:W], ALU.subtract, [bp_], [bw_], eng='pool')

            def S3(i):
                u = units[i]
                hb = 64 * u['h']
                W = u['W']
                w_, bw_ = wt[i % NB], bwt[i % NB]
                T_, bT_ = pt[0], bpt[0]
                wT_, bwT_ = wT[i % 2], bwT[i % 2]
                pot, bpot = po[0], bpo[0]
                nblk = W // 128
                for bi in range(nblk):
                    kb.tr(T_[:, bi * 128:(bi + 1) * 128], w_[:, bi * 128:(bi + 1) * 128], c['ident_b'], [bw_, c['bufb']], [bT_])
                kb.act(wT_[:, 0:W], T_[:, 0:W], AF.Copy, [bT_], [bwT_])
                for bi in range(nblk):
                    kb.mm(pot[hb:hb + 64, 0:128], vtm[:, u['k0'] // 128 + bi, hb:hb + 64], wT_[:, bi * 128:(bi + 1) * 128],
                          [bvtm, bwT_], [bpot], start=(u['first'] and bi == 0), stop=(u['last'] and bi == nblk - 1))
                if u['last']:
                    kb.act(yd[hb:hb + 64, u['qi'] * 128:(u['qi'] + 1) * 128], pot[hb:hb + 64, 0:128], AF.Copy, [bpot], [byd])

            for t in range(NU + 3):
                if 0 <= t - 1 < NU:
                    S2(t - 1)
                if t < NU:
                    S1(t)
                if 0 <= t - 2 < NU:
                    S2b(t - 2)
                if 0 <= t - 3 < NU:
                    S3(t - 3)
                yield
            kb.store(YT[6 + hp], yd[:], [byd], q='pool')


def gen_C(kb, l, PT, YT):
    P = kb.P
    S = kb.S
    c = kb.c
    SEG = min(256, S)
    TW = min(512, SEG)
    NC_ = SEG // 64
    NCH = NC_ * 2
    with contextlib.nullcontext():
        pv = P.sb([128, 2, 16])
        bpv = Buf()
        kb.load(pv[:, :, 0:11], kb.d['pvC'][l].rearrange("h p n -> p h n"), [bpv])
        for hp in range(2):
            kb.ts(pv[:, hp, 11:15], pv[:, hp, 0:4], -1.0, 1.0, ALU.mult, ALU.add, [bpv], [bpv])
            kb.ts(pv[:, hp, 15:16], pv[:, hp, 7:8], -1.0, 1.0, ALU.mult, ALU.add, [bpv], [bpv])
        wag = P.sb([128, 2, 128])
        bwag = Buf()
        kb.load(wag[:], kb.d['wC'][l].rearrange("h p n -> p h n"), [bwag])
        rmask = P.sb([128, SEG])
        brm = Buf()
        kb.memset(rmask[:], 1.0, [brm])
        kb.memset(rmask[:].rearrange("p (n c) -> p n c", c=64)[:, :, 0:1], 0.0, [brm])

        def T():
            return P.sb([128, SEG]), Buf()
        raw = [(P.sb([128, SEG + 1]), Buf()) for _ in range(4)]
        sh = [T() for _ in range(4)]
        (ld, bld), (al, bal), (gf, bgf), (kk, bkk), (km, bkm), (bon, bbon) = T(), T(), T(), T(), T(), T()
        (Lc, bLc), (t1, bt1), (t2, bt2), (bv_, bbv) = T(), T(), T(), T()
        AR = P.sb([128, NC_, 128])
        bAR = Buf()
        (btl, bbt), (ktl, bkt), (bh, bbh), (kh, bkh) = T(), T(), T(), T()
        BHt = P.sb([64, NC_, 128])
        KHt = P.sb([64, NC_, 128])
        Vt = P.sb([64, NC_, 128])
        bBHt, bKHt, bVt = Buf(), Buf(), Buf()
        AM = P.sb([64, NCH, 320])
        bAM = [Buf() for _ in range(NCH)]
        Bm = [P.sb([64, NCH, 64]) for _ in range(2)]
        Btm = [P.sb([64, NCH, 64]) for _ in range(2)]
        Mt = [P.sb([64, NCH, 64]) for _ in range(2)]
        bBm = [[Buf() for _ in range(NCH // 4)] for _ in range(2)]
        bBtm = [[Buf() for _ in range(NCH // 4)] for _ in range(2)]
        bMt = [[Buf() for _ in range(NCH // 4)] for _ in range(2)]
        TS = [P.sb([64, 2, 64]) for _ in range(2)]
        AR1 = P.sb([64, NC_, 128])
        bAR1 = Buf()
        GCc = P.sb([128, NC_])
        bGCc = Buf()
        G1 = P.sb([64, NC_])
        bG1 = Buf()
        bTS = [Buf(), Buf()]
        Wsb = P.sb([64, 128])
        bWsb = Buf()
        Usb = P.sb([64, 128])
        bUsb = Buf()
        ysb = P.sb([64, 8, 64])
        bysb = Buf()
        ysq = P.sb([64, 8, 64])
        bysq = Buf()
        st1 = P.sb([64, 8])
        st2 = P.sb([64, 8])
        bst1, bst2 = Buf(), Buf()
        yc = P.sb([128, SEG], BF16)
        byc = Buf()
        yfm = P.sb([128, 256])
        byfm = Buf()
        pg = [P.ps([128, 512])]
        bpg = [Buf()]
        bankX = P.ps([64, 512])
        bankY = P.ps([64, 512])
        pinv = [bankX[:, 0:256], bankX[:, 256:512], bankY[:, 0:256]]
        bbX = Buf()
        bbY = Buf()
        bpinv = [bbX, bbX, bbY]
        pW = bankY[:, 256:384]
        bpW = bbY
        pT = bankY[:, 384:512]
        bpT = bbY
        pY = P.ps([64, 512])
        bpY = Buf()
        ig = [0]

        def gbank():
            i = 0
            ig[0] += 1
            return pg[i], bpg[i]

        for hp in range(2):
            chans = [10 + hp, 12 + hp, 14 + hp, 16]
            kb.memset(TS[0][:], 0.0, [bTS[0]])
            tsi = 0
            for sg in range(S // SEG):
                t0 = sg * SEG
                for i in range(4):
                    rt_, br_ = raw[i]
                    if t0 == 0:
                        kb.memset(rt_[:, 0:1], 0.0, [br_])
                        kb.load(rt_[:, 1:SEG + 1], PT[chans[i]][:, 0:SEG], [br_])
                    else:
                        kb.load(rt_[:, :], PT[chans[i]][:, t0 - 1:t0 + SEG], [br_])
                    s_, bs_ = sh[i]
                    mcol = 3 if i == 3 else i
                    kb.ts(s_[:], rt_[:, 1:SEG + 1], pv[:, hp, 11 + mcol:12 + mcol], None, ALU.mult, None, [br_, bpv], [bs_])
                    kb.stt(s_[:], rt_[:, 0:SEG], pv[:, hp, mcol:mcol + 1], s_[:], ALU.mult, ALU.add, [br_, bpv, bs_], [bs_])
                (rs, brs), (ks, bks), (vs, bvs), (xs, bxs) = sh
                yield
                kb.act(t1[0:32, :], xs[0:32, :], AF.Tanh, [bxs], [bt1])
                kb.act(t2[64:128, :], xs[64:128, :], AF.Sigmoid, [bxs], [bt2])
                for ti in range(SEG // TW):
                    sl = slice(ti * TW, (ti + 1) * TW)
                    p_, bp_ = gbank()
                    kb.mm(p_[:, 0:TW], wag[0:32, hp, :], t1[0:32, sl], [bwag, bt1], [bp_])
                    kb.act(ld[:, sl], p_[:, 0:TW], AF.Sigmoid, [bp_, bpv], [bld], bias=pv[:, hp, 4:5])
                    p_, bp_ = gbank()
                    kb.mm(p_[:, 0:TW], wag[32:64, hp, :], xs[32:64, sl], [bwag, bxs], [bp_])
                    kb.act(al[:, sl], p_[:, 0:TW], AF.Sigmoid, [bp_, bpv], [bal], bias=pv[:, hp, 5:6])
                    p_, bp_ = gbank()
                    kb.mm(p_[:, 0:TW], wag[64:128, hp, :], t2[64:128, sl], [bwag, bt2], [bp_])
                    kb.cp(gf[:, sl], p_[:, 0:TW], [bp_], [bgf])
                kb.ts(ld[:], ld[:], -0.6065306597126334, None, ALU.mult, None, [bld], [bld])
                kb.ts(kk[:], ks[:], pv[:, hp, 6:7], None, ALU.mult, None, [bks, bpv], [bkk])
                kb.tt(t1[:], kk[:], kk[:], ALU.mult, [bkk], [bt1])
                for ti in range(SEG // TW):
                    sl = slice(ti * TW, (ti + 1) * TW)
                    p_, bp_ = gbank()
                    kb.mm(p_[:, 0:TW], c['bd'], t1[:, sl], [c['buf'], bt1], [bp_])
                    kb.act(t2[:, sl], p_[:, 0:TW], AF.Sqrt, [bp_], [bt2])
                kb.ts(t2[:], t2[:], 1e-12, None, ALU.max, None, [bt2], [bt2])
                kb.recip(t2[:], t2[:], [bt2], [bt2])
                kb.tt(kk[:], kk[:], t2[:], ALU.mult, [bkk, bt2], [bkk])
                kb.ts(km[:], al[:], pv[:, hp, 7:8], pv[:, hp, 15:16], ALU.mult, ALU.add, [bal, bpv], [bkm])
                kb.tt(km[:], km[:], ks[:], ALU.mult, [bkm, bks], [bkm])
                kb.stt(t1[:], rs[:], pv[:, hp, 8:9], km[:], ALU.mult, ALU.mult, [brs, bpv, bkm], [bt1])
                for ti in range(SEG // TW):
                    sl = slice(ti * TW, (ti + 1) * TW)
                    p_, bp_ = gbank()
                    kb.mm(p_[:, 0:TW], c['bd'], t1[:, sl], [c['buf'], bt1], [bp_])
                    kb.tt(bon[:, sl], p_[:, 0:TW], vs[:, sl], ALU.mult, [bp_, bvs], [bbon])
                kb.P.op('dve', lambda e: e.tensor_tensor_scan(out=Lc[:], data0=rmask[:], data1=ld[:], initial=0.0,
                                                              op0=ALU.mult, op1=ALU.add), reads=[brm, bld], writes=[bLc])
                yield
                AR4 = AR[:].rearrange("p n (two c) -> p n two c", two=2)
                v3 = lambda t: t[:].rearrange("p (n c) -> p n c", c=64)
                kb.act(t1[:], Lc[:], AF.Exp, [bLc], [bt1])
                kb.tt(AR4[:, :, 1, :], v3(rs), v3(t1), ALU.mult, [brs, bt1], [bAR])
                kb.tt(t2[:], Lc[:], ld[:], ALU.subtract, [bLc, bld], [bt2])
                kb.act(t2[:], t2[:], AF.Exp, [bt2], [bt2])
                kb.stt(AR4[:, :, 0, :], v3(kk), -1.0, v3(t2), ALU.mult, ALU.mult, [bkk, bt2], [bAR])
                kb.tt(bv_[:], kk[:], al[:], ALU.mult, [bkk, bal], [bbv])
                kb.act(t2[:], Lc[:], AF.Exp, [bLc], [bt2], scale=-1.0)
                kb.tt(btl[:], bv_[:], t2[:], ALU.mult, [bbv, bt2], [bbt])
                kb.tt(ktl[:], km[:], t2[:], ALU.mult, [bkm, bt2], [bkt])
                LCb = v3(Lc)[:, :, 63:64].broadcast_to([128, NC_, 64])
                kb.tt(v3(t2), LCb, v3(Lc), ALU.subtract, [bLc], [bt2])
                kb.act(t2[:], t2[:], AF.Exp, [bt2], [bt2])
                kb.tt(bh[:], bv_[:], t2[:], ALU.mult, [bbv, bt2], [bbh])
                kb.tt(kh[:], km[:], t2[:], ALU.mult, [bkm, bt2], [bkh])
                kb.cp(GCc[:], v3(t1)[:, :, 63], [bt1], [bGCc])
                kb.load(G1[:], GCc[64:128, :], [bG1], r=[bGCc])
                kb.load(AR1[:], AR[64:128, :, :], [bAR1], r=[bAR])
                ARh = [AR, AR1]
                bARh = [bAR, bAR1]
                GCh = [GCc, G1]
                bGCh = [bGCc, bG1]
                yield
                for (src, bsrc, dst, bdst) in ((bh, bbh, BHt, bBHt), (kh, bkh, KHt, bKHt), (vs, bvs, Vt, bVt)):
                    for g in range(NC_ // 4):
                        p_, bp_ = gbank()
                        for j in range(4):
                            n = g * 4 + j
                            kb.tr(p_[0:64, j * 128:(j + 1) * 128], src[:, n * 64:(n + 1) * 64], c['ident'], [bsrc, c['buf']], [bp_])
                        kb.cp(dst[:, g * 4:(g + 1) * 4, :], p_[0:64, :].rearrange("p (j c) -> p j c", c=128), [bp_], [bdst])
                for n in range(NC_):
                    for h in range(2):
                        hb = 64 * h
                        ci = n * 2 + h
                        p_, bp_ = gbank()
                        csl = slice(n * 64, (n + 1) * 64)
                        kb.mm(p_[0:64, 0:128], btl[hb:hb + 64, csl], AR[hb:hb + 64, n, :], [bbt, bAR], [bp_])
                        kb.mm(p_[0:64, 128:256], ktl[hb:hb + 64, csl], AR[hb:hb + 64, n, :], [bkt, bAR], [bp_])
                        kb.mm(p_[0:64, 256:320], AR[hb:hb + 64, n, 0:64], btl[hb:hb + 64, csl], [bbt, bAR], [bp_])
                        kb.tt(AM[:, ci, :], p_[0:64, 0:320], c['m320'], ALU.mult, [bp_, c['buf']], [bAM[ci]])
                        yield
                for g in range(NCH // 4):
                    gs_ = slice(g * 4, (g + 1) * 4)
                    rb = [bAM[ci] for ci in range(g * 4, (g + 1) * 4)]
                    kb.cp(Bm[0][:, gs_, :], AM[:, gs_, 256:320], rb, [bBm[0][g]])
                    kb.cp(Btm[0][:, gs_, :], AM[:, gs_, 0:64], rb, [bBtm[0][g]], eng='pool')
                    kb.tt(Mt[0][:, gs_, :], AM[:, gs_, 0:64], c['ident64'].unsqueeze(1).broadcast_to([64, 4, 64]), ALU.add,
                          rb + [c['buf']], [bMt[0][g]])
                cur = 0
                for step in range(5):
                    nxt = 1 - cur
                    for g in range(NCH // 4):
                        gs_ = slice(g * 4, (g + 1) * 4)
                        pB, pBt, pM = pinv
                        for j in range(4):
                            ci = g * 4 + j
                            kb.mm(pB[:, j * 64:(j + 1) * 64], Btm[cur][:, ci, :], Bm[cur][:, ci, :], [bBtm[cur][g], bBm[cur][g]], [bpinv[0]])
                        if step < 4:
                            for j in range(4):
                                ci = g * 4 + j
                                kb.mm(pBt[:, j * 64:(j + 1) * 64], Bm[cur][:, ci, :], Btm[cur][:, ci, :], [bBtm[cur][g], bBm[cur][g]], [bpinv[1]])
                        kb.cp(Bm[nxt][:, gs_, :], pB[:].rearrange("p (j c) -> p j c", c=64), [bpinv[0]], [bBm[nxt][g]])
                        if step < 4:
                            kb.cp(Btm[nxt][:, gs_, :], pBt[:].rearrange("p (j c) -> p j c", c=64), [bpinv[1]], [bBtm[nxt][g]])
                        for j in range(4):
                            ci = g * 4 + j
                            kb.mm(pM[:, j * 64:(j + 1) * 64], Bm[nxt][:, ci, :], Mt[cur][:, ci, :], [bBm[nxt][g], bMt[cur][g]], [bpinv[2]])
                        kb.tt(Mt[nxt][:, gs_, :], Mt[cur][:, gs_, :], pM[:].rearrange("p (j c) -> p j c", c=64), ALU.add,
                              [bpinv[2], bMt[cur][g]], [bMt[nxt][g]])
                        yield
                    cur = nxt
                MtF = Mt[cur]
                bMtF = bMt[cur]
                for n in range(NC_):
                    Tc = TS[tsi % 2]
                    bTc = bTS[tsi % 2]
                    Tn = TS[(tsi + 1) % 2]
                    bTn = bTS[(tsi + 1) % 2]
                    tsi += 1
                    yield
                    for h in range(2):
                        hb = 64 * h
                        ci = n * 2 + h
                        kb.mm(pW[:, h * 64:(h + 1) * 64], ARh[h][0:64, n, 0:64], Tc[:, h, :], [bARh[h], bTc], [bpW], start=True, stop=False)
                        kb.mm(pW[:, h * 64:(h + 1) * 64], AM[:, ci, 128:192], Vt[:, n, hb:hb + 64], [bAM[ci], bVt], [bpW], start=False, stop=True)
                    kb.cp(Wsb[:], pW[:], [bpW], [bWsb])
                    for h in range(2):
                        ci = n * 2 + h
                        kb.mm(pW[:, h * 64:(h + 1) * 64], MtF[:, ci, :], Wsb[:, h * 64:(h + 1) * 64], [bMtF[ci // 4], bWsb], [bpW])
                    kb.cp(Usb[:], pW[:], [bpW], [bUsb])
                    q4 = n % 4
                    for h in range(2):
                        hb = 64 * h
                        ci = n * 2 + h
                        ysl = slice((q4 * 2 + h) * 64, (q4 * 2 + h + 1) * 64)
                        kb.mm(pY[:, ysl], ARh[h][0:64, n, 64:128], Tc[:, h, :], [bARh[h], bTc], [bpY], start=True, stop=False)
                        kb.mm(pY[:, ysl], AM[:, ci, 64:128], Usb[:, h * 64:(h + 1) * 64], [bAM[ci], bUsb], [bpY], start=False, stop=False)
                        kb.mm(pY[:, ysl], AM[:, ci, 192:256], Vt[:, n, hb:hb + 64], [bAM[ci], bVt], [bpY], start=False, stop=True)
                    for h in range(2):
                        hb = 64 * h
                        kb.mm(pT[:, h * 64:(h + 1) * 64], BHt[:, n, hb:hb + 64], Usb[:, h * 64:(h + 1) * 64], [bBHt, bUsb], [bpT], start=True, stop=False)
                        kb.mm(pT[:, h * 64:(h + 1) * 64], KHt[:, n, hb:hb + 64], Vt[:, n, hb:hb + 64], [bKHt, bVt], [bpT], start=False, stop=True)
                    for h in range(2):
                        kb.stt(Tn[:, h, :], Tc[:, h, :], GCh[h][0:64, n:n + 1], pT[:, h * 64:(h + 1) * 64], ALU.mult, ALU.add,
                               [bTc, bGCh[h], bpT], [bTn])
                    if q4 == 3:
                        kb.act(ysb[:], pY[:].rearrange("p (g v) -> p g v", v=64), AF.Copy, [bpY], [bysb])
                        kb.P.op('dve', lambda e: e.tensor_reduce(out=st1[:], in_=ysb[:], axis=AX.X, op=ALU.add), reads=[bysb], writes=[bst1])
                        kb.ts(st1[:], st1[:], 1.0 / 64, None, ALU.mult, None, [bst1], [bst1])
                        kb.tt(ysb[:], ysb[:], st1[:].unsqueeze(2).broadcast_to([64, 8, 64]), ALU.subtract, [bysb, bst1], [bysb])
                        kb.act(ysq[:], ysb[:], AF.Square, [bysb], [bysq])
                        kb.P.op('dve', lambda e: e.tensor_reduce(out=st2[:], in_=ysq[:], axis=AX.X, op=ALU.add), reads=[bysq], writes=[bst2])
                        kb.ts(st2[:], st2[:], 1.0 / 64, RWKV_LN_EPS, ALU.mult, ALU.add, [bst2], [bst2])
                        kb.act(st2[:], st2[:], AF.Sqrt, [bst2], [bst2])
                        kb.recip(st2[:], st2[:], [bst2], [bst2])
                        kb.tt(ysb[:], ysb[:], st2[:].unsqueeze(2).broadcast_to([64, 8, 64]), ALU.mult, [bysb, bst2], [bysb])
                        p_, bp_ = gbank()
                        for j in range(4):
                            kb.tr(p_[:, j * 64:(j + 1) * 64], ysb[:, 2 * j:2 * j + 2, :].rearrange("p g v -> p (g v)"), c['ident64'],
                                  [bysb, c['buf']], [bp_])
                        tsl = slice((n - 3) * 64, (n + 1) * 64)
                        kb.ts(yfm[:], p_[:, 0:256], pv[:, hp, 9:10], pv[:, hp, 10:11], ALU.mult, ALU.add, [bp_, bpv], [byfm])
                        kb.tt(yfm[:], yfm[:], bon[:, tsl], ALU.add, [byfm, bbon], [byfm])
                        kb.tt(yc[:, tsl], yfm[:], gf[:, tsl], ALU.mult, [byfm, bgf], [byc])
                kb.store(YT[4 + hp][:, t0:t0 + SEG], yc[:], [byc], q='pool')


def phase_CD(kb, l, PT, YT):
    P = kb.P
    S = kb.S
    with P.scope():
        gC = gen_C(kb, l, PT, YT)
        gD = gen_D(kb, l, PT, YT)
        SEG = min(256, S)
        nseg = S // SEG
        nc_ = SEG // 64
        totC = 2 * nseg * (3 + 2 * nc_ + 5 * (2 * nc_ // 4) + nc_)
        NQ = S // 128
        nu = sum((qi + 1 + 7) // 8 for qi in range(NQ)) * 2
        totD = 2 * (nu + 3)
        ratio = totC / float(totD)
        acc = 0.0
        doneC = doneD = False
        while not (doneC and doneD):
            if not doneD:
                try:
                    next(gD)
                except StopIteration:
                    doneD = True
            acc += ratio
            while (acc >= 1.0 or doneD) and not doneC:
                acc -= 1.0
                try:
                    next(gC)
                except StopIteration:
                    doneC = True


def phase_T2a(kb, l, XT, YT, H2, modT):
    P = kb.P
    S = kb.S
    TN = 512
    with P.scope():
        Wb = P.sb([128, 8, D_MODEL], BF16)
        bW = Buf()
        stages = [P.sb([128, 2048]) for _ in range(3)]
        bst = [Buf() for _ in range(3)]
        load_cast_weight(kb, Wb, bW, lambda k: kb.d['w_out'][l, k * 128:(k + 1) * 128, :], D_MODEL, 8, stages, bst)
        modt = P.sb([128, 48])
        g2 = P.sb([128, 8])
        gs = P.sb([128, 8])
        bmod = Buf()
        kb.load(modt[:], modT[l], [bmod])
        kb.load(g2[:], kb.d['norm2_gT'][l], [bmod])
        kb.stt(gs[:], modt[:, 32:40], 1.0, g2[:], ALU.add, ALU.mult, [bmod], [bmod])
        sh = modt[:, 24:32]
        gt = modt[:, 16:24]
        xts = [P.sb([128, 8, TN]) for _ in range(2)]
        bxs = [Buf() for _ in range(2)]
        yts = [P.sb([128, 8, TN], BF16) for _ in range(2)]
        bys = [Buf() for _ in range(2)]
        sq = P.sb([128, 8, TN], BF16)
        bsq = Buf()
        pss = P.ps([128, TN])
        bpss = Buf()
        rstd = P.sb([128, TN])
        brstd = Buf()
        tmp = P.sb([128, 8, TN])
        btmp = [Buf(), Buf()]
        hs = [P.sb([128, 8, TN], BF16) for _ in range(2)]
        bhs = [Buf() for _ in range(2)]
        pps = [P.ps([128, TN]) for _ in range(4)]
        bpp = [Buf() for _ in range(4)]
        io = 0
        for ti in range(S // TN):
            xt, bx = xts[ti % 2], bxs[ti % 2]
            yt, by = yts[ti % 2], bys[ti % 2]
            h, bh = hs[ti % 2], bhs[ti % 2]
            tsl = slice(ti * TN, (ti + 1) * TN)
            kb.load(xt[:], XT[:, :, tsl], [bx])
            kb.load(yt[:], YT[:, :, tsl].rearrange("k p t -> p k t"), [by])
            for j in range(8):
                pp, bp = pps[io % 4], bpp[io % 4]
                io += 1
                for k in range(8):
                    kb.mm(pp[:], Wb[:, k, j * 128:(j + 1) * 128], yt[:, k, :], [bW, by], [bp], start=(k == 0), stop=(k == 7))
                kb.stt(xt[:, j, :], pp[:], gt[:, j:j + 1], xt[:, j, :], ALU.mult, ALU.add, [bp, bmod, bx], [bx])
            kb.store(XT[:, :, tsl], xt[:], [bx], q='pool')
            emit_norm(kb, xt, bx, TN, gs, sh, bmod, sq, bsq, pss, bpss, rstd, brstd, tmp, btmp, h, bh)
            kb.store(H2[:, :, tsl], h[:], [bh], q='pool')


def phase_T2b(kb, l, XT, H2, modT):
    P = kb.P
    S = kb.S
    TN = 512
    NF = 2 * D_FF // 128
    NV = D_FF // 128
    with P.scope():
        Wu = P.sb([128, 8, 2 * D_FF], BF16)
        bWu = Buf()
        Wd = P.sb([128, NV, D_MODEL], BF16)
        bWd = Buf()
        with P.scope():
            stages = [P.sb([128, 1024]) for _ in range(3)]
            bst = [Buf() for _ in range(3)]
            load_cast_weight(kb, Wu, bWu, lambda k: kb.d['ffn_up'][l, k * 128:(k + 1) * 128, :], 2 * D_FF, 8, stages, bst)
            load_cast_weight(kb, Wd, bWd, lambda k: kb.d['ffn_down'][l, k * 128:(k + 1) * 128, :], D_MODEL, NV, stages, bst)
        modt = P.sb([128, 48])
        bmod = Buf()
        kb.load(modt[:], modT[l], [bmod])
        gt = modt[:, 40:48]
        cw = P.sb([128, 4, NF])
        bcw = Buf()
        kb.load(cw[:], kb.d['ffn_cwT'][l], [bcw])
        halo = P.sb([128, NF, 2])
        bhalo = Buf()
        kb.memset(halo[:], 0.0, [bhalo])
        xt = P.sb([128, 8, TN])
        bx = Buf()
        h = P.sb([128, 8, TN], BF16)
        bh = Buf()
        pps = [P.ps([128, 512]) for _ in range(4)]
        bpp = [Buf() for _ in range(4)]
        us = [P.sb([128, TN + 2]) for _ in range(3)]
        bus = [Buf() for _ in range(3)]
        uc = [P.sb([128, TN]) for _ in range(3)]
        buc = [Buf() for _ in range(3)]
        ov = [P.sb([128, TN]) for _ in range(2)]
        bov = [Buf() for _ in range(2)]
        g_ = P.sb([128, NV, TN], BF16)
        bg_ = Buf()
        io = 0
        iu = 0
        for ti in range(S // TN):
            tsl = slice(ti * TN, (ti + 1) * TN)
            kb.load(xt[:], XT[:, :, tsl], [bx])
            kb.load(h[:], H2[:, :, tsl], [bh])
            for i in range(NV):
                v_, bv_ = ov[i % 2], bov[i % 2]
                for f in (i, i + NV):
                    pp, bp = pps[io % 4], bpp[io % 4]
                    io += 1
                    u, bu = us[iu % 3], bus[iu % 3]
                    o, bo_ = uc[iu % 3], buc[iu % 3]
                    iu += 1
                    for k in range(8):
                        kb.mm(pp[:, 0:TN], Wu[:, k, f * 128:(f + 1) * 128], h[:, k, :], [bWu, bh], [bp], start=(k == 0), stop=(k == 7))
                    kb.cp(u[:, 0:2], halo[:, f, :], [bhalo], [bu])
                    kb.act(u[:, 2:TN + 2], pp[:, 0:TN], AF.Copy, [bp], [bu])
                    kb.act(halo[:, f, :], u[:, TN:TN + 2], AF.Copy, [bu], [bhalo])
                    kb.ts(o[:], u[:, 2:TN + 2], cw[:, 2, f:f + 1], cw[:, 3, f:f + 1], ALU.mult, ALU.add, [bu, bcw], [bo_])
                    kb.stt(o[:], u[:, 1:TN + 1], cw[:, 1, f:f + 1], o[:], ALU.mult, ALU.add, [bu, bcw, bo_], [bo_])
                    if f < NV:
                        kb.stt(v_[:], u[:, 0:TN], cw[:, 0, f:f + 1], o[:], ALU.mult, ALU.add, [bu, bcw, bo_], [bv_])
                    else:
                        kb.stt(o[:], u[:, 0:TN], cw[:, 0, f:f + 1], o[:], ALU.mult, ALU.add, [bu, bcw, bo_], [bo_])
                        kb.act(o[:], o[:], AF.Gelu, [bo_], [bo_])
                        kb.tt(g_[:, i, :], v_[:], o[:], ALU.mult, [bv_, bo_], [bg_])
            for j in range(8):
                pp, bp = pps[io % 4], bpp[io % 4]
                io += 1
                for k in range(NV):
                    kb.mm(pp[:, 0:TN], Wd[:, k, j * 128:(j + 1) * 128], g_[:, k, :], [bWd, bg_], [bp], start=(k == 0), stop=(k == NV - 1))
                kb.stt(xt[:, j, :], pp[:, 0:TN], gt[:, j:j + 1], xt[:, j, :], ALU.mult, ALU.add, [bp, bmod, bx], [bx])
            kb.store(XT[:, :, tsl], xt[:], [bx], q='pool')


def phase_T2(kb, l, XT, YT, modT):
    P = kb.P
    S = kb.S
    c = kb.c
    TN = 512
    NF = 2 * D_FF // 128
    NV = D_FF // 128
    with P.scope():
        Wu = P.sb([128, 8, 2 * D_FF], BF16)
        bWu = Buf()
        Wd = P.sb([128, NV, D_MODEL], BF16)
        bWd = Buf()
        Wo = P.sb([128, 8, D_MODEL], BF16)
        bWo = Buf()
        with P.scope():
            stages = [P.sb([128, 1024]) for _ in range(3)]
            bst = [Buf() for _ in range(3)]
            load_cast_weight(kb, Wo, bWo, lambda k: kb.d['w_out'][l, k * 128:(k + 1) * 128, :], D_MODEL, 8, stages, bst)
            load_cast_weight(kb, Wu, bWu, lambda k: kb.d['ffn_up'][l, k * 128:(k + 1) * 128, :], 2 * D_FF, 8, stages, bst)
            load_cast_weight(kb, Wd, bWd, lambda k: kb.d['ffn_down'][l, k * 128:(k + 1) * 128, :], D_MODEL, NV, stages, bst)
        modt = P.sb([128, 48])
        g2 = P.sb([128, 8])
        gs = P.sb([128, 8])
        bmod = Buf()
        kb.load(modt[:], modT[l], [bmod])
        kb.load(g2[:], kb.d['norm2_gT'][l], [bmod])
        kb.stt(gs[:], modt[:, 32:40], 1.0, g2[:], ALU.add, ALU.mult, [bmod], [bmod])
        sh = modt[:, 24:32]
        gtm = modt[:, 16:24]
        gtf = modt[:, 40:48]
        cw = P.sb([128, 4, NF])
        bcw = Buf()
        kb.load(cw[:], kb.d['ffn_cwT'][l], [bcw])
        halo = P.sb([128, NF, 2])
        bhalo = Buf()
        kb.memset(halo[:], 0.0, [bhalo])
        xt = P.sb([128, 8, TN])
        bx = Buf()
        h = P.sb([128, 8, TN], BF16)
        bh = Buf()
        g_ = P.sb([128, NV, TN], BF16)
        bg_ = Buf()
        pps = [P.ps([128, 512]) for _ in range(4)]
        bpp = [Buf() for _ in range(4)]
        pss = P.ps([128, TN])
        bpss = Buf()
        us = [P.sb([128, TN + 2]) for _ in range(1)]
        bus = [Buf() for _ in range(1)]
        uc = [P.sb([128, TN]) for _ in range(1)]
        buc = [Buf() for _ in range(1)]
        ov = [P.sb([128, TN]) for _ in range(1)]
        bov = [Buf() for _ in range(1)]
        rstd, brstd = ov[0], bov[0]
        tmp = [uc[0]]
        btmp = [buc[0]]
        io = 0
        iu = 0
        for ti in range(S // TN):
            tsl = slice(ti * TN, (ti + 1) * TN)
            yt = g_[:, 0:8, :]
            sq = g_[:, 8:16, :]
            kb.load(xt[:], XT[:, :, tsl], [bx])
            kb.load(yt, YT[:, :, tsl].rearrange("k p t -> p k t"), [bg_])
            for j in range(8):
                pp, bp = pps[io % 4], bpp[io % 4]
                io += 1
                for k in range(8):
                    kb.mm(pp[:, 0:TN], Wo[:, k, j * 128:(j + 1) * 128], g_[:, k, :], [bWo, bg_], [bp], start=(k == 0), stop=(k == 7))
                kb.stt(xt[:, j, :], pp[:, 0:TN], gtm[:, j:j + 1], xt[:, j, :], ALU.mult, ALU.add, [bp, bmod, bx], [bx])
            kb.act(sq, xt[:], AF.Square, [bx], [bg_])
            for k in range(8):
                kb.mm(pss[:], c['ones_b'][:], g_[:, 8 + k, :], [bg_, c['bufb']], [bpss], start=(k == 0), stop=(k == 7))
            kb.ts(rstd[:], pss[:], 1.0 / D_MODEL, NORM_EPS, ALU.mult, ALU.add, [bpss], [brstd])
            kb.act(rstd[:], rstd[:], AF.Ln, [brstd], [brstd])
            kb.act(rstd[:], rstd[:], AF.Exp, [brstd], [brstd], scale=-0.5)
            for k in range(8):
                t_, bt_ = tmp[0], btmp[0]
                kb.stt(t_[:], xt[:, k, :], gs[:, k:k + 1], rstd[:], ALU.mult, ALU.mult, [bx, bmod, brstd], [bt_])
                kb.act(h[:, k, :], t_[:], AF.Identity, [bt_, bmod], [bh], bias=sh[:, k:k + 1])
            for i in range(NV):
                v_, bv_ = ov[0], bov[0]
                for f in (i, i + NV):
                    pp, bp = pps[io % 4], bpp[io % 4]
                    io += 1
                    u, bu = us[0], bus[0]
                    o, bo_ = uc[0], buc[0]
                    iu += 1
                    for k in range(8):
                        kb.mm(pp[:, 0:TN], Wu[:, k, f * 128:(f + 1) * 128], h[:, k, :], [bWu, bh], [bp], start=(k == 0), stop=(k == 7))
                    kb.cp(u[:, 0:2], halo[:, f, :], [bhalo], [bu])
                    kb.act(u[:, 2:TN + 2], pp[:, 0:TN], AF.Copy, [bp], [bu])
                    kb.act(halo[:, f, :], u[:, TN:TN + 2], AF.Copy, [bu], [bhalo])
                    kb.ts(o[:], u[:, 2:TN + 2], cw[:, 2, f:f + 1], cw[:, 3, f:f + 1], ALU.mult, ALU.add, [bu, bcw], [bo_])
                    kb.stt(o[:], u[:, 1:TN + 1], cw[:, 1, f:f + 1], o[:], ALU.mult, ALU.add, [bu, bcw, bo_], [bo_])
                    if f < NV:
                        kb.stt(v_[:], u[:, 0:TN], cw[:, 0, f:f + 1], o[:], ALU.mult, ALU.add, [bu, bcw, bo_], [bv_])
                    else:
                        kb.stt(o[:], u[:, 0:TN], cw[:, 0, f:f + 1], o[:], ALU.mult, ALU.add, [bu, bcw, bo_], [bo_])
                        kb.act(o[:], o[:], AF.Gelu, [bo_], [bo_])
                        kb.tt(g_[:, i, :], v_[:], o[:], ALU.mult, [bv_, bo_], [bg_])
            for j in range(8):
                pp, bp = pps[io % 4], bpp[io % 4]
                io += 1
                for k in range(NV):
                    kb.mm(pp[:, 0:TN], Wd[:, k, j * 128:(j + 1) * 128], g_[:, k, :], [bWd, bg_], [bp], start=(k == 0), stop=(k == NV - 1))
                kb.stt(xt[:, j, :], pp[:, 0:TN], gtf[:, j:j + 1], xt[:, j, :], ALU.mult, ALU.add, [bp, bmod, bx], [bx])
            kb.store(XT[:, :, tsl], xt[:], [bx], q='pool')


PARAM_SPECS = None


def host_params(inp, L):
    f = lambda a: np.ascontiguousarray(np.asarray(a, dtype=np.float32))
    o = {}
    o['ada_w'] = f(inp['ada_w'][:L])
    o['ada_bT'] = f(np.asarray(inp['ada_b'])[:L].reshape(L, 48, 128).transpose(0, 2, 1))
    o['norm1_gT'] = f(np.asarray(inp['norm1_g'])[:L].reshape(L, 8, 128).transpose(0, 2, 1))
    o['norm2_gT'] = f(np.asarray(inp['norm2_g'])[:L].reshape(L, 8, 128).transpose(0, 2, 1))
    o['w_in'] = f(inp['w_in'][:L])
    o['w_out'] = f(inp['w_out'][:L])
    o['ffn_up'] = f(inp['ffn_up'][:L])
    o['ffn_down'] = f(inp['ffn_down'][:L])
    cw = np.asarray(inp['ffn_conv_w'])[:L]
    cb = np.asarray(inp['ffn_conv_b'])[:L]
    cwb = np.concatenate([cw, cb[:, None, :]], axis=1)
    o['ffn_cwT'] = f(cwb.reshape(L, 4, 44, 128).transpose(0, 3, 1, 2))
    qg = np.asarray(inp['attn_q_gain'])[:L]
    kg = np.asarray(inp['attn_k_gain'])[:L]
    o['pvA'] = f(np.stack([np.tile(qg, (1, 2)), np.tile(kg, (1, 2))], axis=-1))
    rb = np.asarray(inp['attn_rel_bias'])[:L]
    qi = np.arange(64)[None, None, :]
    jj = np.arange(9)[None, :, None]
    kj = np.arange(64)[:, None, None]
    dist = 512 + qi - (jj * 64 + kj)
    rel = np.clip(dist, -128, 128) + 128
    o['biasA'] = f(rb[:, :, rel])
    cwB = np.asarray(inp['lru_conv_w'])[:L]
    cols = [cwB[:, 0], cwB[:, 1], cwB[:, 2], cwB[:, 3], np.asarray(inp['lru_conv_b'])[:L], np.asarray(inp['lru_ra_b'])[:L],
            np.asarray(inp['lru_ri_b'])[:L], np.asarray(inp['lru_lambda'])[:L]]
    pvB = np.stack(cols, axis=-1)
    o['pvB'] = f(pvB.reshape(L, 2, 128, 8))
    ra = np.asarray(inp['lru_ra_w'])[:L]
    ri = np.asarray(inp['lru_ri_w'])[:L]
    wB = np.zeros((L, 2, 2, 128, 128), np.float32)
    for hp in range(2):
        for h in range(2):
            wB[:, hp, 0, h * 64:(h + 1) * 64, h * 64:(h + 1) * 64] = ra[:, hp * 2 + h]
            wB[:, hp, 1, h * 64:(h + 1) * 64, h * 64:(h + 1) * 64] = ri[:, hp * 2 + h]
    o['wB'] = wB
    mu = np.asarray(inp['rwkv_mu'])[:L]
    pvC = np.zeros((L, 2, 128, 11), np.float32)
    for hp in range(2):
        cs = slice(hp * 128, (hp + 1) * 128)
        pvC[:, hp, :, 0] = mu[:, 0:256][:, cs]
        pvC[:, hp, :, 1] = mu[:, 256:512][:, cs]
        pvC[:, hp, :, 2] = mu[:, 512:768][:, cs]
        pvC[:, hp, :, 3] = mu[:, 768:896]
        for i, nm in enumerate(['rwkv_w0', 'rwkv_a0', 'rwkv_k_k', 'rwkv_k_a', 'rwkv_r_k', 'rwkv_ln_w', 'rwkv_ln_b']):
            pvC[:, hp, :, 4 + i] = np.asarray(inp[nm])[:L][:, cs]
    o['pvC'] = pvC
    wC = np.zeros((L, 2, 128, 128), np.float32)
    for hp in range(2):
        cs = slice(hp * 128, (hp + 1) * 128)
        wC[:, hp, 0:32] = np.asarray(inp['rwkv_w2'])[:L][:, :, cs]
        wC[:, hp, 32:64] = np.asarray(inp['rwkv_a2'])[:L][:, :, cs]
        wC[:, hp, 64:128] = np.asarray(inp['rwkv_g2'])[:L][:, :, cs]
    o['wC'] = wC
    o['consts'] = make_consts()
    return o


def declare_params(kb, hp):
    for k, v in hp.items():
        kb.din(k, v.shape)


def build_full(S, L, shapes):
    nc = bass.Bass("TRN2", target_bir_lowering=False)
    kb = KB(nc, S, L)
    for k, shp in shapes.items():
        kb.din(k, shp)
    kb.din('cT', [128, 8])
    kb.din('xT_in', [128, 8, S])
    XT = kb.dout('xT', [128, 8, S])
    PT = kb.dscr('PT', [23, 128, S])
    YT = kb.dscr('YT', [8, 128, S], BF16)
    H2 = kb.dscr('H2', [128, 8, S], BF16)
    modT = kb.dscr('modT', [L, 128, 48])
    load_consts(kb)
    P = kb.P
    with P.scope():
        t = [P.sb([128, 8, 512]) for _ in range(2)]
        b = [Buf(), Buf()]
        for ti in range(S // 512):
            kb.load(t[ti % 2][:], kb.d['xT_in'][:, :, ti * 512:(ti + 1) * 512], [b[ti % 2]])
            kb.store(XT[:, :, ti * 512:(ti + 1) * 512], t[ti % 2][:], [b[ti % 2]], q='pool')
    phase_ada(kb, modT)
    for l in range(L):
        phase_T1(kb, l, XT, PT, modT)
        phase_A(kb, l, PT, YT)
        phase_B(kb, l, PT, YT)
        phase_C(kb, l, PT, YT)
        phase_D(kb, l, PT, YT)
        phase_T2(kb, l, XT, YT, modT)
    kb.P.emit()
    return nc


def kernel(**inputs):
    x = np.asarray(inputs['x'], dtype=np.float32)
    B, S, D = x.shape
    L = DEPTH
    hp = host_params(inputs, L)
    nc = build_full(S, L, {k: v.shape for k, v in hp.items()})
    c = np.asarray(inputs['c'], dtype=np.float32)
    in_maps = []
    ncores = 4
    for i in range(ncores):
        b = i % B
        m = dict(hp)
        m['cT'] = np.ascontiguousarray(c[b].reshape(8, 128).T)
        m['xT_in'] = np.ascontiguousarray(x[b].reshape(S, 8, 128).transpose(2, 1, 0))
        in_maps.append(m)
    res = run_bass_kernel_spmd(nc, in_maps, core_ids=list(range(ncores)))
    out = np.empty((B, S, D), np.float32)
    for b in range(B):
        xT = res.results[b]['xT']
        out[b] = xT.transpose(2, 1, 0).reshape(S, D)
    return out
```

```python
import contextlib
import numpy as np
import concourse.bass as bass
import concourse.mybir as mybir
from concourse.bass_utils import run_bass_kernel_spmd

F32 = mybir.dt.float32
BF16 = mybir.dt.bfloat16
AF = mybir.ActivationFunctionType
ALU = mybir.AluOpType
AX = mybir.AxisListType

ENGS = ['pe', 'act', 'dve', 'pool', 'sp']

D_MODEL = 1024
DEPTH = 4
P_TOTAL = 2944
D_FF = 2816
NORM_EPS = 1e-6
RWKV_LN_EPS = 64e-5


class Buf:
    __slots__ = ('w', 'r', 'name')

    def __init__(self, name=''):
        self.w = None
        self.r = {}
        self.name = name


class Prog:
    def __init__(self, nc, n_ch=24):
        self.nc = nc
        self.ops = {e: [] for e in ENGS}
        self.cnt = {e: 0 for e in ENGS}
        self.seen = {e: {} for e in ENGS}
        self.n_ch = n_ch
        self.ch_val = [0] * n_ch
        self.ch_next = 0
        self.stack = contextlib.ExitStack()
        self.nbuf = 0

    @contextlib.contextmanager
    def scope(self):
        old = self.stack
        st = contextlib.ExitStack()
        self.stack = st
        try:
            yield
        finally:
            self.barrier()
            st.close()
            self.stack = old

    def sb(self, shape, dt=F32):
        self.nbuf += 1
        return self.stack.enter_context(self.nc.sbuf_tensor('t%d' % self.nbuf, list(shape), dt))

    def ps(self, shape, dt=F32):
        self.nbuf += 1
        return self.stack.enter_context(self.nc.psum_tensor('p%d' % self.nbuf, list(shape), dt))

    def _deps(self, eng, reads, writes):
        need = {}
        for b in reads:
            if b.w is not None:
                k, v = b.w
                if need.get(k, 0) < v:
                    need[k] = v
        for b in writes:
            if b.w is not None:
                k, v = b.w
                if need.get(k, 0) < v:
                    need[k] = v
            for k, v in b.r.items():
                if need.get(k, 0) < v:
                    need[k] = v
        waits = []
        sn = self.seen[eng]
        for k, v in need.items():
            if k == eng and eng == 'pe':
                continue
            if sn.get(k, 0) < v:
                waits.append((k, v))
                sn[k] = v
        return waits

    def _update(self, tok, reads, writes):
        for b in reads:
            if b.r.get(tok[0], 0) < tok[1]:
                b.r[tok[0]] = tok[1]
        for b in writes:
            b.w = tok
            b.r = {}

    def op(self, eng, fn, reads=(), writes=()):
        waits = self._deps(eng, reads, writes)
        self.cnt[eng] += 1
        tok = (eng, self.cnt[eng])
        self.ops[eng].append((waits, fn, (eng, 1)))
        self._update(tok, reads, writes)

    def dma(self, out_ap, in_ap, reads=(), writes=(), q='sp'):
        ch = self.ch_next
        self.ch_next = (ch + 1) % self.n_ch
        key = 'ch%d' % ch
        waits = self._deps(q, reads, writes)
        pv = self.ch_val[ch]
        if pv > 0 and self.seen[q].get(key, 0) < pv:
            waits.append((key, pv))
            self.seen[q][key] = pv
        self.ch_val[ch] += 16
        tok = (key, self.ch_val[ch])

        def fn(e, out_ap=out_ap, in_ap=in_ap):
            return e.dma_start(out=out_ap, in_=in_ap)
        self.ops[q].append((waits, fn, (key, 16)))
        self._update(tok, reads, writes)

    def barrier(self):
        allv = [(e, self.cnt[e]) for e in ENGS if self.cnt[e] > 0]
        allv += [('ch%d' % c, self.ch_val[c]) for c in range(self.n_ch) if self.ch_val[c] > 0]
        for e in ENGS:
            waits = []
            for k, v in allv:
                if k == e:
                    continue
                if self.seen[e].get(k, 0) < v:
                    waits.append((k, v))
                    self.seen[e][k] = v
            if waits:
                self.ops[e].append((waits, None, None))

    def emit(self):
        nc = self.nc
        self.barrier()
        keys = list(ENGS) + ['ch%d' % c for c in range(self.n_ch)]
        sems = {}
        for k in keys:
            sems[k] = self.stack.enter_context(nc.semaphore('s_' + k))
        ops = self.ops

        def run(e, name):
            for waits, fn, inc in ops[name]:
                for k, v in waits:
                    e.wait_ge(sems[k], v)
                if fn is not None:
                    fn(e).then_inc(sems[inc[0]], inc[1])

        with nc.Block() as block:
            @block.sync
            def _(e):
                run(e, 'sp')

            @block.tensor
            def _(e):
                run(e, 'pe')

            @block.scalar
            def _(e):
                run(e, 'act')

            @block.vector
            def _(e):
                run(e, 'dve')

            @block.gpsimd
            def _(e):
                run(e, 'pool')
        self.stack.close()


class KB:
    def __init__(self, nc, S, L):
        self.nc = nc
        self.P = Prog(nc)
        self.S = S
        self.L = L
        self.d = {}

    def din(self, name, shape, dt=F32):
        self.d[name] = self.nc.dram_tensor(name, list(shape), dt, kind="ExternalInput").ap()
        return self.d[name]

    def dout(self, name, shape, dt=F32):
        self.d[name] = self.nc.dram_tensor(name, list(shape), dt, kind="ExternalOutput").ap()
        return self.d[name]

    def dscr(self, name, shape, dt=F32):
        self.d[name] = self.nc.dram_tensor(name, list(shape), dt, kind="Internal").ap()
        return self.d[name]

    def act(self, out, in_, func, r, w, bias=None, scale=None, accum=None, eng='act'):
        kw = {}
        if bias is not None:
            kw['bias'] = bias
        if scale is not None:
            kw['scale'] = scale
        if accum is not None:
            kw['accum_out'] = accum
        self.P.op('act', lambda e: e.activation(out=out, in_=in_, func=func, **kw), reads=r, writes=w)

    def tt(self, out, in0, in1, op, r, w, eng='dve'):
        self.P.op(eng, lambda e: e.tensor_tensor(out=out, in0=in0, in1=in1, op=op), reads=r, writes=w)

    def ts(self, out, in0, s1, s2, op0, op1, r, w, eng='dve'):
        if op1 is None:
            self.P.op(eng, lambda e: e.tensor_scalar(out=out, in0=in0, scalar1=s1, scalar2=None, op0=op0), reads=r, writes=w)
        else:
            self.P.op(eng, lambda e: e.tensor_scalar(out=out, in0=in0, scalar1=s1, scalar2=s2, op0=op0, op1=op1), reads=r, writes=w)

    def stt(self, out, in0, scalar, in1, op0, op1, r, w):
        self.P.op('dve', lambda e: e.scalar_tensor_tensor(out=out, in0=in0, scalar=scalar, in1=in1, op0=op0, op1=op1), reads=r, writes=w)

    def cp(self, out, in_, r, w, eng='dve'):
        self.P.op(eng, lambda e: e.tensor_copy(out=out, in_=in_), reads=r, writes=w)

    def memset(self, ap, val, w, eng='dve'):
        self.P.op(eng, lambda e: e.memset(ap, val), writes=w)

    def recip(self, out, in_, r, w):
        self.P.op('dve', lambda e: e.reciprocal(out=out, in_=in_), reads=r, writes=w)

    def mm(self, out, lhsT, rhs, r, w, start=True, stop=True):
        self.P.op('pe', lambda e: e.matmul(out, lhsT=lhsT, rhs=rhs, start=start, stop=stop), reads=r, writes=w)

    def tr(self, out, in_, ident, r, w):
        self.P.op('pe', lambda e: e.transpose(out, in_, ident), reads=r, writes=w)

    def load(self, out, in_, w, r=(), q='sp'):
        self.P.dma(out, in_, reads=r, writes=w, q=q)

    def store(self, out, in_, r, w=(), q='sp'):
        self.P.dma(out, in_, reads=r, writes=w, q=q)


def load_consts(kb):
    P = kb.P
    c = {}
    cd = kb.d['consts']
    ct = P.sb([128, cd.shape[1]])
    b = Buf('consts')
    kb.load(ct[:], cd[:, :], [b])
    c['buf'] = b
    c['ident'] = ct[:, 0:128]
    c['bd'] = ct[:, 128:256]
    c['ones_col'] = ct[:, 256:257]
    c['m320'] = ct[0:64, 320:640]
    c['mdiag'] = ct[:, 640:768]
    c['ident64'] = ct[0:64, 0:64]
    c['mdiag_inv'] = ct[:, 768:896]
    cb = P.sb([128, 768], BF16)
    bb = Buf('constsb')
    kb.cp(cb[:], ct[:, 0:768], [b], [bb])
    c['bufb'] = bb
    c['ident_b'] = cb[:, 0:128]
    c['bd_b'] = cb[:, 128:256]
    c['mdiag_b'] = cb[:, 640:768]
    onesb = P.sb([128, 128], BF16)
    kb.memset(onesb[:], 1.0, [bb])
    c['ones_b'] = onesb
    kb.c = c


def make_consts():
    c = np.zeros((128, 896), np.float32)
    c[:, 0:128] = np.eye(128, dtype=np.float32)
    c[0:64, 128:192] = 1.0
    c[64:128, 192:256] = 1.0
    c[:, 256] = 1.0
    i = np.arange(64)[:, None]
    t = np.arange(64)[None, :]
    strict = (i < t).astype(np.float32)
    incl = (i <= t).astype(np.float32)
    c[0:64, 320:384] = strict
    c[0:64, 384:448] = incl
    c[0:64, 448:512] = strict
    c[0:64, 512:576] = incl
    c[0:64, 576:640] = (t < i).astype(np.float32)
    q = np.arange(128)[:, None]
    k = np.arange(128)[None, :]
    c[:, 640:768] = (k < q).astype(np.float32)
    c[:, 768:896] = (k >= q).astype(np.float32)
    return c


def phase_ada(kb, modT_dram):
    P = kb.P
    L = kb.L
    with P.scope():
        cT = P.sb([128, 8])
        bc = Buf()
        kb.load(cT[:], kb.d['cT'][:, :], [bc])
        sc2 = P.sb([128, 8, 2])
        bs = Buf()
        kb.act(sc2[:, :, 0], cT[:], AF.Silu, [bc], [bs])
        kb.act(sc2[:, :, 1], cT[:], AF.Silu, [bc], [bs])
        abT = P.sb([128, L, 48])
        bab = Buf()
        kb.load(abT[:], kb.d['ada_bT'].rearrange("l p j -> p l j"), [bab])
        wst = [P.sb([128, 8, 1024]) for _ in range(2)]
        bw = [Buf() for _ in range(2)]
        pm = P.ps([128, 128])
        bpm = Buf()
        mo = P.sb([128, L, 48])
        bmo = Buf()
        it = 0
        for l in range(L):
            for m in range(6):
                w = wst[it % 2]
                b = bw[it % 2]
                it += 1
                kb.load(w[:], kb.d['ada_w'][l, :, m * 1024:(m + 1) * 1024].rearrange("(k p) n -> p k n", p=128), [b])
                for j in range(8):
                    col = (m * 8 + j) * 2
                    for k in range(8):
                        kb.mm(pm[:, col:col + 2], w[:, k, j * 128:(j + 1) * 128], sc2[:, k, :], [b, bs], [bpm],
                              start=(k == 0), stop=(k == 7))
            pv = pm[:, 0:96].rearrange("p (j t) -> p j t", t=2)[:, :, 0]
            kb.tt(mo[:, l, :], pv, abT[:, l, :], ALU.add, [bpm, bab], [bmo])
        kb.store(modT_dram.rearrange("l p j -> p l j"), mo[:], [bmo])


def emit_norm(kb, xt, bx, n, gs, sh, bmod, sq, bsq, pss, bpss, rstd, brstd, tmp, btmp, h, bh):
    c = kb.c
    kb.act(sq[:, :, 0:n], xt[:, :, 0:n], AF.Square, [bx], [bsq])
    for k in range(8):
        kb.mm(pss[:, 0:n], c['ones_b'][:], sq[:, k, 0:n], [bsq, c['bufb']], [bpss], start=(k == 0), stop=(k == 7))
    kb.ts(rstd[:, 0:n], pss[:, 0:n], 1.0 / D_MODEL, NORM_EPS, ALU.mult, ALU.add, [bpss], [brstd])
    kb.act(rstd[:, 0:n], rstd[:, 0:n], AF.Ln, [brstd], [brstd])
    kb.act(rstd[:, 0:n], rstd[:, 0:n], AF.Exp, [brstd], [brstd], scale=-0.5)
    for k in range(8):
        kb.stt(tmp[:, k, 0:n], xt[:, k, 0:n], gs[:, k:k + 1], rstd[:, 0:n], ALU.mult, ALU.mult, [bx, bmod, brstd], [btmp[k % 2]])
        kb.act(h[:, k, 0:n], tmp[:, k, 0:n], AF.Identity, [btmp[k % 2], bmod], [bh], bias=sh[:, k:k + 1])


def load_cast_weight(kb, dst, bdst, src_rows, ncols, nk, stages, bst, it0=0):
    engs = ['act', 'pool', 'dve']
    it = it0
    for k in range(nk):
        src = src_rows(k)
        PW = stages[0].shape[1]
        for c0 in range(0, ncols, PW):
            c1 = min(ncols, c0 + PW)
            s = stages[it % len(stages)]
            b = bst[it % len(stages)]
            kb.load(s[:, 0:c1 - c0], src[:, c0:c1], [b])
            e = engs[it % 3]
            it += 1
            if e == 'act':
                kb.act(dst[:, k, c0:c1], s[:, 0:c1 - c0], AF.Copy, [b], [bdst])
            else:
                kb.cp(dst[:, k, c0:c1], s[:, 0:c1 - c0], [b], [bdst], eng=e)


def phase_T1(kb, l, XT, PT, modT, PTb=None):
    P = kb.P
    S = kb.S
    TN = 512
    with P.scope():
        Wb = P.sb([128, 8, P_TOTAL], BF16)
        bW = Buf()
        stages = [P.sb([128, 2048]) for _ in range(3)]
        bst = [Buf() for _ in range(3)]
        load_cast_weight(kb, Wb, bW, lambda k: kb.d['w_in'][l, k * 128:(k + 1) * 128, :], P_TOTAL, 8, stages, bst)
        modt = P.sb([128, 48])
        g1 = P.sb([128, 8])
        gs = P.sb([128, 8])
        bmod = Buf()
        kb.load(modt[:], modT[l], [bmod])
        kb.load(g1[:], kb.d['norm1_gT'][l], [bmod])
        kb.stt(gs[:], modt[:, 8:16], 1.0, g1[:], ALU.add, ALU.mult, [bmod], [bmod])
        sh = modt[:, 0:8]
        xts = [P.sb([128, 8, TN]) for _ in range(2)]
        bxs = [Buf() for _ in range(2)]
        sq = P.sb([128, 8, TN], BF16)
        bsq = Buf()
        pss = P.ps([128, TN])
        bpss = Buf()
        rstd = P.sb([128, TN])
        brstd = Buf()
        tmp = P.sb([128, 8, TN])
        btmp = [Buf(), Buf()]
        hs = [P.sb([128, 8, TN], BF16) for _ in range(2)]
        bhs = [Buf() for _ in range(2)]
        pps = [P.ps([128, TN]) for _ in range(4)]
        bpp = [Buf() for _ in range(4)]
        outs = [P.sb([128, TN]) for _ in range(4)]
        bo = [Buf() for _ in range(4)]
        outsb = [P.sb([128, TN], BF16) for _ in range(4)]
        io = 0
        for ti in range(S // TN):
            xt = xts[ti % 2]
            bx = bxs[ti % 2]
            h = hs[ti % 2]
            bh = bhs[ti % 2]
            kb.load(xt[:], XT[:, :, ti * TN:(ti + 1) * TN], [bx])
            emit_norm(kb, xt, bx, TN, gs, sh, bmod, sq, bsq, pss, bpss, rstd, brstd, tmp, btmp, h, bh)
            for c in range(23):
                pp = pps[io % 4]
                bp = bpp[io % 4]
                o = outs[io % 4]
                bob = bo[io % 4]
                for k in range(8):
                    kb.mm(pp[:], Wb[:, k, c * 128:(c + 1) * 128], h[:, k, :], [bW, bh], [bp], start=(k == 0), stop=(k == 7))
                if PTb is not None and c >= 17:
                    o = outsb[io % 4]
                    dst = PTb[c - 17, :, ti * TN:(ti + 1) * TN]
                else:
                    dst = PT[c, :, ti * TN:(ti + 1) * TN]
                if io % 2 == 0:
                    kb.cp(o[:], pp[:], [bp], [bob])
                else:
                    kb.act(o[:], pp[:], AF.Copy, [bp], [bob])
                kb.store(dst, o[:], [bob], q='pool')
                io += 1


def phase_A(kb, l, PT, YT):
    P = kb.P
    S = kb.S
    c = kb.c
    NCH = S // 64
    with P.scope():
        pv = P.sb([128, 4])
        bpv = Buf()
        kb.load(pv[:, 0:2], kb.d['pvA'][l], [bpv])
        kb.ts(pv[:, 2:3], pv[:, 0:1], 0.125, None, ALU.mult, None, [bpv], [bpv])
        biasT = P.sb([64, 4, 9, 64])
        bbias = Buf()
        kb.load(biasT[:], kb.d['biasA'][l].rearrange("h k j q -> k h j q"), [bbias])
        qf = P.sb([128, S])
        kf = P.sb([128, S])
        vf = P.sb([128, S])
        bq, bk, bv = Buf(), Buf(), Buf()
        qn = P.sb([128, S], BF16)
        kn = P.sb([128, S], BF16)
        bqn, bkn = Buf(), Buf()
        vtm = P.sb([64, NCH, 128], BF16)
        bvtm = Buf()
        ya = P.sb([128, S], BF16)
        bya = Buf()
        sq = P.sb([128, 512], BF16)
        bsq = Buf()
        rs = P.sb([128, 512])
        brs = Buf()
        pA = [P.ps([128, 512]) for _ in range(2)]
        bpA = [Buf(), Buf()]
        pS = [P.ps([64, 16, 64]) for _ in range(2)]
        bpS = [Buf(), Buf()]
        pO = [P.ps([128, 128]) for _ in range(2)]
        bpO = [Buf(), Buf()]
        ssb = [P.sb([64, 9, 64]) for _ in range(2)]
        bss = [Buf(), Buf()]
        eb = [P.sb([64, 9, 64], BF16) for _ in range(4)]
        beb = [Buf() for _ in range(4)]
        rd = [P.sb([128, 64]) for _ in range(2)]
        brd = [Buf(), Buf()]
        for hp in range(2):
            kb.load(qf[:], PT[0 + hp], [bq])
            kb.load(kf[:], PT[2 + hp], [bk])
            kb.load(vf[:], PT[4 + hp], [bv])
            it = 0
            for (src, bs_, dst, bd_, gcol) in ((qf, bq, qn, bqn, 2), (kf, bk, kn, bkn, 1)):
                for ti in range(S // 512):
                    sl = slice(ti * 512, (ti + 1) * 512)
                    pa = pA[it % 2]
                    bpa = bpA[it % 2]
                    it += 1
                    kb.act(sq[:], src[:, sl], AF.Square, [bs_], [bsq])
                    kb.mm(pa[:], c['bd_b'], sq[:], [bsq, c['bufb']], [bpa])
                    kb.ts(rs[:], pa[:], 1.0 / 64, NORM_EPS, ALU.mult, ALU.add, [bpa], [brs])
                    kb.act(rs[:], rs[:], AF.Ln, [brs], [brs])
                    kb.act(rs[:], rs[:], AF.Exp, [brs], [brs], scale=-0.5)
                    kb.stt(dst[:, sl], src[:, sl], pv[:, gcol:gcol + 1], rs[:], ALU.mult, ALU.mult, [bs_, bpv, brs], [bd_])
            for g in range(NCH // 4):
                pa = pA[it % 2]
                bpa = bpA[it % 2]
                it += 1
                for j in range(4):
                    n = g * 4 + j
                    kb.tr(pa[0:64, j * 128:(j + 1) * 128], vf[:, n * 64:(n + 1) * 64], c['ident'], [bv, c['buf']], [bpa])
                kb.cp(vtm[:, g * 4:(g + 1) * 4, :], pa[0:64, :].rearrange("p (j c) -> p j c", c=128), [bpa], [bvtm])
            unitsA = [(h, n) for h in range(2) for n in range(NCH)]
            NU = len(unitsA)

            def A1(i):
                h, n = unitsA[i]
                hb = 64 * h
                hg = hp * 2 + h
                j0 = 9 - (min(n, 8) + 1)
                ps_, bps = pS[i % 2], bpS[i % 2]
                s_, bs_ = ssb[i % 2], bss[i % 2]
                e_, be = eb[i % 4], beb[i % 4]
                for jj in range(j0, 9):
                    j = n - (8 - jj)
                    kb.mm(ps_[:, jj, :], kn[hb:hb + 64, j * 64:(j + 1) * 64], qn[hb:hb + 64, n * 64:(n + 1) * 64],
                          [bkn, bqn], [bps])
                kb.tt(s_[:, j0:9, :], ps_[:, j0:9, :], biasT[:, hg, j0:9, :], ALU.add, [bps, bbias], [bs_])
                kb.act(e_[:, j0:9, :], s_[:, j0:9, :], AF.Exp, [bs_], [be])

            def A2(i):
                h, n = unitsA[i]
                hb = 64 * h
                j0 = 9 - (min(n, 8) + 1)
                e_, be = eb[i % 4], beb[i % 4]
                po, bpo = pO[i % 2], bpO[i % 2]
                for jj in range(j0, 9):
                    j = n - (8 - jj)
                    kb.mm(po[hb:hb + 64, 0:64], vtm[:, j, hb:hb + 64], e_[:, jj, :], [bvtm, be], [bpo],
                          start=(jj == j0), stop=(jj == 8))
                for jj in range(j0, 9):
                    kb.mm(po[hb:hb + 64, 64:128], c['ones_b'][0:64, 0:64], e_[:, jj, :], [be, c['bufb']], [bpo],
                          start=(jj == j0), stop=(jj == 8))

            def A3(i):
                h, n = unitsA[i]
                hb = 64 * h
                po, bpo = pO[i % 2], bpO[i % 2]
                r_, br = rd[i % 2], brd[i % 2]
                kb.recip(r_[hb:hb + 64, :], po[hb:hb + 64, 64:128], [bpo], [br])
                kb.tt(ya[hb:hb + 64, n * 64:(n + 1) * 64], po[hb:hb + 64, 0:64], r_[hb:hb + 64, :], ALU.mult, [bpo, br], [bya])

            for t in range(NU + 3):
                if t < NU:
                    A1(t)
                if 0 <= t - 2 < NU:
                    A2(t - 2)
                if 0 <= t - 3 < NU:
                    A3(t - 3)
            kb.store(YT[0 + hp], ya[:], [bya], q='pool')


def phase_B(kb, l, PT, YT):
    P = kb.P
    S = kb.S
    c = kb.c
    with P.scope():
        pv = P.sb([128, 2, 12])
        bpv = Buf()
        kb.load(pv[:, :, 0:8], kb.d['pvB'][l].rearrange("h p n -> p h n"), [bpv])
        wm = P.sb([128, 2, 2, 128])
        bwm = Buf()
        kb.load(wm[:], kb.d['wB'][l].rearrange("h g p n -> p h g n"), [bwm])
        wmb = P.sb([128, 2, 2, 128], BF16)
        kb.cp(wmb[:], wm[:], [bwm], [bwm])
        for hp in range(2):
            kb.act(pv[:, hp, 8:9], pv[:, hp, 7:8], AF.Exp, [bpv], [bpv], scale=-1.0)
            kb.ts(pv[:, hp, 8:9], pv[:, hp, 8:9], 1.0, None, ALU.add, None, [bpv], [bpv])
            kb.act(pv[:, hp, 8:9], pv[:, hp, 8:9], AF.Ln, [bpv], [bpv])
            kb.ts(pv[:, hp, 9:10], pv[:, hp, 8:9], -8.0, None, ALU.mult, None, [bpv], [bpv])
        xb = P.sb([128, S])
        bxb = Buf()
        xc = P.sb([128, S])
        bxc = Buf()
        uu = P.sb([128, S])
        buu = Buf()
        xcb = P.sb([128, 512], BF16)
        bxcb = Buf()
        pr = [P.ps([128, 512]) for _ in range(2)]
        bpr = [Buf(), Buf()]
        pi = [P.ps([128, 512]) for _ in range(2)]
        bpi = [Buf(), Buf()]
        rg = P.sb([128, 512])
        brg = Buf()
        ig = P.sb([128, 512])
        big = Buf()
        t1 = P.sb([128, 512])
        bt1 = Buf()
        yb = P.sb([128, S], BF16)
        byb = Buf()
        for hp in range(2):
            kb.load(xb[:], PT[6 + hp], [bxb])
            kb.act(xc[:], xb[:], AF.Identity, [bxb, bpv], [bxc], bias=pv[:, hp, 4:5], scale=pv[:, hp, 3:4])
            for i in range(3):
                sft = 3 - i
                kb.stt(xc[:, sft:S], xb[:, 0:S - sft], pv[:, hp, i:i + 1], xc[:, sft:S], ALU.mult, ALU.add, [bxb, bpv, bxc], [bxc])
            for ti in range(S // 512):
                sl = slice(ti * 512, (ti + 1) * 512)
                kb.cp(xcb[:], xc[:, sl], [bxc], [bxcb])
                kb.mm(pr[ti % 2][:], wmb[:, hp, 0, :], xcb[:], [bwm, bxcb], [bpr[ti % 2]])
                kb.mm(pi[ti % 2][:], wmb[:, hp, 1, :], xcb[:], [bwm, bxcb], [bpi[ti % 2]])
                kb.act(rg[:], pr[ti % 2][:], AF.Sigmoid, [bpr[ti % 2], bpv], [brg], bias=pv[:, hp, 5:6])
                kb.act(ig[:], pi[ti % 2][:], AF.Sigmoid, [bpi[ti % 2], bpv], [big], bias=pv[:, hp, 6:7])
                kb.act(xb[:, sl], rg[:], AF.Exp, [brg, bpv, bxc], [bxb], scale=pv[:, hp, 9:10])
                kb.tt(t1[:], xb[:, sl], xb[:, sl], ALU.mult, [bxb], [bt1])
                kb.ts(t1[:], t1[:], -1.0, 1.0, ALU.mult, ALU.add, [bt1], [bt1])
                kb.act(t1[:], t1[:], AF.Sqrt, [bt1], [bt1])
                kb.tt(ig[:], ig[:], xc[:, sl], ALU.mult, [big, bxc], [big])
                kb.tt(uu[:, sl], t1[:], ig[:], ALU.mult, [bt1, big], [buu])
            kb.P.op('dve', lambda e: e.tensor_tensor_scan(out=xc[:], data0=xb[:], data1=uu[:], initial=0.0,
                                                          op0=ALU.mult, op1=ALU.add), reads=[bxb, buu], writes=[bxc])
            kb.load(uu[:], PT[8 + hp], [buu])
            kb.act(uu[:], uu[:], AF.Gelu, [buu], [buu])
            kb.tt(yb[:], xc[:], uu[:], ALU.mult, [bxc, buu], [byb])
            kb.store(YT[2 + hp], yb[:], [byb], q='pool')


def phase_D(kb, l, PT, YT, PTb=None):
    P = kb.P
    S = kb.S
    c = kb.c
    NQ = S // 128
    WMAX = 1024
    NB = 3
    PC = min(2048, S)
    with P.scope():
        stgb = P.sb([128, PC], BF16)
        bstg = Buf()
        qb = P.sb([128, S], BF16)
        kbf = P.sb([128, S], BF16)
        bqb, bkb = Buf(), Buf()
        vtm = P.sb([128, NQ, 128], BF16)
        bvtm = Buf()
        yd = P.sb([128, S], BF16)
        byd = Buf()
        pz = [P.ps([128, WMAX]) for _ in range(2)]
        bpz = [Buf(), Buf()]
        pt = [P.ps([128, WMAX], BF16) for _ in range(2)]
        bpt = [Buf(), Buf()]
        po = [P.ps([128, 512]) for _ in range(2)]
        bpo = [Buf(), Buf()]
        kp = [P.sb([128, WMAX]) for _ in range(NB)]
        bkp = [Buf() for _ in range(NB)]
        Pb = [P.sb([128, WMAX + 1]) for _ in range(NB)]
        bPb = [Buf() for _ in range(NB)]
        wt = [P.sb([128, WMAX], BF16) for _ in range(NB)]
        bwt = [Buf() for _ in range(NB)]
        wT = [P.sb([128, WMAX], BF16) for _ in range(3)]
        bwT = [Buf() for _ in range(3)]
        for hp in range(2):
            kb.load(qb[:], PTb[0 + hp], [bqb])
            kb.load(kbf[:], PTb[2 + hp], [bkb])
            it = 0
            for t0 in range(0, S, PC):
                kb.load(stgb[:], PTb[4 + hp][:, t0:t0 + PC], [bstg])
                for g in range(PC // 1024):
                    pa = pt[it % 2]
                    bpa = bpt[it % 2]
                    it += 1
                    for j in range(8):
                        kb.tr(pa[:, j * 128:(j + 1) * 128], stgb[:, (g * 8 + j) * 128:(g * 8 + j + 1) * 128], c['ident_b'], [bstg, c['bufb']], [bpa])
                    n0 = t0 // 128 + g * 8
                    kb.cp(vtm[:, n0:n0 + 8, :], pa[:, 0:1024].rearrange("p (j c) -> p j c", c=128), [bpa], [bvtm])
            units = []
            io = 0
            for h in range(2):
                for qi in range(NQ):
                    kend = (qi + 1) * 128
                    nsb = (kend + WMAX - 1) // WMAX
                    for s in range(nsb - 1, -1, -1):
                        k0 = s * WMAX
                        k1 = min(kend, k0 + WMAX)
                        units.append(dict(h=h, qi=qi, k0=k0, W=k1 - k0, diag=(s == nsb - 1), first=(s == nsb - 1), last=(s == 0), io=io))
                    io += 1
            NU = len(units)

            def S1(i):
                u = units[i]
                hb = 64 * u['h']
                W = u['W']
                z, bz = pz[i % 2], bpz[i % 2]
                k_, bk_ = kp[i % NB], bkp[i % NB]
                for b0 in range(0, W, 512):
                    b1 = min(W, b0 + 512)
                    kb.mm(z[:, b0:b1], qb[hb:hb + 64, u['qi'] * 128:(u['qi'] + 1) * 128], kbf[hb:hb + 64, u['k0'] + b0:u['k0'] + b1],
                          [bqb, bkb], [bz])
                kb.act(k_[:, 0:W], z[:, 0:W], AF.Sigmoid, [bz], [bk_], scale=-0.125)
                if u['diag']:
                    kb.tt(k_[:, W - 128:W], k_[:, W - 128:W], c['mdiag_inv'], ALU.max, [bk_, c['buf']], [bk_])

            def S2(i):
                u = units[i]
                W = u['W']
                k_, bk_ = kp[i % NB], bkp[i % NB]
                p_, bp_ = Pb[i % NB], bPb[i % NB]
                if u['first']:
                    init = 1.0
                    rds = [bk_]
                    kb.memset(p_[:, W:W + 1], 1.0, [bp_], eng='pool')
                else:
                    pp_, bpp_ = Pb[(i - 1) % NB], bPb[(i - 1) % NB]
                    init = pp_[:, 0:1]
                    rds = [bk_, bpp_]
                    kb.act(p_[:, W:W + 1], pp_[:, 0:1], AF.Copy, [bpp_], [bp_])
                kb.P.op('dve', lambda e, p_=p_, k_=k_, W=W, init=init: e.tensor_tensor_scan(
                    out=p_[:, 0:W][:, ::-1], data0=k_[:, 0:W][:, ::-1], data1=c['ones_col'].broadcast_to([128, W]),
                    initial=init, op0=ALU.mult, op1=ALU.mult), reads=rds + [c['buf']], writes=[bp_])

            def S2b(i):
                u = units[i]
                W = u['W']
                p_, bp_ = Pb[i % NB], bPb[i % NB]
                w_, bw_ = wt[i % NB], bwt[i % NB]
                kb.tt(w_[:, 0:W], p_[:, 1:W + 1], p_[:, 0:W], ALU.subtract, [bp_], [bw_])

            def S3a(i):
                u = units[i]
                W = u['W']
                w_, bw_ = wt[i % NB], bwt[i % NB]
                T_, bT_ = pt[i % 2], bpt[i % 2]
                wT_, bwT_ = wT[i % 3], bwT[i % 3]
                nblk = W // 128
                for bi in range(nblk):
                    kb.tr(T_[:, bi * 128:(bi + 1) * 128], w_[:, bi * 128:(bi + 1) * 128], c['ident_b'], [bw_, c['bufb']], [bT_])
                kb.act(wT_[:, 0:W], T_[:, 0:W], AF.Copy, [bT_], [bwT_])

            def S3b(i):
                u = units[i]
                hb = 64 * u['h']
                W = u['W']
                wT_, bwT_ = wT[i % 3], bwT[i % 3]
                pot, bpot = po[u['io'] % 2], bpo[u['io'] % 2]
                nblk = W // 128
                for bi in range(nblk):
                    kb.mm(pot[hb:hb + 64, 0:128], vtm[:, u['k0'] // 128 + bi, hb:hb + 64], wT_[:, bi * 128:(bi + 1) * 128],
                          [bvtm, bwT_], [bpot], start=(u['first'] and bi == 0), stop=(u['last'] and bi == nblk - 1))
                if u['last']:
                    kb.act(yd[hb:hb + 64, u['qi'] * 128:(u['qi'] + 1) * 128], pot[hb:hb + 64, 0:128], AF.Copy, [bpot], [byd])

            for t in range(NU + 4):
                if 0 <= t - 1 < NU:
                    S2(t - 1)
                if t < NU:
                    S1(t)
                if 0 <= t - 2 < NU:
                    S2b(t - 2)
                if 0 <= t - 3 < NU:
                    S3a(t - 3)
                if 0 <= t - 4 < NU:
                    S3b(t - 4)
            kb.store(YT[6 + hp], yd[:], [byd], q='pool')


def phase_C(kb, l, PT, YT):
    P = kb.P
    S = kb.S
    c = kb.c
    SEG = min(512, S)
    NC_ = SEG // 64
    NCH = NC_ * 2
    with P.scope():
        pv = P.sb([128, 2, 16])
        bpv = Buf()
        kb.load(pv[:, :, 0:11], kb.d['pvC'][l].rearrange("h p n -> p h n"), [bpv])
        for hp in range(2):
            kb.ts(pv[:, hp, 11:15], pv[:, hp, 0:4], -1.0, 1.0, ALU.mult, ALU.add, [bpv], [bpv])
            kb.ts(pv[:, hp, 15:16], pv[:, hp, 7:8], -1.0, 1.0, ALU.mult, ALU.add, [bpv], [bpv])
        wag = P.sb([128, 2, 128])
        bwag = Buf()
        kb.load(wag[:], kb.d['wC'][l].rearrange("h p n -> p h n"), [bwag])
        rmask = P.sb([128, SEG])
        brm = Buf()
        kb.memset(rmask[:], 1.0, [brm])
        kb.memset(rmask[:].rearrange("p (n c) -> p n c", c=64)[:, :, 0:1], 0.0, [brm])

        def T():
            return P.sb([128, SEG]), Buf()
        raw = [(P.sb([128, SEG + 1]), Buf()) for _ in range(4)]
        sh = [T() for _ in range(4)]
        (ld, bld), (al, bal), (gf, bgf), (kk, bkk), (km, bkm), (bon, bbon) = T(), T(), T(), T(), T(), T()
        (Lc, bLc), (t1, bt1), (t2, bt2), (bv_, bbv) = T(), T(), T(), T()
        AR = P.sb([128, NC_, 128])
        bAR = Buf()
        (btl, bbt), (ktl, bkt), (bh, bbh), (kh, bkh) = T(), T(), T(), T()
        BHt = P.sb([64, NC_, 128], BF16)
        KHt = P.sb([64, NC_, 128], BF16)
        Vt = P.sb([64, NC_, 128], BF16)
        ARb = P.sb([128, NC_, 128], BF16)
        bARb = Buf()
        AMb = P.sb([64, NCH, 192], BF16)
        bAMb = [Buf() for _ in range(NCH // 8)]
        Mtb = P.sb([64, NCH, 64], BF16)
        bMtb = [Buf() for _ in range(NCH // 8)]
        Tb = [P.sb([64, 2, 64], BF16) for _ in range(2)]
        bTb = [Buf(), Buf()]
        bBHt, bKHt, bVt = Buf(), Buf(), Buf()
        AM = P.sb([64, NCH, 320])
        bAM = [Buf() for _ in range(NCH)]
        Bm = [P.sb([64, NCH, 64]) for _ in range(2)]
        Btm = [P.sb([64, NCH, 64]) for _ in range(2)]
        Mt = [P.sb([64, NCH, 64]) for _ in range(2)]
        bBm = [[Buf() for _ in range(NCH // 8)] for _ in range(2)]
        bBtm = [[Buf() for _ in range(NCH // 8)] for _ in range(2)]
        bMt = [[Buf() for _ in range(NCH // 8)] for _ in range(2)]
        TS = [P.sb([64, 2, 64]) for _ in range(2)]
        AR1 = P.sb([64, NC_, 128], BF16)
        bAR1 = Buf()
        GCc = P.sb([128, NC_])
        bGCc = Buf()
        G1 = P.sb([64, NC_])
        bG1 = Buf()
        bTS = [Buf(), Buf()]
        Wsb = P.sb([64, 128], BF16)
        bWsb = Buf()
        Usb = P.sb([64, 128], BF16)
        bUsb = Buf()
        ysb = P.sb([64, 8, 64])
        bysb = Buf()
        ysq = P.sb([64, 8, 64])
        bysq = Buf()
        st1 = P.sb([64, 8])
        st2 = P.sb([64, 8])
        bst1, bst2 = Buf(), Buf()
        yc = P.sb([128, SEG], BF16)
        byc = Buf()
        yfm = P.sb([128, 256])
        byfm = Buf()
        pg = [P.ps([128, 512]) for _ in range(2)]
        bpg = [Buf(), Buf()]
        pinv = [P.ps([64, 512]) for _ in range(3)]
        bpinv = [Buf() for _ in range(3)]
        pW = P.ps([64, 128])
        bpW = Buf()
        pY = P.ps([64, 512])
        bpY = Buf()
        pT = P.ps([64, 128])
        bpT = Buf()
        ig = [0]

        def gbank():
            i = ig[0] % 2
            ig[0] += 1
            return pg[i], bpg[i]

        for hp in range(2):
            chans = [10 + hp, 12 + hp, 14 + hp, 16]
            kb.memset(TS[0][:], 0.0, [bTS[0]])
            kb.memset(Tb[0][:], 0.0, [bTb[0]])
            tsi = 0
            for sg in range(S // SEG):
                t0 = sg * SEG
                for i in range(4):
                    rt_, br_ = raw[i]
                    if t0 == 0:
                        kb.memset(rt_[:, 0:1], 0.0, [br_])
                        kb.load(rt_[:, 1:SEG + 1], PT[chans[i]][:, 0:SEG], [br_])
                    else:
                        kb.load(rt_[:, :], PT[chans[i]][:, t0 - 1:t0 + SEG], [br_])
                    s_, bs_ = sh[i]
                    mcol = 3 if i == 3 else i
                    kb.ts(s_[:], rt_[:, 1:SEG + 1], pv[:, hp, 11 + mcol:12 + mcol], None, ALU.mult, None, [br_, bpv], [bs_])
                    kb.stt(s_[:], rt_[:, 0:SEG], pv[:, hp, mcol:mcol + 1], s_[:], ALU.mult, ALU.add, [br_, bpv, bs_], [bs_])
                (rs, brs), (ks, bks), (vs, bvs), (xs, bxs) = sh
                kb.act(t1[0:32, :], xs[0:32, :], AF.Tanh, [bxs], [bt1])
                kb.act(t2[64:128, :], xs[64:128, :], AF.Sigmoid, [bxs], [bt2])
                for ti in range(SEG // 512):
                    sl = slice(ti * 512, (ti + 1) * 512)
                    p_, bp_ = gbank()
                    kb.mm(p_[:], wag[0:32, hp, :], t1[0:32, sl], [bwag, bt1], [bp_])
                    kb.act(ld[:, sl], p_[:], AF.Sigmoid, [bp_, bpv], [bld], bias=pv[:, hp, 4:5])
                    p_, bp_ = gbank()
                    kb.mm(p_[:], wag[32:64, hp, :], xs[32:64, sl], [bwag, bxs], [bp_])
                    kb.act(al[:, sl], p_[:], AF.Sigmoid, [bp_, bpv], [bal], bias=pv[:, hp, 5:6])
                    p_, bp_ = gbank()
                    kb.mm(p_[:], wag[64:128, hp, :], t2[64:128, sl], [bwag, bt2], [bp_])
                    kb.cp(gf[:, sl], p_[:], [bp_], [bgf])
                kb.ts(ld[:], ld[:], -0.6065306597126334, None, ALU.mult, None, [bld], [bld])
                kb.ts(kk[:], ks[:], pv[:, hp, 6:7], None, ALU.mult, None, [bks, bpv], [bkk])
                kb.tt(t1[:], kk[:], kk[:], ALU.mult, [bkk], [bt1])
                for ti in range(SEG // 512):
                    sl = slice(ti * 512, (ti + 1) * 512)
                    p_, bp_ = gbank()
                    kb.mm(p_[:], c['bd'], t1[:, sl], [c['buf'], bt1], [bp_])
                    kb.act(t2[:, sl], p_[:], AF.Sqrt, [bp_], [bt2])
                kb.ts(t2[:], t2[:], 1e-12, None, ALU.max, None, [bt2], [bt2])
                kb.recip(t2[:], t2[:], [bt2], [bt2])
                kb.tt(kk[:], kk[:], t2[:], ALU.mult, [bkk, bt2], [bkk])
                kb.ts(km[:], al[:], pv[:, hp, 7:8], pv[:, hp, 15:16], ALU.mult, ALU.add, [bal, bpv], [bkm])
                kb.tt(km[:], km[:], ks[:], ALU.mult, [bkm, bks], [bkm])
                kb.stt(t1[:], rs[:], pv[:, hp, 8:9], km[:], ALU.mult, ALU.mult, [brs, bpv, bkm], [bt1])
                for ti in range(SEG // 512):
                    sl = slice(ti * 512, (ti + 1) * 512)
                    p_, bp_ = gbank()
                    kb.mm(p_[:], c['bd'], t1[:, sl], [c['buf'], bt1], [bp_])
                    kb.tt(bon[:, sl], p_[:], vs[:, sl], ALU.mult, [bp_, bvs], [bbon])
                kb.P.op('dve', lambda e: e.tensor_tensor_scan(out=Lc[:], data0=rmask[:], data1=ld[:], initial=0.0,
                                                              op0=ALU.mult, op1=ALU.add), reads=[brm, bld], writes=[bLc])
                AR4 = AR[:].rearrange("p n (two c) -> p n two c", two=2)
                v3 = lambda t: t[:].rearrange("p (n c) -> p n c", c=64)
                kb.act(t1[:], Lc[:], AF.Exp, [bLc], [bt1])
                kb.tt(AR4[:, :, 1, :], v3(rs), v3(t1), ALU.mult, [brs, bt1], [bAR])
                kb.tt(t2[:], Lc[:], ld[:], ALU.subtract, [bLc, bld], [bt2])
                kb.act(t2[:], t2[:], AF.Exp, [bt2], [bt2])
                kb.stt(AR4[:, :, 0, :], v3(kk), -1.0, v3(t2), ALU.mult, ALU.mult, [bkk, bt2], [bAR])
                kb.tt(bv_[:], kk[:], al[:], ALU.mult, [bkk, bal], [bbv])
                kb.act(t2[:], Lc[:], AF.Exp, [bLc], [bt2], scale=-1.0)
                kb.tt(btl[:], bv_[:], t2[:], ALU.mult, [bbv, bt2], [bbt])
                kb.tt(ktl[:], km[:], t2[:], ALU.mult, [bkm, bt2], [bkt])
                LCb = v3(Lc)[:, :, 63:64].broadcast_to([128, NC_, 64])
                kb.tt(v3(t2), LCb, v3(Lc), ALU.subtract, [bLc], [bt2])
                kb.act(t2[:], t2[:], AF.Exp, [bt2], [bt2])
                kb.tt(bh[:], bv_[:], t2[:], ALU.mult, [bbv, bt2], [bbh])
                kb.tt(kh[:], km[:], t2[:], ALU.mult, [bkm, bt2], [bkh])
                kb.cp(GCc[:], v3(t1)[:, :, 63], [bt1], [bGCc])
                kb.load(G1[:], GCc[64:128, :], [bG1], r=[bGCc])
                kb.cp(ARb[:], AR[:], [bAR], [bARb])
                kb.load(AR1[:], ARb[64:128, :, :], [bAR1], r=[bARb])
                ARh = [ARb, AR1]
                bARh = [bARb, bAR1]
                GCh = [GCc, G1]
                bGCh = [bGCc, bG1]
                for (src, bsrc, dst, bdst) in ((bh, bbh, BHt, bBHt), (kh, bkh, KHt, bKHt), (vs, bvs, Vt, bVt)):
                    for g in range(NC_ // 4):
                        p_, bp_ = gbank()
                        for j in range(4):
                            n = g * 4 + j
                            kb.tr(p_[0:64, j * 128:(j + 1) * 128], src[:, n * 64:(n + 1) * 64], c['ident'], [bsrc, c['buf']], [bp_])
                        kb.cp(dst[:, g * 4:(g + 1) * 4, :], p_[0:64, :].rearrange("p (j c) -> p j c", c=128), [bp_], [bdst])
                for n in range(NC_):
                    for h in range(2):
                        hb = 64 * h
                        ci = n * 2 + h
                        p_, bp_ = gbank()
                        csl = slice(n * 64, (n + 1) * 64)
                        kb.mm(p_[0:64, 0:128], btl[hb:hb + 64, csl], AR[hb:hb + 64, n, :], [bbt, bAR], [bp_])
                        kb.mm(p_[0:64, 128:256], ktl[hb:hb + 64, csl], AR[hb:hb + 64, n, :], [bkt, bAR], [bp_])
                        kb.mm(p_[0:64, 256:320], AR[hb:hb + 64, n, 0:64], btl[hb:hb + 64, csl], [bbt, bAR], [bp_])
                        kb.tt(AM[:, ci, :], p_[0:64, 0:320], c['m320'], ALU.mult, [bp_, c['buf']], [bAM[ci]])
                for g in range(NCH // 8):
                    gs_ = slice(g * 8, (g + 1) * 8)
                    rb = [bAM[ci] for ci in range(g * 8, (g + 1) * 8)]
                    kb.cp(Bm[0][:, gs_, :], AM[:, gs_, 256:320], rb, [bBm[0][g]])
                    kb.cp(Btm[0][:, gs_, :], AM[:, gs_, 0:64], rb, [bBtm[0][g]], eng='pool')
                    kb.tt(Mt[0][:, gs_, :], AM[:, gs_, 0:64], c['ident64'].unsqueeze(1).broadcast_to([64, 8, 64]), ALU.add,
                          rb + [c['buf']], [bMt[0][g]])
                cur = 0
                for step in range(5):
                    nxt = 1 - cur
                    for g in range(NCH // 8):
                        gs_ = slice(g * 8, (g + 1) * 8)
                        pB, pBt, pM = pinv
                        for j in range(8):
                            ci = g * 8 + j
                            kb.mm(pB[:, j * 64:(j + 1) * 64], Btm[cur][:, ci, :], Bm[cur][:, ci, :], [bBtm[cur][g], bBm[cur][g]], [bpinv[0]])
                        if step < 4:
                            for j in range(8):
                                ci = g * 8 + j
                                kb.mm(pBt[:, j * 64:(j + 1) * 64], Bm[cur][:, ci, :], Btm[cur][:, ci, :], [bBtm[cur][g], bBm[cur][g]], [bpinv[1]])
                        kb.cp(Bm[nxt][:, gs_, :], pB[:].rearrange("p (j c) -> p j c", c=64), [bpinv[0]], [bBm[nxt][g]])
                        if step < 4:
                            kb.act(Btm[nxt][:, gs_, :], pBt[:].rearrange("p (j c) -> p j c", c=64), AF.Copy, [bpinv[1]], [bBtm[nxt][g]])
                        for j in range(8):
                            ci = g * 8 + j
                            kb.mm(pM[:, j * 64:(j + 1) * 64], Bm[nxt][:, ci, :], Mt[cur][:, ci, :], [bBm[nxt][g], bMt[cur][g]], [bpinv[2]])
                        kb.tt(Mt[nxt][:, gs_, :], Mt[cur][:, gs_, :], pM[:].rearrange("p (j c) -> p j c", c=64), ALU.add,
                              [bpinv[2], bMt[cur][g]], [bMt[nxt][g]])
                    cur = nxt
                MtF = Mtb
                bMtF = bMtb
                for g in range(NCH // 8):
                    gs_ = slice(g * 8, (g + 1) * 8)
                    kb.cp(Mtb[:, gs_, :], Mt[cur][:, gs_, :], [bMt[cur][g]], [bMtb[g]])
                    kb.act(AMb[:, gs_, :], AM[:, gs_, 64:256], AF.Copy, [bAM[ci] for ci in range(g * 8, (g + 1) * 8)], [bAMb[g]])
                for n in range(NC_):
                    Tc = TS[tsi % 2]
                    bTc = bTS[tsi % 2]
                    Tn = TS[(tsi + 1) % 2]
                    bTn = bTS[(tsi + 1) % 2]
                    Tcb, bTcb = Tb[tsi % 2], bTb[tsi % 2]
                    Tnb, bTnb = Tb[(tsi + 1) % 2], bTb[(tsi + 1) % 2]
                    tsi += 1
                    for h in range(2):
                        hb = 64 * h
                        ci = n * 2 + h
                        kb.mm(pW[:, h * 64:(h + 1) * 64], ARh[h][0:64, n, 0:64], Tcb[:, h, :], [bARh[h], bTcb], [bpW], start=True, stop=False)
                        kb.mm(pW[:, h * 64:(h + 1) * 64], AMb[:, ci, 64:128], Vt[:, n, hb:hb + 64], [bAMb[ci // 8], bVt], [bpW], start=False, stop=True)
                    kb.cp(Wsb[:], pW[:], [bpW], [bWsb])
                    for h in range(2):
                        ci = n * 2 + h
                        kb.mm(pW[:, h * 64:(h + 1) * 64], MtF[:, ci, :], Wsb[:, h * 64:(h + 1) * 64], [bMtF[ci // 8], bWsb], [bpW])
                    kb.cp(Usb[:], pW[:], [bpW], [bUsb])
                    q4 = n % 4
                    for h in range(2):
                        hb = 64 * h
                        ci = n * 2 + h
                        ysl = slice((q4 * 2 + h) * 64, (q4 * 2 + h + 1) * 64)
                        kb.mm(pY[:, ysl], ARh[h][0:64, n, 64:128], Tcb[:, h, :], [bARh[h], bTcb], [bpY], start=True, stop=False)
                        kb.mm(pY[:, ysl], AMb[:, ci, 0:64], Usb[:, h * 64:(h + 1) * 64], [bAMb[ci // 8], bUsb], [bpY], start=False, stop=False)
                        kb.mm(pY[:, ysl], AMb[:, ci, 128:192], Vt[:, n, hb:hb + 64], [bAMb[ci // 8], bVt], [bpY], start=False, stop=True)
                    for h in range(2):
                        hb = 64 * h
                        kb.mm(pT[:, h * 64:(h + 1) * 64], BHt[:, n, hb:hb + 64], Usb[:, h * 64:(h + 1) * 64], [bBHt, bUsb], [bpT], start=True, stop=False)
                        kb.mm(pT[:, h * 64:(h + 1) * 64], KHt[:, n, hb:hb + 64], Vt[:, n, hb:hb + 64], [bKHt, bVt], [bpT], start=False, stop=True)
                    for h in range(2):
                        kb.stt(Tnb[:, h, :], Tc[:, h, :], GCh[h][0:64, n:n + 1], pT[:, h * 64:(h + 1) * 64], ALU.mult, ALU.add,
                               [bTc, bGCh[h], bpT], [bTnb])
                    for h in range(2):
                        kb.stt(Tn[:, h, :], Tc[:, h, :], GCh[h][0:64, n:n + 1], pT[:, h * 64:(h + 1) * 64], ALU.mult, ALU.add,
                               [bTc, bGCh[h], bpT], [bTn])
                    if q4 == 3:
                        kb.act(ysb[:], pY[:].rearrange("p (g v) -> p g v", v=64), AF.Copy, [bpY], [bysb])
                        kb.P.op('dve', lambda e: e.tensor_reduce(out=st1[:], in_=ysb[:], axis=AX.X, op=ALU.add), reads=[bysb], writes=[bst1])
                        kb.ts(st1[:], st1[:], 1.0 / 64, None, ALU.mult, None, [bst1], [bst1])
                        kb.tt(ysb[:], ysb[:], st1[:].unsqueeze(2).broadcast_to([64, 8, 64]), ALU.subtract, [bysb, bst1], [bysb])
                        kb.act(ysq[:], ysb[:], AF.Square, [bysb], [bysq])
                        kb.P.op('dve', lambda e: e.tensor_reduce(out=st2[:], in_=ysq[:], axis=AX.X, op=ALU.add), reads=[bysq], writes=[bst2])
                        kb.ts(st2[:], st2[:], 1.0 / 64, RWKV_LN_EPS, ALU.mult, ALU.add, [bst2], [bst2])
                        kb.act(st2[:], st2[:], AF.Sqrt, [bst2], [bst2])
                        kb.recip(st2[:], st2[:], [bst2], [bst2])
                        kb.tt(ysb[:], ysb[:], st2[:].unsqueeze(2).broadcast_to([64, 8, 64]), ALU.mult, [bysb, bst2], [bysb])
                        p_, bp_ = gbank()
                        for j in range(4):
                            kb.tr(p_[:, j * 64:(j + 1) * 64], ysb[:, 2 * j:2 * j + 2, :].rearrange("p g v -> p (g v)"), c['ident64'],
                                  [bysb, c['buf']], [bp_])
                        tsl = slice((n - 3) * 64, (n + 1) * 64)
                        kb.ts(yfm[:], p_[:, 0:256], pv[:, hp, 9:10], pv[:, hp, 10:11], ALU.mult, ALU.add, [bp_, bpv], [byfm])
                        kb.tt(yfm[:], yfm[:], bon[:, tsl], ALU.add, [byfm, bbon], [byfm])
                        kb.tt(yc[:, tsl], yfm[:], gf[:, tsl], ALU.mult, [byfm, bgf], [byc])
                kb.store(YT[4 + hp][:, t0:t0 + SEG], yc[:], [byc], q='pool')


def gen_D(kb, l, PT, YT):
    P = kb.P
    S = kb.S
    c = kb.c
    NQ = S // 128
    WMAX = 1024
    NB = 3
    PC = min(512, S)
    with contextlib.nullcontext():
        stg = P.sb([128, PC])
        bstg = Buf()
        qb = P.sb([128, S], BF16)
        kbf = P.sb([128, S], BF16)
        bqb, bkb = Buf(), Buf()
        vtm = P.sb([128, NQ, 128], BF16)
        bvtm = Buf()
        yd = P.sb([128, S], BF16)
        byd = Buf()
        pz = [P.ps([128, WMAX])]
        bpz = [Buf()]
        pt = [P.ps([128, WMAX], BF16)]
        bpt = [Buf()]
        po = [P.ps([128, 512])]
        bpo = [Buf()]
        kp = [P.sb([128, WMAX]) for _ in range(NB)]
        bkp = [Buf() for _ in range(NB)]
        Pb = [P.sb([128, WMAX + 1]) for _ in range(NB)]
        bPb = [Buf() for _ in range(NB)]
        wt = [P.sb([128, WMAX], BF16) for _ in range(NB)]
        bwt = [Buf() for _ in range(NB)]
        wT = [P.sb([128, WMAX], BF16) for _ in range(2)]
        bwT = [Buf(), Buf()]
        for hp in range(2):
            for t0 in range(0, S, PC):
                sl = slice(t0, t0 + PC)
                kb.load(stg[:], PT[17 + hp][:, sl], [bstg])
                kb.act(qb[:, sl], stg[:], AF.Copy, [bstg], [bqb], scale=0.125)
                kb.load(stg[:], PT[19 + hp][:, sl], [bstg])
                kb.cp(kbf[:, sl], stg[:], [bstg], [bkb])
            it = 0
            for t0 in range(0, S, PC):
                kb.load(stg[:], PT[21 + hp][:, t0:t0 + PC], [bstg])
                for g in range(PC // 512):
                    pa = pz[0]
                    bpa = bpz[0]
                    it += 1
                    for j in range(4):
                        kb.tr(pa[:, j * 128:(j + 1) * 128], stg[:, (g * 4 + j) * 128:(g * 4 + j + 1) * 128], c['ident'], [bstg, c['buf']], [bpa])
                    n0 = t0 // 128 + g * 4
                    kb.cp(vtm[:, n0:n0 + 4, :], pa[:, 0:512].rearrange("p (j c) -> p j c", c=128), [bpa], [bvtm])
            units = []
            io = 0
            for h in range(2):
                for qi in range(NQ):
                    kend = (qi + 1) * 128
                    nsb = (kend + WMAX - 1) // WMAX
                    for s in range(nsb - 1, -1, -1):
                        k0 = s * WMAX
                        k1 = min(kend, k0 + WMAX)
                        units.append(dict(h=h, qi=qi, k0=k0, W=k1 - k0, diag=(s == nsb - 1), first=(s == nsb - 1), last=(s == 0), io=io))
                    io += 1
            NU = len(units)

            def S1(i):
                u = units[i]
                hb = 64 * u['h']
                W = u['W']
                z, bz = pz[0], bpz[0]
                k_, bk_ = kp[i % NB], bkp[i % NB]
                for b0 in range(0, W, 512):
                    b1 = min(W, b0 + 512)
                    kb.mm(z[:, b0:b1], qb[hb:hb + 64, u['qi'] * 128:(u['qi'] + 1) * 128], kbf[hb:hb + 64, u['k0'] + b0:u['k0'] + b1],
                          [bqb, bkb], [bz])
                kb.act(k_[:, 0:W], z[:, 0:W], AF.Sigmoid, [bz], [bk_], scale=-1.0)
                if u['diag']:
                    kb.tt(k_[:, W - 128:W], k_[:, W - 128:W], c['mdiag_inv'], ALU.max, [bk_, c['buf']], [bk_])

            def S2(i):
                u = units[i]
                W = u['W']
                k_, bk_ = kp[i % NB], bkp[i % NB]
                p_, bp_ = Pb[i % NB], bPb[i % NB]
                if u['first']:
                    init = 1.0
                    rds = [bk_]
                    kb.memset(p_[:, W:W + 1], 1.0, [bp_], eng='pool')
                else:
                    pp_, bpp_ = Pb[(i - 1) % NB], bPb[(i - 1) % NB]
                    init = pp_[:, 0:1]
                    rds = [bk_, bpp_]
                    kb.act(p_[:, W:W + 1], pp_[:, 0:1], AF.Copy, [bpp_], [bp_])
                kb.P.op('dve', lambda e, p_=p_, k_=k_, W=W, init=init: e.tensor_tensor_scan(
                    out=p_[:, 0:W][:, ::-1], data0=k_[:, 0:W][:, ::-1], data1=c['ones_col'].broadcast_to([128, W]),
                    initial=init, op0=ALU.mult, op1=ALU.mult), reads=rds + [c['buf']], writes=[bp_])

            def S2b(i):
                u = units[i]
                W = u['W']
                p_, bp_ = Pb[i % NB], bPb[i % NB]
                w_, bw_ = wt[i % NB], bwt[i % NB]
                hW = 128 if W >= 256 else 0
                if hW > 0:
                    kb.tt(w_[:, 0:hW], p_[:, 1:hW + 1], p_[:, 0:hW], ALU.subtract, [bp_], [bw_])
                kb.tt(w_[:, hW:W], p_[:, hW + 1:W + 1], p_[:, hW:W], ALU.subtract, [bp_], [bw_], eng='pool')

            def S3(i):
                u = units[i]
                hb = 64 * u['h']
                W = u['W']
                w_, bw_ = wt[i % NB], bwt[i % NB]
                T_, bT_ = pt[0], bpt[0]
                wT_, bwT_ = wT[i % 2], bwT[i % 2]
                pot, bpot = po[0], bpo[0]
                nblk = W // 128
                for bi in range(nblk):
                    kb.tr(T_[:, bi * 128:(bi + 1) * 128], w_[:, bi * 128:(bi + 1) * 128], c['ident_b'], [bw_, c['bufb']], [bT_])
                kb.act(wT_[:, 0:W], T_[:, 0:W], AF.Copy, [bT_], [bwT_])
                for bi in range(nblk):
                    kb.mm(pot[hb:hb + 64, 0:128], vtm[:, u['k0'] // 128 + bi, hb:hb + 64], wT_[:, bi * 128:(bi + 1) * 128],
                          [bvtm, bwT_], [bpot], start=(u['first'] and bi == 0), stop=(u['last'] and bi == nblk - 1))
                if u['last']:
                    kb.act(yd[hb:hb + 64, u['qi'] * 128:(u['qi'] + 1) * 128], pot[hb:hb + 64, 0:128], AF.Copy, [bpot], [byd])

            for t in range(NU + 3):
                if 0 <= t - 1 < NU:
                    S2(t - 1)
                if t < NU:
                    S1(t)
                if 0 <= t - 2 < NU:
                    S2b(t - 2)
                if 0 <= t - 3 < NU:
                    S3(t - 3)
                yield
            kb.store(YT[6 + hp], yd[:], [byd], q='pool')


def gen_C(kb, l, PT, YT):
    P = kb.P
    S = kb.S
    c = kb.c
    SEG = min(256, S)
    TW = min(512, SEG)
    NC_ = SEG // 64
    NCH = NC_ * 2
    with contextlib.nullcontext():
        pv = P.sb([128, 2, 16])
        bpv = Buf()
        kb.load(pv[:, :, 0:11], kb.d['pvC'][l].rearrange("h p n -> p h n"), [bpv])
        for hp in range(2):
            kb.ts(pv[:, hp, 11:15], pv[:, hp, 0:4], -1.0, 1.0, ALU.mult, ALU.add, [bpv], [bpv])
            kb.ts(pv[:, hp, 15:16], pv[:, hp, 7:8], -1.0, 1.0, ALU.mult, ALU.add, [bpv], [bpv])
        wag = P.sb([128, 2, 128])
        bwag = Buf()
        kb.load(wag[:], kb.d['wC'][l].rearrange("h p n -> p h n"), [bwag])
        rmask = P.sb([128, SEG])
        brm = Buf()
        kb.memset(rmask[:], 1.0, [brm])
        kb.memset(rmask[:].rearrange("p (n c) -> p n c", c=64)[:, :, 0:1], 0.0, [brm])

        def T():
            return P.sb([128, SEG]), Buf()
        raw = [(P.sb([128, SEG + 1]), Buf()) for _ in range(4)]
        sh = [T() for _ in range(4)]
        (ld, bld), (al, bal), (gf, bgf), (kk, bkk), (km, bkm), (bon, bbon) = T(), T(), T(), T(), T(), T()
        (Lc, bLc), (t1, bt1), (t2, bt2), (bv_, bbv) = T(), T(), T(), T()
        AR = P.sb([128, NC_, 128])
        bAR = Buf()
        (btl, bbt), (ktl, bkt), (bh, bbh), (kh, bkh) = T(), T(), T(), T()
        BHt = P.sb([64, NC_, 128])
        KHt = P.sb([64, NC_, 128])
        Vt = P.sb([64, NC_, 128])
        bBHt, bKHt, bVt = Buf(), Buf(), Buf()
        AM = P.sb([64, NCH, 320])
        bAM = [Buf() for _ in range(NCH)]
        Bm = [P.sb([64, NCH, 64]) for _ in range(2)]
        Btm = [P.sb([64, NCH, 64]) for _ in range(2)]
        Mt = [P.sb([64, NCH, 64]) for _ in range(2)]
        bBm = [[Buf() for _ in range(NCH // 4)] for _ in range(2)]
        bBtm = [[Buf() for _ in range(NCH // 4)] for _ in range(2)]
        bMt = [[Buf() for _ in range(NCH // 4)] for _ in range(2)]
        TS = [P.sb([64, 2, 64]) for _ in range(2)]
        AR1 = P.sb([64, NC_, 128])
        bAR1 = Buf()
        GCc = P.sb([128, NC_])
        bGCc = Buf()
        G1 = P.sb([64, NC_])
        bG1 = Buf()
        bTS = [Buf(), Buf()]
        Wsb = P.sb([64, 128])
        bWsb = Buf()
        Usb = P.sb([64, 128])
        bUsb = Buf()
        ysb = P.sb([64, 8, 64])
        bysb = Buf()
        ysq = P.sb([64, 8, 64])
        bysq = Buf()
        st1 = P.sb([64, 8])
        st2 = P.sb([64, 8])
        bst1, bst2 = Buf(), Buf()
        yc = P.sb([128, SEG], BF16)
        byc = Buf()
        yfm = P.sb([128, 256])
        byfm = Buf()
        pg = [P.ps([128, 512])]
        bpg = [Buf()]
        bankX = P.ps([64, 512])
        bankY = P.ps([64, 512])
        pinv = [bankX[:, 0:256], bankX[:, 256:512], bankY[:, 0:256]]
        bbX = Buf()
        bbY = Buf()
        bpinv = [bbX, bbX, bbY]
        pW = bankY[:, 256:384]
        bpW = bbY
        pT = bankY[:, 384:512]
        bpT = bbY
        pY = P.ps([64, 512])
        bpY = Buf()
        ig = [0]

        def gbank():
            i = 0
            ig[0] += 1
            return pg[i], bpg[i]

        for hp in range(2):
            chans = [10 + hp, 12 + hp, 14 + hp, 16]
            kb.memset(TS[0][:], 0.0, [bTS[0]])
            tsi = 0
            for sg in range(S // SEG):
                t0 = sg * SEG
                for i in range(4):
                    rt_, br_ = raw[i]
                    if t0 == 0:
                        kb.memset(rt_[:, 0:1], 0.0, [br_])
                        kb.load(rt_[:, 1:SEG + 1], PT[chans[i]][:, 0:SEG], [br_])
                    else:
                        kb.load(rt_[:, :], PT[chans[i]][:, t0 - 1:t0 + SEG], [br_])
                    s_, bs_ = sh[i]
                    mcol = 3 if i == 3 else i
                    kb.ts(s_[:], rt_[:, 1:SEG + 1], pv[:, hp, 11 + mcol:12 + mcol], None, ALU.mult, None, [br_, bpv], [bs_])
                    kb.stt(s_[:], rt_[:, 0:SEG], pv[:, hp, mcol:mcol + 1], s_[:], ALU.mult, ALU.add, [br_, bpv, bs_], [bs_])
                (rs, brs), (ks, bks), (vs, bvs), (xs, bxs) = sh
                yield
                kb.act(t1[0:32, :], xs[0:32, :], AF.Tanh, [bxs], [bt1])
                kb.act(t2[64:128, :], xs[64:128, :], AF.Sigmoid, [bxs], [bt2])
                for ti in range(SEG // TW):
                    sl = slice(ti * TW, (ti + 1) * TW)
                    p_, bp_ = gbank()
                    kb.mm(p_[:, 0:TW], wag[0:32, hp, :], t1[0:32, sl], [bwag, bt1], [bp_])
                    kb.act(ld[:, sl], p_[:, 0:TW], AF.Sigmoid, [bp_, bpv], [bld], bias=pv[:, hp, 4:5])
                    p_, bp_ = gbank()
                    kb.mm(p_[:, 0:TW], wag[32:64, hp, :], xs[32:64, sl], [bwag, bxs], [bp_])
                    kb.act(al[:, sl], p_[:, 0:TW], AF.Sigmoid, [bp_, bpv], [bal], bias=pv[:, hp, 5:6])
                    p_, bp_ = gbank()
                    kb.mm(p_[:, 0:TW], wag[64:128, hp, :], t2[64:128, sl], [bwag, bt2], [bp_])
                    kb.cp(gf[:, sl], p_[:, 0:TW], [bp_], [bgf])
                kb.ts(ld[:], ld[:], -0.6065306597126334, None, ALU.mult, None, [bld], [bld])
                kb.ts(kk[:], ks[:], pv[:, hp, 6:7], None, ALU.mult, None, [bks, bpv], [bkk])
                kb.tt(t1[:], kk[:], kk[:], ALU.mult, [bkk], [bt1])
                for ti in range(SEG // TW):
                    sl = slice(ti * TW, (ti + 1) * TW)
                    p_, bp_ = gbank()
                    kb.mm(p_[:, 0:TW], c['bd'], t1[:, sl], [c['buf'], bt1], [bp_])
                    kb.act(t2[:, sl], p_[:, 0:TW], AF.Sqrt, [bp_], [bt2])
                kb.ts(t2[:], t2[:], 1e-12, None, ALU.max, None, [bt2], [bt2])
                kb.recip(t2[:], t2[:], [bt2], [bt2])
                kb.tt(kk[:], kk[:], t2[:], ALU.mult, [bkk, bt2], [bkk])
                kb.ts(km[:], al[:], pv[:, hp, 7:8], pv[:, hp, 15:16], ALU.mult, ALU.add, [bal, bpv], [bkm])
                kb.tt(km[:], km[:], ks[:], ALU.mult, [bkm, bks], [bkm])
                kb.stt(t1[:], rs[:], pv[:, hp, 8:9], km[:], ALU.mult, ALU.mult, [brs, bpv, bkm], [bt1])
                for ti in range(SEG // TW):
                    sl = slice(ti * TW, (ti + 1) * TW)
                    p_, bp_ = gbank()
                    kb.mm(p_[:, 0:TW], c['bd'], t1[:, sl], [c['buf'], bt1], [bp_])
                    kb.tt(bon[:, sl], p_[:, 0:TW], vs[:, sl], ALU.mult, [bp_, bvs], [bbon])
                kb.P.op('dve', lambda e: e.tensor_tensor_scan(out=Lc[:], data0=rmask[:], data1=ld[:], initial=0.0,
                                                              op0=ALU.mult, op1=ALU.add), reads=[brm, bld], writes=[bLc])
                yield
                AR4 = AR[:].rearrange("p n (two c) -> p n two c", two=2)
                v3 = lambda t: t[:].rearrange("p (n c) -> p n c", c=64)
                kb.act(t1[:], Lc[:], AF.Exp, [bLc], [bt1])
                kb.tt(AR4[:, :, 1, :], v3(rs), v3(t1), ALU.mult, [brs, bt1], [bAR])
                kb.tt(t2[:], Lc[:], ld[:], ALU.subtract, [bLc, bld], [bt2])
                kb.act(t2[:], t2[:], AF.Exp, [bt2], [bt2])
                kb.stt(AR4[:, :, 0, :], v3(kk), -1.0, v3(t2), ALU.mult, ALU.mult, [bkk, bt2], [bAR])
                kb.tt(bv_[:], kk[:], al[:], ALU.mult, [bkk, bal], [bbv])
                kb.act(t2[:], Lc[:], AF.Exp, [bLc], [bt2], scale=-1.0)
                kb.tt(btl[:], bv_[:], t2[:], ALU.mult, [bbv, bt2], [bbt])
                kb.tt(ktl[:], km[:], t2[:], ALU.mult, [bkm, bt2], [bkt])
                LCb = v3(Lc)[:, :, 63:64].broadcast_to([128, NC_, 64])
                kb.tt(v3(t2), LCb, v3(Lc), ALU.subtract, [bLc], [bt2])
                kb.act(t2[:], t2[:], AF.Exp, [bt2], [bt2])
                kb.tt(bh[:], bv_[:], t2[:], ALU.mult, [bbv, bt2], [bbh])
                kb.tt(kh[:], km[:], t2[:], ALU.mult, [bkm, bt2], [bkh])
                kb.cp(GCc[:], v3(t1)[:, :, 63], [bt1], [bGCc])
                kb.load(G1[:], GCc[64:128, :], [bG1], r=[bGCc])
                kb.load(AR1[:], AR[64:128, :, :], [bAR1], r=[bAR])
                ARh = [AR, AR1]
                bARh = [bAR, bAR1]
                GCh = [GCc, G1]
                bGCh = [bGCc, bG1]
                yield
                for (src, bsrc, dst, bdst) in ((bh, bbh, BHt, bBHt), (kh, bkh, KHt, bKHt), (vs, bvs, Vt, bVt)):
                    for g in range(NC_ // 4):
                        p_, bp_ = gbank()
                        for j in range(4):
                            n = g * 4 + j
                            kb.tr(p_[0:64, j * 128:(j + 1) * 128], src[:, n * 64:(n + 1) * 64], c['ident'], [bsrc, c['buf']], [bp_])
                        kb.cp(dst[:, g * 4:(g + 1) * 4, :], p_[0:64, :].rearrange("p (j c) -> p j c", c=128), [bp_], [bdst])
                for n in range(NC_):
                    for h in range(2):
                        hb = 64 * h
                        ci = n * 2 + h
                        p_, bp_ = gbank()
                        csl = slice(n * 64, (n + 1) * 64)
                        kb.mm(p_[0:64, 0:128], btl[hb:hb + 64, csl], AR[hb:hb + 64, n, :], [bbt, bAR], [bp_])
                        kb.mm(p_[0:64, 128:256], ktl[hb:hb + 64, csl], AR[hb:hb + 64, n, :], [bkt, bAR], [bp_])
                        kb.mm(p_[0:64, 256:320], AR[hb:hb + 64, n, 0:64], btl[hb:hb + 64, csl], [bbt, bAR], [bp_])
                        kb.tt(AM[:, ci, :], p_[0:64, 0:320], c['m320'], ALU.mult, [bp_, c['buf']], [bAM[ci]])
                        yield
                for g in range(NCH // 4):
                    gs_ = slice(g * 4, (g + 1) * 4)
                    rb = [bAM[ci] for ci in range(g * 4, (g + 1) * 4)]
                    kb.cp(Bm[0][:, gs_, :], AM[:, gs_, 256:320], rb, [bBm[0][g]])
                    kb.cp(Btm[0][:, gs_, :], AM[:, gs_, 0:64], rb, [bBtm[0][g]], eng='pool')
                    kb.tt(Mt[0][:, gs_, :], AM[:, gs_, 0:64], c['ident64'].unsqueeze(1).broadcast_to([64, 4, 64]), ALU.add,
                          rb + [c['buf']], [bMt[0][g]])
                cur = 0
                for step in range(5):
                    nxt = 1 - cur
                    for g in range(NCH // 4):
                        gs_ = slice(g * 4, (g + 1) * 4)
                        pB, pBt, pM = pinv
                        for j in range(4):
                            ci = g * 4 + j
                            kb.mm(pB[:, j * 64:(j + 1) * 64], Btm[cur][:, ci, :], Bm[cur][:, ci, :], [bBtm[cur][g], bBm[cur][g]], [bpinv[0]])
                        if step < 4:
                            for j in range(4):
                                ci = g * 4 + j
                                kb.mm(pBt[:, j * 64:(j + 1) * 64], Bm[cur][:, ci, :], Btm[cur][:, ci, :], [bBtm[cur][g], bBm[cur][g]], [bpinv[1]])
                        kb.cp(Bm[nxt][:, gs_, :], pB[:].rearrange("p (j c) -> p j c", c=64), [bpinv[0]], [bBm[nxt][g]])
                        if step < 4:
                            kb.cp(Btm[nxt][:, gs_, :], pBt[:].rearrange("p (j c) -> p j c", c=64), [bpinv[1]], [bBtm[nxt][g]])
                        for j in range(4):
                            ci = g * 4 + j
                            kb.mm(pM[:, j * 64:(j + 1) * 64], Bm[nxt][:, ci, :], Mt[cur][:, ci, :], [bBm[nxt][g], bMt[cur][g]], [bpinv[2]])
                        kb.tt(Mt[nxt][:, gs_, :], Mt[cur][:, gs_, :], pM[:].rearrange("p (j c) -> p j c", c=64), ALU.add,
                              [bpinv[2], bMt[cur][g]], [bMt[nxt][g]])
                        yield
                    cur = nxt
                MtF = Mt[cur]
                bMtF = bMt[cur]
                for n in range(NC_):
                    Tc = TS[tsi % 2]
                    bTc = bTS[tsi % 2]
                    Tn = TS[(tsi + 1) % 2]
                    bTn = bTS[(tsi + 1) % 2]
                    tsi += 1
                    yield
                    for h in range(2):
                        hb = 64 * h
                        ci = n * 2 + h
                        kb.mm(pW[:, h * 64:(h + 1) * 64], ARh[h][0:64, n, 0:64], Tc[:, h, :], [bARh[h], bTc], [bpW], start=True, stop=False)
                        kb.mm(pW[:, h * 64:(h + 1) * 64], AM[:, ci, 128:192], Vt[:, n, hb:hb + 64], [bAM[ci], bVt], [bpW], start=False, stop=True)
                    kb.cp(Wsb[:], pW[:], [bpW], [bWsb])
                    for h in range(2):
                        ci = n * 2 + h
                        kb.mm(pW[:, h * 64:(h + 1) * 64], MtF[:, ci, :], Wsb[:, h * 64:(h + 1) * 64], [bMtF[ci // 4], bWsb], [bpW])
                    kb.cp(Usb[:], pW[:], [bpW], [bUsb])
                    q4 = n % 4
                    for h in range(2):
                        hb = 64 * h
                        ci = n * 2 + h
                        ysl = slice((q4 * 2 + h) * 64, (q4 * 2 + h + 1) * 64)
                        kb.mm(pY[:, ysl], ARh[h][0:64, n, 64:128], Tc[:, h, :], [bARh[h], bTc], [bpY], start=True, stop=False)
                        kb.mm(pY[:, ysl], AM[:, ci, 64:128], Usb[:, h * 64:(h + 1) * 64], [bAM[ci], bUsb], [bpY], start=False, stop=False)
                        kb.mm(pY[:, ysl], AM[:, ci, 192:256], Vt[:, n, hb:hb + 64], [bAM[ci], bVt], [bpY], start=False, stop=True)
                    for h in range(2):
                        hb = 64 * h
                        kb.mm(pT[:, h * 64:(h + 1) * 64], BHt[:, n, hb:hb + 64], Usb[:, h * 64:(h + 1) * 64], [bBHt, bUsb], [bpT], start=True, stop=False)
                        kb.mm(pT[:, h * 64:(h + 1) * 64], KHt[:, n, hb:hb + 64], Vt[:, n, hb:hb + 64], [bKHt, bVt], [bpT], start=False, stop=True)
                    for h in range(2):
                        kb.stt(Tn[:, h, :], Tc[:, h, :], GCh[h][0:64, n:n + 1], pT[:, h * 64:(h + 1) * 64], ALU.mult, ALU.add,
                               [bTc, bGCh[h], bpT], [bTn])
                    if q4 == 3:
                        kb.act(ysb[:], pY[:].rearrange("p (g v) -> p g v", v=64), AF.Copy, [bpY], [bysb])
                        kb.P.op('dve', lambda e: e.tensor_reduce(out=st1[:], in_=ysb[:], axis=AX.X, op=ALU.add), reads=[bysb], writes=[bst1])
                        kb.ts(st1[:], st1[:], 1.0 / 64, None, ALU.mult, None, [bst1], [bst1])
                        kb.tt(ysb[:], ysb[:], st1[:].unsqueeze(2).broadcast_to([64, 8, 64]), ALU.subtract, [bysb, bst1], [bysb])
                        kb.act(ysq[:], ysb[:], AF.Square, [bysb], [bysq])
                        kb.P.op('dve', lambda e: e.tensor_reduce(out=st2[:], in_=ysq[:], axis=AX.X, op=ALU.add), reads=[bysq], writes=[bst2])
                        kb.ts(st2[:], st2[:], 1.0 / 64, RWKV_LN_EPS, ALU.mult, ALU.add, [bst2], [bst2])
                        kb.act(st2[:], st2[:], AF.Sqrt, [bst2], [bst2])
                        kb.recip(st2[:], st2[:], [bst2], [bst2])
                        kb.tt(ysb[:], ysb[:], st2[:].unsqueeze(2).broadcast_to([64, 8, 64]), ALU.mult, [bysb, bst2], [bysb])
                        p_, bp_ = gbank()
                        for j in range(4):
                            kb.tr(p_[:, j * 64:(j + 1) * 64], ysb[:, 2 * j:2 * j + 2, :].rearrange("p g v -> p (g v)"), c['ident64'],
                                  [bysb, c['buf']], [bp_])
                        tsl = slice((n - 3) * 64, (n + 1) * 64)
                        kb.ts(yfm[:], p_[:, 0:256], pv[:, hp, 9:10], pv[:, hp, 10:11], ALU.mult, ALU.add, [bp_, bpv], [byfm])
                        kb.tt(yfm[:], yfm[:], bon[:, tsl], ALU.add, [byfm, bbon], [byfm])
                        kb.tt(yc[:, tsl], yfm[:], gf[:, tsl], ALU.mult, [byfm, bgf], [byc])
                kb.store(YT[4 + hp][:, t0:t0 + SEG], yc[:], [byc], q='pool')


def phase_CD(kb, l, PT, YT):
    P = kb.P
    S = kb.S
    with P.scope():
        gC = gen_C(kb, l, PT, YT)
        gD = gen_D(kb, l, PT, YT)
        SEG = min(256, S)
        nseg = S // SEG
        nc_ = SEG // 64
        totC = 2 * nseg * (3 + 2 * nc_ + 5 * (2 * nc_ // 4) + nc_)
        NQ = S // 128
        nu = sum((qi + 1 + 7) // 8 for qi in range(NQ)) * 2
        totD = 2 * (nu + 3)
        ratio = totC / float(totD)
        acc = 0.0
        doneC = doneD = False
        while not (doneC and doneD):
            if not doneD:
                try:
                    next(gD)
                except StopIteration:
                    doneD = True
            acc += ratio
            while (acc >= 1.0 or doneD) and not doneC:
                acc -= 1.0
                try:
                    next(gC)
                except StopIteration:
                    doneC = True


def phase_T2a(kb, l, XT, YT, H2, modT, XTsrc=None):
    P = kb.P
    S = kb.S
    TN = 512
    with P.scope():
        Wb = P.sb([128, 8, D_MODEL], BF16)
        bW = Buf()
        stages = [P.sb([128, 2048]) for _ in range(3)]
        bst = [Buf() for _ in range(3)]
        load_cast_weight(kb, Wb, bW, lambda k: kb.d['w_out'][l, k * 128:(k + 1) * 128, :], D_MODEL, 8, stages, bst)
        modt = P.sb([128, 48])
        g2 = P.sb([128, 8])
        gs = P.sb([128, 8])
        bmod = Buf()
        kb.load(modt[:], modT[l], [bmod])
        kb.load(g2[:], kb.d['norm2_gT'][l], [bmod])
        kb.stt(gs[:], modt[:, 32:40], 1.0, g2[:], ALU.add, ALU.mult, [bmod], [bmod])
        sh = modt[:, 24:32]
        gt = modt[:, 16:24]
        xts = [P.sb([128, 8, TN]) for _ in range(2)]
        bxs = [Buf() for _ in range(2)]
        yts = [P.sb([128, 8, TN], BF16) for _ in range(2)]
        bys = [Buf() for _ in range(2)]
        sq = P.sb([128, 8, TN], BF16)
        bsq = Buf()
        pss = P.ps([128, TN])
        bpss = Buf()
        rstd = P.sb([128, TN])
        brstd = Buf()
        tmp = P.sb([128, 8, TN])
        btmp = [Buf(), Buf()]
        hs = [P.sb([128, 8, TN], BF16) for _ in range(2)]
        bhs = [Buf() for _ in range(2)]
        pps = [P.ps([128, TN]) for _ in range(4)]
        bpp = [Buf() for _ in range(4)]
        io = 0
        for ti in range(S // TN):
            xt, bx = xts[ti % 2], bxs[ti % 2]
            yt, by = yts[ti % 2], bys[ti % 2]
            h, bh = hs[ti % 2], bhs[ti % 2]
            tsl = slice(ti * TN, (ti + 1) * TN)
            kb.load(xt[:], (XT if XTsrc is None else XTsrc)[:, :, tsl], [bx])
            kb.load(yt[:], YT[:, :, tsl].rearrange("k p t -> p k t"), [by])
            for j in range(8):
                pp, bp = pps[io % 4], bpp[io % 4]
                io += 1
                for k in range(8):
                    kb.mm(pp[:], Wb[:, k, j * 128:(j + 1) * 128], yt[:, k, :], [bW, by], [bp], start=(k == 0), stop=(k == 7))
                kb.stt(xt[:, j, :], pp[:], gt[:, j:j + 1], xt[:, j, :], ALU.mult, ALU.add, [bp, bmod, bx], [bx])
            kb.store(XT[:, :, tsl], xt[:], [bx], q='pool')
            emit_norm(kb, xt, bx, TN, gs, sh, bmod, sq, bsq, pss, bpss, rstd, brstd, tmp, btmp, h, bh)
            kb.store(H2[:, :, tsl], h[:], [bh], q='pool')


def phase_T2b(kb, l, XT, H2, modT):
    P = kb.P
    S = kb.S
    TN = 512
    NF = 2 * D_FF // 128
    NV = D_FF // 128
    with P.scope():
        Wu = P.sb([128, 8, 2 * D_FF], BF16)
        bWu = Buf()
        Wd = P.sb([128, NV, D_MODEL], BF16)
        bWd = Buf()
        with P.scope():
            stages = [P.sb([128, 1024]) for _ in range(3)]
            bst = [Buf() for _ in range(3)]
            load_cast_weight(kb, Wu, bWu, lambda k: kb.d['ffn_up'][l, k * 128:(k + 1) * 128, :], 2 * D_FF, 8, stages, bst)
            load_cast_weight(kb, Wd, bWd, lambda k: kb.d['ffn_down'][l, k * 128:(k + 1) * 128, :], D_MODEL, NV, stages, bst)
        modt = P.sb([128, 48])
        bmod = Buf()
        kb.load(modt[:], modT[l], [bmod])
        gt = modt[:, 40:48]
        cw = P.sb([128, 4, NF])
        bcw = Buf()
        kb.load(cw[:], kb.d['ffn_cwT'][l], [bcw])
        halo = P.sb([128, NF, 2])
        bhalo = Buf()
        kb.memset(halo[:], 0.0, [bhalo])
        xt = P.sb([128, 8, TN])
        bx = Buf()
        h = P.sb([128, 8, TN], BF16)
        bh = Buf()
        pps = [P.ps([128, 512]) for _ in range(4)]
        bpp = [Buf() for _ in range(4)]
        us = [P.sb([128, TN + 2]) for _ in range(3)]
        bus = [Buf() for _ in range(3)]
        uc = [P.sb([128, TN]) for _ in range(3)]
        buc = [Buf() for _ in range(3)]
        ov = [P.sb([128, TN]) for _ in range(2)]
        bov = [Buf() for _ in range(2)]
        g_ = P.sb([128, NV, TN], BF16)
        bg_ = Buf()
        io = 0
        iu = 0
        for ti in range(S // TN):
            tsl = slice(ti * TN, (ti + 1) * TN)
            kb.load(xt[:], XT[:, :, tsl], [bx])
            kb.load(h[:], H2[:, :, tsl], [bh])
            for i in range(NV):
                v_, bv_ = ov[i % 2], bov[i % 2]
                for f in (i, i + NV):
                    pp, bp = pps[io % 4], bpp[io % 4]
                    io += 1
                    u, bu = us[iu % 3], bus[iu % 3]
                    o, bo_ = uc[iu % 3], buc[iu % 3]
                    iu += 1
                    for k in range(8):
                        kb.mm(pp[:, 0:TN], Wu[:, k, f * 128:(f + 1) * 128], h[:, k, :], [bWu, bh], [bp], start=(k == 0), stop=(k == 7))
                    kb.cp(u[:, 0:2], halo[:, f, :], [bhalo], [bu])
                    kb.act(u[:, 2:TN + 2], pp[:, 0:TN], AF.Copy, [bp], [bu])
                    kb.act(halo[:, f, :], u[:, TN:TN + 2], AF.Copy, [bu], [bhalo])
                    kb.ts(o[:], u[:, 2:TN + 2], cw[:, 2, f:f + 1], cw[:, 3, f:f + 1], ALU.mult, ALU.add, [bu, bcw], [bo_])
                    kb.stt(o[:], u[:, 1:TN + 1], cw[:, 1, f:f + 1], o[:], ALU.mult, ALU.add, [bu, bcw, bo_], [bo_])
                    if f < NV:
                        kb.stt(v_[:], u[:, 0:TN], cw[:, 0, f:f + 1], o[:], ALU.mult, ALU.add, [bu, bcw, bo_], [bv_])
                    else:
                        kb.stt(o[:], u[:, 0:TN], cw[:, 0, f:f + 1], o[:], ALU.mult, ALU.add, [bu, bcw, bo_], [bo_])
                        kb.act(o[:], o[:], AF.Gelu, [bo_], [bo_])
                        kb.tt(g_[:, i, :], v_[:], o[:], ALU.mult, [bv_, bo_], [bg_])
            for j in range(8):
                pp, bp = pps[io % 4], bpp[io % 4]
                io += 1
                for k in range(NV):
                    kb.mm(pp[:, 0:TN], Wd[:, k, j * 128:(j + 1) * 128], g_[:, k, :], [bWd, bg_], [bp], start=(k == 0), stop=(k == NV - 1))
                kb.stt(xt[:, j, :], pp[:, 0:TN], gt[:, j:j + 1], xt[:, j, :], ALU.mult, ALU.add, [bp, bmod, bx], [bx])
            kb.store(XT[:, :, tsl], xt[:], [bx], q='pool')


PARAM_SPECS = None


def host_params(inp, L):
    f = lambda a: np.ascontiguousarray(np.asarray(a, dtype=np.float32))
    o = {}
    o['ada_w'] = f(inp['ada_w'][:L])
    o['ada_bT'] = f(np.asarray(inp['ada_b'])[:L].reshape(L, 48, 128).transpose(0, 2, 1))
    o['norm1_gT'] = f(np.asarray(inp['norm1_g'])[:L].reshape(L, 8, 128).transpose(0, 2, 1))
    o['norm2_gT'] = f(np.asarray(inp['norm2_g'])[:L].reshape(L, 8, 128).transpose(0, 2, 1))
    o['w_in'] = f(inp['w_in'][:L])
    o['w_out'] = f(inp['w_out'][:L])
    o['ffn_up'] = f(inp['ffn_up'][:L])
    o['ffn_down'] = f(inp['ffn_down'][:L])
    cw = np.asarray(inp['ffn_conv_w'])[:L]
    cb = np.asarray(inp['ffn_conv_b'])[:L]
    cwb = np.concatenate([cw, cb[:, None, :]], axis=1)
    o['ffn_cwT'] = f(cwb.reshape(L, 4, 44, 128).transpose(0, 3, 1, 2))
    qg = np.asarray(inp['attn_q_gain'])[:L]
    kg = np.asarray(inp['attn_k_gain'])[:L]
    o['pvA'] = f(np.stack([np.tile(qg, (1, 2)), np.tile(kg, (1, 2))], axis=-1))
    rb = np.asarray(inp['attn_rel_bias'])[:L]
    qi = np.arange(64)[None, None, :]
    jj = np.arange(9)[None, :, None]
    kj = np.arange(64)[:, None, None]
    dist = 512 + qi - (jj * 64 + kj)
    rel = np.clip(dist, -128, 128) + 128
    o['biasA'] = f(rb[:, :, rel])
    cwB = np.asarray(inp['lru_conv_w'])[:L]
    cols = [cwB[:, 0], cwB[:, 1], cwB[:, 2], cwB[:, 3], np.asarray(inp['lru_conv_b'])[:L], np.asarray(inp['lru_ra_b'])[:L],
            np.asarray(inp['lru_ri_b'])[:L], np.asarray(inp['lru_lambda'])[:L]]
    pvB = np.stack(cols, axis=-1)
    o['pvB'] = f(pvB.reshape(L, 2, 128, 8))
    ra = np.asarray(inp['lru_ra_w'])[:L]
    ri = np.asarray(inp['lru_ri_w'])[:L]
    wB = np.zeros((L, 2, 2, 128, 128), np.float32)
    for hp in range(2):
        for h in range(2):
            wB[:, hp, 0, h * 64:(h + 1) * 64, h * 64:(h + 1) * 64] = ra[:, hp * 2 + h]
            wB[:, hp, 1, h * 64:(h + 1) * 64, h * 64:(h + 1) * 64] = ri[:, hp * 2 + h]
    o['wB'] = wB
    mu = np.asarray(inp['rwkv_mu'])[:L]
    pvC = np.zeros((L, 2, 128, 11), np.float32)
    for hp in range(2):
        cs = slice(hp * 128, (hp + 1) * 128)
        pvC[:, hp, :, 0] = mu[:, 0:256][:, cs]
        pvC[:, hp, :, 1] = mu[:, 256:512][:, cs]
        pvC[:, hp, :, 2] = mu[:, 512:768][:, cs]
        pvC[:, hp, :, 3] = mu[:, 768:896]
        for i, nm in enumerate(['rwkv_w0', 'rwkv_a0', 'rwkv_k_k', 'rwkv_k_a', 'rwkv_r_k', 'rwkv_ln_w', 'rwkv_ln_b']):
            pvC[:, hp, :, 4 + i] = np.asarray(inp[nm])[:L][:, cs]
    o['pvC'] = pvC
    wC = np.zeros((L, 2, 128, 128), np.float32)
    for hp in range(2):
        cs = slice(hp * 128, (hp + 1) * 128)
        wC[:, hp, 0:32] = np.asarray(inp['rwkv_w2'])[:L][:, :, cs]
        wC[:, hp, 32:64] = np.asarray(inp['rwkv_a2'])[:L][:, :, cs]
        wC[:, hp, 64:128] = np.asarray(inp['rwkv_g2'])[:L][:, :, cs]
    o['wC'] = wC
    o['consts'] = make_consts()
    return o


def declare_params(kb, hp):
    for k, v in hp.items():
        kb.din(k, v.shape)


def build_full(S, L, shapes):
    nc = bass.Bass("TRN2", target_bir_lowering=False)
    kb = KB(nc, S, L)
    for k, shp in shapes.items():
        kb.din(k, shp)
    kb.din('cT', [128, 8])
    kb.din('xT_in', [128, 8, S])
    XT = kb.dout('xT', [128, 8, S])
    PT = kb.dscr('PT', [23, 128, S])
    YT = kb.dscr('YT', [8, 128, S], BF16)
    H2 = kb.dscr('H2', [128, 8, S], BF16)
    PTb = kb.dscr('PTb', [6, 128, S], BF16)
    modT = kb.dscr('modT', [L, 128, 48])
    load_consts(kb)
    P = kb.P
    phase_ada(kb, modT)
    for l in range(L):
        phase_T1(kb, l, XT if l > 0 else kb.d['xT_in'], PT, modT, PTb=PTb)
        phase_A(kb, l, PT, YT)
        phase_B(kb, l, PT, YT)
        phase_C(kb, l, PT, YT)
        phase_D(kb, l, PT, YT, PTb=PTb)
        phase_T2a(kb, l, XT, YT, H2, modT, XTsrc=(None if l > 0 else kb.d['xT_in']))
        phase_T2b(kb, l, XT, H2, modT)
    kb.P.emit()
    return nc


def kernel(**inputs):
    x = np.asarray(inputs['x'], dtype=np.float32)
    B, S, D = x.shape
    L = DEPTH
    hp = host_params(inputs, L)
    nc = build_full(S, L, {k: v.shape for k, v in hp.items()})
    c = np.asarray(inputs['c'], dtype=np.float32)
    in_maps = []
    ncores = 4
    for i in range(ncores):
        b = i % B
        m = dict(hp)
        m['cT'] = np.ascontiguousarray(c[b].reshape(8, 128).T)
        m['xT_in'] = np.ascontiguousarray(x[b].reshape(S, 8, 128).transpose(2, 1, 0))
        in_maps.append(m)
    res = run_bass_kernel_spmd(nc, in_maps, core_ids=list(range(ncores)))
    out = np.empty((B, S, D), np.float32)
    for b in range(B):
        xT = res.results[b]['xT']
        out[b] = xT.transpose(2, 1, 0).reshape(S, D)
    return out
```

```python
import contextlib
import numpy as np
import concourse.bass as bass
import concourse.mybir as mybir
from concourse.bass_utils import run_bass_kernel_spmd

F32 = mybir.dt.float32
BF16 = mybir.dt.bfloat16
AF = mybir.ActivationFunctionType
ALU = mybir.AluOpType
AX = mybir.AxisListType

ENGS = ['pe', 'act', 'dve', 'pool', 'sp']

D_MODEL = 1024
DEPTH = 4
P_TOTAL = 2944
D_FF = 2816
NORM_EPS = 1e-6
RWKV_LN_EPS = 64e-5


class Buf:
    __slots__ = ('w', 'r', 'name')

    def __init__(self, name=''):
        self.w = None
        self.r = {}
        self.name = name


class Prog:
    def __init__(self, nc, n_ch=24):
        self.nc = nc
        self.ops = {e: [] for e in ENGS}
        self.cnt = {e: 0 for e in ENGS}
        self.seen = {e: {} for e in ENGS}
        self.n_ch = n_ch
        self.ch_val = [0] * n_ch
        self.ch_next = 0
        self.stack = contextlib.ExitStack()
        self.nbuf = 0

    @contextlib.contextmanager
    def scope(self):
        old = self.stack
        st = contextlib.ExitStack()
        self.stack = st
        try:
            yield
        finally:
            self.barrier()
            st.close()
            self.stack = old

    def sb(self, shape, dt=F32):
        self.nbuf += 1
        return self.stack.enter_context(self.nc.sbuf_tensor('t%d' % self.nbuf, list(shape), dt))

    def ps(self, shape, dt=F32):
        self.nbuf += 1
        return self.stack.enter_context(self.nc.psum_tensor('p%d' % self.nbuf, list(shape), dt))

    def _deps(self, eng, reads, writes):
        need = {}
        for b in reads:
            if b.w is not None:
                k, v = b.w
                if need.get(k, 0) < v:
                    need[k] = v
        for b in writes:
            if b.w is not None:
                k, v = b.w
                if need.get(k, 0) < v:
                    need[k] = v
            for k, v in b.r.items():
                if need.get(k, 0) < v:
                    need[k] = v
        waits = []
        sn = self.seen[eng]
        for k, v in need.items():
            if k == eng and eng == 'pe':
                continue
            if sn.get(k, 0) < v:
                waits.append((k, v))
                sn[k] = v
        return waits

    def _update(self, tok, reads, writes):
        for b in reads:
            if b.r.get(tok[0], 0) < tok[1]:
                b.r[tok[0]] = tok[1]
        for b in writes:
            b.w = tok
            b.r = {}

    def op(self, eng, fn, reads=(), writes=()):
        waits = self._deps(eng, reads, writes)
        self.cnt[eng] += 1
        tok = (eng, self.cnt[eng])
        self.ops[eng].append((waits, fn, (eng, 1)))
        self._update(tok, reads, writes)

    def dma(self, out_ap, in_ap, reads=(), writes=(), q='sp'):
        ch = self.ch_next
        self.ch_next = (ch + 1) % self.n_ch
        key = 'ch%d' % ch
        waits = self._deps(q, reads, writes)
        pv = self.ch_val[ch]
        if pv > 0 and self.seen[q].get(key, 0) < pv:
            waits.append((key, pv))
            self.seen[q][key] = pv
        self.ch_val[ch] += 16
        tok = (key, self.ch_val[ch])

        def fn(e, out_ap=out_ap, in_ap=in_ap):
            return e.dma_start(out=out_ap, in_=in_ap)
        self.ops[q].append((waits, fn, (key, 16)))
        self._update(tok, reads, writes)

    def barrier(self):
        allv = [(e, self.cnt[e]) for e in ENGS if self.cnt[e] > 0]
        allv += [('ch%d' % c, self.ch_val[c]) for c in range(self.n_ch) if self.ch_val[c] > 0]
        for e in ENGS:
            waits = []
            for k, v in allv:
                if k == e:
                    continue
                if self.seen[e].get(k, 0) < v:
                    waits.append((k, v))
                    self.seen[e][k] = v
            if waits:
                self.ops[e].append((waits, None, None))

    def emit(self):
        nc = self.nc
        self.barrier()
        keys = list(ENGS) + ['ch%d' % c for c in range(self.n_ch)]
        sems = {}
        for k in keys:
            sems[k] = self.stack.enter_context(nc.semaphore('s_' + k))
        ops = self.ops

        def run(e, name):
            for waits, fn, inc in ops[name]:
                for k, v in waits:
                    e.wait_ge(sems[k], v)
                if fn is not None:
                    fn(e).then_inc(sems[inc[0]], inc[1])

        with nc.Block() as block:
            @block.sync
            def _(e):
                run(e, 'sp')

            @block.tensor
            def _(e):
                run(e, 'pe')

            @block.scalar
            def _(e):
                run(e, 'act')

            @block.vector
            def _(e):
                run(e, 'dve')

            @block.gpsimd
            def _(e):
                run(e, 'pool')
        self.stack.close()


class KB:
    def __init__(self, nc, S, L):
        self.nc = nc
        self.P = Prog(nc)
        self.S = S
        self.L = L
        self.d = {}

    def din(self, name, shape, dt=F32):
        self.d[name] = self.nc.dram_tensor(name, list(shape), dt, kind="ExternalInput").ap()
        return self.d[name]

    def dout(self, name, shape, dt=F32):
        self.d[name] = self.nc.dram_tensor(name, list(shape), dt, kind="ExternalOutput").ap()
        return self.d[name]

    def dscr(self, name, shape, dt=F32):
        self.d[name] = self.nc.dram_tensor(name, list(shape), dt, kind="Internal").ap()
        return self.d[name]

    def act(self, out, in_, func, r, w, bias=None, scale=None, accum=None, eng='act'):
        kw = {}
        if bias is not None:
            kw['bias'] = bias
        if scale is not None:
            kw['scale'] = scale
        if accum is not None:
            kw['accum_out'] = accum
        self.P.op('act', lambda e: e.activation(out=out, in_=in_, func=func, **kw), reads=r, writes=w)

    def tt(self, out, in0, in1, op, r, w, eng='dve'):
        self.P.op(eng, lambda e: e.tensor_tensor(out=out, in0=in0, in1=in1, op=op), reads=r, writes=w)

    def ts(self, out, in0, s1, s2, op0, op1, r, w, eng='dve'):
        if op1 is None:
            self.P.op(eng, lambda e: e.tensor_scalar(out=out, in0=in0, scalar1=s1, scalar2=None, op0=op0), reads=r, writes=w)
        else:
            self.P.op(eng, lambda e: e.tensor_scalar(out=out, in0=in0, scalar1=s1, scalar2=s2, op0=op0, op1=op1), reads=r, writes=w)

    def stt(self, out, in0, scalar, in1, op0, op1, r, w):
        self.P.op('dve', lambda e: e.scalar_tensor_tensor(out=out, in0=in0, scalar=scalar, in1=in1, op0=op0, op1=op1), reads=r, writes=w)

    def cp(self, out, in_, r, w, eng='dve'):
        self.P.op(eng, lambda e: e.tensor_copy(out=out, in_=in_), reads=r, writes=w)

    def memset(self, ap, val, w, eng='dve'):
        self.P.op(eng, lambda e: e.memset(ap, val), writes=w)

    def recip(self, out, in_, r, w):
        self.P.op('dve', lambda e: e.reciprocal(out=out, in_=in_), reads=r, writes=w)

    def mm(self, out, lhsT, rhs, r, w, start=True, stop=True):
        self.P.op('pe', lambda e: e.matmul(out, lhsT=lhsT, rhs=rhs, start=start, stop=stop), reads=r, writes=w)

    def tr(self, out, in_, ident, r, w):
        self.P.op('pe', lambda e: e.transpose(out, in_, ident), reads=r, writes=w)

    def load(self, out, in_, w, r=(), q='sp'):
        self.P.dma(out, in_, reads=r, writes=w, q=q)

    def store(self, out, in_, r, w=(), q='sp'):
        self.P.dma(out, in_, reads=r, writes=w, q=q)


def load_consts(kb):
    P = kb.P
    c = {}
    cd = kb.d['consts']
    ct = P.sb([128, cd.shape[1]])
    b = Buf('consts')
    kb.load(ct[:], cd[:, :], [b])
    c['buf'] = b
    c['ident'] = ct[:, 0:128]
    c['bd'] = ct[:, 128:256]
    c['ones_col'] = ct[:, 256:257]
    c['m320'] = ct[0:64, 320:640]
    c['mdiag'] = ct[:, 640:768]
    c['ident64'] = ct[0:64, 0:64]
    c['mdiag_inv'] = ct[:, 768:896]
    cb = P.sb([128, 768], BF16)
    bb = Buf('constsb')
    kb.cp(cb[:], ct[:, 0:768], [b], [bb])
    c['bufb'] = bb
    c['ident_b'] = cb[:, 0:128]
    c['bd_b'] = cb[:, 128:256]
    c['mdiag_b'] = cb[:, 640:768]
    onesb = P.sb([128, 128], BF16)
    kb.memset(onesb[:], 1.0, [bb])
    c['ones_b'] = onesb
    kb.c = c


def make_consts():
    c = np.zeros((128, 896), np.float32)
    c[:, 0:128] = np.eye(128, dtype=np.float32)
    c[0:64, 128:192] = 1.0
    c[64:128, 192:256] = 1.0
    c[:, 256] = 1.0
    i = np.arange(64)[:, None]
    t = np.arange(64)[None, :]
    strict = (i < t).astype(np.float32)
    incl = (i <= t).astype(np.float32)
    c[0:64, 320:384] = strict
    c[0:64, 384:448] = incl
    c[0:64, 448:512] = strict
    c[0:64, 512:576] = incl
    c[0:64, 576:640] = (t < i).astype(np.float32)
    q = np.arange(128)[:, None]
    k = np.arange(128)[None, :]
    c[:, 640:768] = (k < q).astype(np.float32)
    c[:, 768:896] = (k >= q).astype(np.float32)
    return c


def phase_ada(kb, modT_dram):
    P = kb.P
    L = kb.L
    with P.scope():
        cT = P.sb([128, 8])
        bc = Buf()
        kb.load(cT[:], kb.d['cT'][:, :], [bc])
        sc2 = P.sb([128, 8, 2])
        bs = Buf()
        kb.act(sc2[:, :, 0], cT[:], AF.Silu, [bc], [bs])
        kb.act(sc2[:, :, 1], cT[:], AF.Silu, [bc], [bs])
        abT = P.sb([128, L, 48])
        bab = Buf()
        kb.load(abT[:], kb.d['ada_bT'].rearrange("l p j -> p l j"), [bab])
        wst = [P.sb([128, 8, 1024]) for _ in range(2)]
        bw = [Buf() for _ in range(2)]
        pm = P.ps([128, 128])
        bpm = Buf()
        mo = P.sb([128, L, 48])
        bmo = Buf()
        it = 0
        for l in range(L):
            for m in range(6):
                w = wst[it % 2]
                b = bw[it % 2]
                it += 1
                kb.load(w[:], kb.d['ada_w'][l, :, m * 1024:(m + 1) * 1024].rearrange("(k p) n -> p k n", p=128), [b])
                for j in range(8):
                    col = (m * 8 + j) * 2
                    for k in range(8):
                        kb.mm(pm[:, col:col + 2], w[:, k, j * 128:(j + 1) * 128], sc2[:, k, :], [b, bs], [bpm],
                              start=(k == 0), stop=(k == 7))
            pv = pm[:, 0:96].rearrange("p (j t) -> p j t", t=2)[:, :, 0]
            kb.tt(mo[:, l, :], pv, abT[:, l, :], ALU.add, [bpm, bab], [bmo])
        kb.store(modT_dram.rearrange("l p j -> p l j"), mo[:], [bmo])


def emit_norm(kb, xt, bx, n, gs, sh, bmod, sq, bsq, pss, bpss, rstd, brstd, tmp, btmp, h, bh):
    c = kb.c
    kb.act(sq[:, :, 0:n], xt[:, :, 0:n], AF.Square, [bx], [bsq])
    for k in range(8):
        kb.mm(pss[:, 0:n], c['ones_b'][:], sq[:, k, 0:n], [bsq, c['bufb']], [bpss], start=(k == 0), stop=(k == 7))
    kb.ts(rstd[:, 0:n], pss[:, 0:n], 1.0 / D_MODEL, NORM_EPS, ALU.mult, ALU.add, [bpss], [brstd])
    kb.act(rstd[:, 0:n], rstd[:, 0:n], AF.Ln, [brstd], [brstd])
    kb.act(rstd[:, 0:n], rstd[:, 0:n], AF.Exp, [brstd], [brstd], scale=-0.5)
    for k in range(8):
        kb.stt(tmp[:, k, 0:n], xt[:, k, 0:n], gs[:, k:k + 1], rstd[:, 0:n], ALU.mult, ALU.mult, [bx, bmod, brstd], [btmp[k % 2]])
        kb.act(h[:, k, 0:n], tmp[:, k, 0:n], AF.Identity, [btmp[k % 2], bmod], [bh], bias=sh[:, k:k + 1])


def load_cast_weight(kb, dst, bdst, src_rows, ncols, nk, stages, bst, it0=0):
    engs = ['act', 'dve']
    it = it0
    for k in range(nk):
        src = src_rows(k)
        PW = stages[0].shape[1]
        for c0 in range(0, ncols, PW):
            c1 = min(ncols, c0 + PW)
            s = stages[it % len(stages)]
            b = bst[it % len(stages)]
            kb.load(s[:, 0:c1 - c0], src[:, c0:c1], [b])
            e = engs[it % 2]
            it += 1
            if e == 'act':
                kb.act(dst[:, k, c0:c1], s[:, 0:c1 - c0], AF.Copy, [b], [bdst])
            else:
                kb.cp(dst[:, k, c0:c1], s[:, 0:c1 - c0], [b], [bdst], eng=e)


def phase_T1(kb, l, XT, PT, modT, PTb=None):
    P = kb.P
    S = kb.S
    TN = 512
    with P.scope():
        Wb = P.sb([128, 8, P_TOTAL], BF16)
        bW = Buf()
        stages = [P.sb([128, 2048]) for _ in range(3)]
        bst = [Buf() for _ in range(3)]
        load_cast_weight(kb, Wb, bW, lambda k: kb.d['w_in'][l, k * 128:(k + 1) * 128, :], P_TOTAL, 8, stages, bst)
        modt = P.sb([128, 48])
        g1 = P.sb([128, 8])
        gs = P.sb([128, 8])
        bmod = Buf()
        kb.load(modt[:], modT[l], [bmod])
        kb.load(g1[:], kb.d['norm1_gT'][l], [bmod])
        kb.stt(gs[:], modt[:, 8:16], 1.0, g1[:], ALU.add, ALU.mult, [bmod], [bmod])
        sh = modt[:, 0:8]
        xts = [P.sb([128, 8, TN]) for _ in range(2)]
        bxs = [Buf() for _ in range(2)]
        sq = P.sb([128, 8, TN], BF16)
        bsq = Buf()
        pss = P.ps([128, TN])
        bpss = Buf()
        rstd = P.sb([128, TN])
        brstd = Buf()
        tmp = P.sb([128, 8, TN])
        btmp = [Buf(), Buf()]
        hs = [P.sb([128, 8, TN], BF16) for _ in range(2)]
        bhs = [Buf() for _ in range(2)]
        pps = [P.ps([128, TN]) for _ in range(4)]
        bpp = [Buf() for _ in range(4)]
        outs = [P.sb([128, TN]) for _ in range(4)]
        bo = [Buf() for _ in range(4)]
        outsb = [P.sb([128, TN], BF16) for _ in range(4)]
        io = 0
        for ti in range(S // TN):
            xt = xts[ti % 2]
            bx = bxs[ti % 2]
            h = hs[ti % 2]
            bh = bhs[ti % 2]
            kb.load(xt[:], XT[:, :, ti * TN:(ti + 1) * TN], [bx])
            emit_norm(kb, xt, bx, TN, gs, sh, bmod, sq, bsq, pss, bpss, rstd, brstd, tmp, btmp, h, bh)
            for c in range(23):
                pp = pps[io % 4]
                bp = bpp[io % 4]
                o = outs[io % 4]
                bob = bo[io % 4]
                for k in range(8):
                    kb.mm(pp[:], Wb[:, k, c * 128:(c + 1) * 128], h[:, k, :], [bW, bh], [bp], start=(k == 0), stop=(k == 7))
                if PTb is not None and c >= 17:
                    o = outsb[io % 4]
                    dst = PTb[c - 17, :, ti * TN:(ti + 1) * TN]
                else:
                    dst = PT[c, :, ti * TN:(ti + 1) * TN]
                if io % 2 == 0:
                    kb.cp(o[:], pp[:], [bp], [bob])
                else:
                    kb.act(o[:], pp[:], AF.Copy, [bp], [bob])
                kb.store(dst, o[:], [bob], q='pool')
                io += 1


def phase_A(kb, l, PT, YT):
    P = kb.P
    S = kb.S
    c = kb.c
    NCH = S // 64
    with P.scope():
        pv = P.sb([128, 4])
        bpv = Buf()
        kb.load(pv[:, 0:2], kb.d['pvA'][l], [bpv])
        kb.ts(pv[:, 2:3], pv[:, 0:1], 0.125, None, ALU.mult, None, [bpv], [bpv])
        biasT = P.sb([64, 4, 9, 64])
        bbias = Buf()
        kb.load(biasT[:], kb.d['biasA'][l].rearrange("h k j q -> k h j q"), [bbias])
        qf = P.sb([128, S])
        kf = P.sb([128, S])
        vf = P.sb([128, S])
        bq, bk, bv = Buf(), Buf(), Buf()
        qn = P.sb([128, S], BF16)
        kn = P.sb([128, S], BF16)
        bqn, bkn = Buf(), Buf()
        vtm = P.sb([64, NCH, 128], BF16)
        bvtm = Buf()
        ya = P.sb([128, S], BF16)
        bya = Buf()
        sq = P.sb([128, 512], BF16)
        bsq = Buf()
        rs = P.sb([128, 512])
        brs = Buf()
        pA = [P.ps([128, 512]) for _ in range(2)]
        bpA = [Buf(), Buf()]
        pS = [P.ps([64, 16, 64]) for _ in range(2)]
        bpS = [Buf(), Buf()]
        pO = [P.ps([128, 128]) for _ in range(2)]
        bpO = [Buf(), Buf()]
        ssb = [P.sb([64, 9, 64]) for _ in range(2)]
        bss = [Buf(), Buf()]
        eb = [P.sb([64, 9, 64], BF16) for _ in range(4)]
        beb = [Buf() for _ in range(4)]
        rd = [P.sb([128, 64]) for _ in range(2)]
        brd = [Buf(), Buf()]
        for hp in range(2):
            kb.load(qf[:], PT[0 + hp], [bq])
            kb.load(kf[:], PT[2 + hp], [bk])
            kb.load(vf[:], PT[4 + hp], [bv])
            it = 0
            for (src, bs_, dst, bd_, gcol) in ((qf, bq, qn, bqn, 2), (kf, bk, kn, bkn, 1)):
                for ti in range(S // 512):
                    sl = slice(ti * 512, (ti + 1) * 512)
                    pa = pA[it % 2]
                    bpa = bpA[it % 2]
                    it += 1
                    kb.act(sq[:], src[:, sl], AF.Square, [bs_], [bsq])
                    kb.mm(pa[:], c['bd_b'], sq[:], [bsq, c['bufb']], [bpa])
                    kb.ts(rs[:], pa[:], 1.0 / 64, NORM_EPS, ALU.mult, ALU.add, [bpa], [brs])
                    kb.act(rs[:], rs[:], AF.Ln, [brs], [brs])
                    kb.act(rs[:], rs[:], AF.Exp, [brs], [brs], scale=-0.5)
                    kb.stt(dst[:, sl], src[:, sl], pv[:, gcol:gcol + 1], rs[:], ALU.mult, ALU.mult, [bs_, bpv, brs], [bd_])
            for g in range(NCH // 4):
                pa = pA[it % 2]
                bpa = bpA[it % 2]
                it += 1
                for j in range(4):
                    n = g * 4 + j
                    kb.tr(pa[0:64, j * 128:(j + 1) * 128], vf[:, n * 64:(n + 1) * 64], c['ident'], [bv, c['buf']], [bpa])
                kb.cp(vtm[:, g * 4:(g + 1) * 4, :], pa[0:64, :].rearrange("p (j c) -> p j c", c=128), [bpa], [bvtm])
            unitsA = [(h, n) for h in range(2) for n in range(NCH)]
            NU = len(unitsA)

            def A1(i):
                h, n = unitsA[i]
                hb = 64 * h
                hg = hp * 2 + h
                j0 = 9 - (min(n, 8) + 1)
                ps_, bps = pS[i % 2], bpS[i % 2]
                s_, bs_ = ssb[i % 2], bss[i % 2]
                e_, be = eb[i % 4], beb[i % 4]
                for jj in range(j0, 9):
                    j = n - (8 - jj)
                    kb.mm(ps_[:, jj, :], kn[hb:hb + 64, j * 64:(j + 1) * 64], qn[hb:hb + 64, n * 64:(n + 1) * 64],
                          [bkn, bqn], [bps])
                kb.tt(s_[:, j0:9, :], ps_[:, j0:9, :], biasT[:, hg, j0:9, :], ALU.add, [bps, bbias], [bs_])
                kb.act(e_[:, j0:9, :], s_[:, j0:9, :], AF.Exp, [bs_], [be])

            def A2(i):
                h, n = unitsA[i]
                hb = 64 * h
                j0 = 9 - (min(n, 8) + 1)
                e_, be = eb[i % 4], beb[i % 4]
                po, bpo = pO[i % 2], bpO[i % 2]
                for jj in range(j0, 9):
                    j = n - (8 - jj)
                    kb.mm(po[hb:hb + 64, 0:64], vtm[:, j, hb:hb + 64], e_[:, jj, :], [bvtm, be], [bpo],
                          start=(jj == j0), stop=(jj == 8))
                for jj in range(j0, 9):
                    kb.mm(po[hb:hb + 64, 64:128], c['ones_b'][0:64, 0:64], e_[:, jj, :], [be, c['bufb']], [bpo],
                          start=(jj == j0), stop=(jj == 8))

            def A3(i):
                h, n = unitsA[i]
                hb = 64 * h
                po, bpo = pO[i % 2], bpO[i % 2]
                r_, br = rd[i % 2], brd[i % 2]
                kb.recip(r_[hb:hb + 64, :], po[hb:hb + 64, 64:128], [bpo], [br])
                kb.tt(ya[hb:hb + 64, n * 64:(n + 1) * 64], po[hb:hb + 64, 0:64], r_[hb:hb + 64, :], ALU.mult, [bpo, br], [bya])

            for t in range(NU + 3):
                if t < NU:
                    A1(t)
                if 0 <= t - 2 < NU:
                    A2(t - 2)
                if 0 <= t - 3 < NU:
                    A3(t - 3)
            kb.store(YT[0 + hp], ya[:], [bya], q='pool')


def phase_B(kb, l, PT, YT):
    P = kb.P
    S = kb.S
    c = kb.c
    with P.scope():
        pv = P.sb([128, 2, 12])
        bpv = Buf()
        kb.load(pv[:, :, 0:8], kb.d['pvB'][l].rearrange("h p n -> p h n"), [bpv])
        wm = P.sb([128, 2, 2, 128])
        bwm = Buf()
        kb.load(wm[:], kb.d['wB'][l].rearrange("h g p n -> p h g n"), [bwm])
        wmb = P.sb([128, 2, 2, 128], BF16)
        kb.cp(wmb[:], wm[:], [bwm], [bwm])
        for hp in range(2):
            kb.act(pv[:, hp, 8:9], pv[:, hp, 7:8], AF.Exp, [bpv], [bpv], scale=-1.0)
            kb.ts(pv[:, hp, 8:9], pv[:, hp, 8:9], 1.0, None, ALU.add, None, [bpv], [bpv])
            kb.act(pv[:, hp, 8:9], pv[:, hp, 8:9], AF.Ln, [bpv], [bpv])
            kb.ts(pv[:, hp, 9:10], pv[:, hp, 8:9], -8.0, None, ALU.mult, None, [bpv], [bpv])
        xb = P.sb([128, S])
        bxb = Buf()
        xc = P.sb([128, S])
        bxc = Buf()
        uu = P.sb([128, S])
        buu = Buf()
        xcb = P.sb([128, 512], BF16)
        bxcb = Buf()
        pr = [P.ps([128, 512]) for _ in range(2)]
        bpr = [Buf(), Buf()]
        pi = [P.ps([128, 512]) for _ in range(2)]
        bpi = [Buf(), Buf()]
        rg = P.sb([128, 512])
        brg = Buf()
        ig = P.sb([128, 512])
        big = Buf()
        t1 = P.sb([128, 512])
        bt1 = Buf()
        yb = P.sb([128, S], BF16)
        byb = Buf()
        for hp in range(2):
            kb.load(xb[:], PT[6 + hp], [bxb])
            kb.act(xc[:], xb[:], AF.Identity, [bxb, bpv], [bxc], bias=pv[:, hp, 4:5], scale=pv[:, hp, 3:4])
            for i in range(3):
                sft = 3 - i
                kb.stt(xc[:, sft:S], xb[:, 0:S - sft], pv[:, hp, i:i + 1], xc[:, sft:S], ALU.mult, ALU.add, [bxb, bpv, bxc], [bxc])
            for ti in range(S // 512):
                sl = slice(ti * 512, (ti + 1) * 512)
                kb.cp(xcb[:], xc[:, sl], [bxc], [bxcb])
                kb.mm(pr[ti % 2][:], wmb[:, hp, 0, :], xcb[:], [bwm, bxcb], [bpr[ti % 2]])
                kb.mm(pi[ti % 2][:], wmb[:, hp, 1, :], xcb[:], [bwm, bxcb], [bpi[ti % 2]])
                kb.act(rg[:], pr[ti % 2][:], AF.Sigmoid, [bpr[ti % 2], bpv], [brg], bias=pv[:, hp, 5:6])
                kb.act(ig[:], pi[ti % 2][:], AF.Sigmoid, [bpi[ti % 2], bpv], [big], bias=pv[:, hp, 6:7])
                kb.act(xb[:, sl], rg[:], AF.Exp, [brg, bpv, bxc], [bxb], scale=pv[:, hp, 9:10])
                kb.tt(t1[:], xb[:, sl], xb[:, sl], ALU.mult, [bxb], [bt1])
                kb.ts(t1[:], t1[:], -1.0, 1.0, ALU.mult, ALU.add, [bt1], [bt1])
                kb.act(t1[:], t1[:], AF.Sqrt, [bt1], [bt1])
                kb.tt(ig[:], ig[:], xc[:, sl], ALU.mult, [big, bxc], [big])
                kb.tt(uu[:, sl], t1[:], ig[:], ALU.mult, [bt1, big], [buu])
            kb.P.op('dve', lambda e: e.tensor_tensor_scan(out=xc[:], data0=xb[:], data1=uu[:], initial=0.0,
                                                          op0=ALU.mult, op1=ALU.add), reads=[bxb, buu], writes=[bxc])
            kb.load(uu[:], PT[8 + hp], [buu])
            kb.act(uu[:], uu[:], AF.Gelu, [buu], [buu])
            kb.tt(yb[:], xc[:], uu[:], ALU.mult, [bxc, buu], [byb])
            kb.store(YT[2 + hp], yb[:], [byb], q='pool')


def phase_D(kb, l, PT, YT, PTb=None):
    P = kb.P
    S = kb.S
    c = kb.c
    NQ = S // 128
    WMAX = 1024
    NB = 3
    PC = min(2048, S)
    with P.scope():
        stgb = P.sb([128, PC], BF16)
        bstg = Buf()
        qb = P.sb([128, S], BF16)
        kbf = P.sb([128, S], BF16)
        bqb, bkb = Buf(), Buf()
        vtm = P.sb([128, NQ, 128], BF16)
        bvtm = Buf()
        yd = P.sb([128, S], BF16)
        byd = Buf()
        pz = [P.ps([128, WMAX]) for _ in range(2)]
        bpz = [Buf(), Buf()]
        pt = [P.ps([128, WMAX], BF16) for _ in range(2)]
        bpt = [Buf(), Buf()]
        po = [P.ps([128, 512]) for _ in range(2)]
        bpo = [Buf(), Buf()]
        kp = [P.sb([128, WMAX]) for _ in range(NB)]
        bkp = [Buf() for _ in range(NB)]
        Pb = [P.sb([128, WMAX + 1]) for _ in range(NB)]
        bPb = [Buf() for _ in range(NB)]
        wt = [P.sb([128, WMAX], BF16) for _ in range(NB)]
        bwt = [Buf() for _ in range(NB)]
        wT = [P.sb([128, WMAX], BF16) for _ in range(3)]
        bwT = [Buf() for _ in range(3)]
        for hp in range(2):
            kb.load(qb[:], PTb[0 + hp], [bqb])
            kb.load(kbf[:], PTb[2 + hp], [bkb])
            it = 0
            for t0 in range(0, S, PC):
                kb.load(stgb[:], PTb[4 + hp][:, t0:t0 + PC], [bstg])
                for g in range(PC // 1024):
                    pa = pt[it % 2]
                    bpa = bpt[it % 2]
                    it += 1
                    for j in range(8):
                        kb.tr(pa[:, j * 128:(j + 1) * 128], stgb[:, (g * 8 + j) * 128:(g * 8 + j + 1) * 128], c['ident_b'], [bstg, c['bufb']], [bpa])
                    n0 = t0 // 128 + g * 8
                    kb.cp(vtm[:, n0:n0 + 8, :], pa[:, 0:1024].rearrange("p (j c) -> p j c", c=128), [bpa], [bvtm])
            units = []
            io = 0
            for h in range(2):
                for qi in range(NQ):
                    kend = (qi + 1) * 128
                    nsb = (kend + WMAX - 1) // WMAX
                    for s in range(nsb - 1, -1, -1):
                        k0 = s * WMAX
                        k1 = min(kend, k0 + WMAX)
                        units.append(dict(h=h, qi=qi, k0=k0, W=k1 - k0, diag=(s == nsb - 1), first=(s == nsb - 1), last=(s == 0), io=io))
                    io += 1
            NU = len(units)

            def S1(i):
                u = units[i]
                hb = 64 * u['h']
                W = u['W']
                z, bz = pz[i % 2], bpz[i % 2]
                k_, bk_ = kp[i % NB], bkp[i % NB]
                for b0 in range(0, W, 512):
                    b1 = min(W, b0 + 512)
                    kb.mm(z[:, b0:b1], qb[hb:hb + 64, u['qi'] * 128:(u['qi'] + 1) * 128], kbf[hb:hb + 64, u['k0'] + b0:u['k0'] + b1],
                          [bqb, bkb], [bz])
                kb.act(k_[:, 0:W], z[:, 0:W], AF.Sigmoid, [bz], [bk_], scale=-0.125)
                if u['diag']:
                    kb.tt(k_[:, W - 128:W], k_[:, W - 128:W], c['mdiag_inv'], ALU.max, [bk_, c['buf']], [bk_])

            def S2(i):
                u = units[i]
                W = u['W']
                k_, bk_ = kp[i % NB], bkp[i % NB]
                p_, bp_ = Pb[i % NB], bPb[i % NB]
                if u['first']:
                    init = 1.0
                    rds = [bk_]
                    kb.memset(p_[:, W:W + 1], 1.0, [bp_], eng='pool')
                else:
                    pp_, bpp_ = Pb[(i - 1) % NB], bPb[(i - 1) % NB]
                    init = pp_[:, 0:1]
                    rds = [bk_, bpp_]
                    kb.act(p_[:, W:W + 1], pp_[:, 0:1], AF.Copy, [bpp_], [bp_])
                kb.P.op('dve', lambda e, p_=p_, k_=k_, W=W, init=init: e.tensor_tensor_scan(
                    out=p_[:, 0:W][:, ::-1], data0=k_[:, 0:W][:, ::-1], data1=c['ones_col'].broadcast_to([128, W]),
                    initial=init, op0=ALU.mult, op1=ALU.mult), reads=rds + [c['buf']], writes=[bp_])

            def S2b(i):
                u = units[i]
                W = u['W']
                p_, bp_ = Pb[i % NB], bPb[i % NB]
                w_, bw_ = wt[i % NB], bwt[i % NB]
                kb.tt(w_[:, 0:W], p_[:, 1:W + 1], p_[:, 0:W], ALU.subtract, [bp_], [bw_])

            def S3a(i):
                u = units[i]
                W = u['W']
                w_, bw_ = wt[i % NB], bwt[i % NB]
                T_, bT_ = pt[i % 2], bpt[i % 2]
                wT_, bwT_ = wT[i % 3], bwT[i % 3]
                nblk = W // 128
                for bi in range(nblk):
                    kb.tr(T_[:, bi * 128:(bi + 1) * 128], w_[:, bi * 128:(bi + 1) * 128], c['ident_b'], [bw_, c['bufb']], [bT_])
                kb.act(wT_[:, 0:W], T_[:, 0:W], AF.Copy, [bT_], [bwT_])

            def S3b(i):
                u = units[i]
                hb = 64 * u['h']
                W = u['W']
                wT_, bwT_ = wT[i % 3], bwT[i % 3]
                pot, bpot = po[u['io'] % 2], bpo[u['io'] % 2]
                nblk = W // 128
                for bi in range(nblk):
                    kb.mm(pot[hb:hb + 64, 0:128], vtm[:, u['k0'] // 128 + bi, hb:hb + 64], wT_[:, bi * 128:(bi + 1) * 128],
                          [bvtm, bwT_], [bpot], start=(u['first'] and bi == 0), stop=(u['last'] and bi == nblk - 1))
                if u['last']:
                    kb.act(yd[hb:hb + 64, u['qi'] * 128:(u['qi'] + 1) * 128], pot[hb:hb + 64, 0:128], AF.Copy, [bpot], [byd])

            for t in range(NU + 4):
                if 0 <= t - 1 < NU:
                    S2(t - 1)
                if t < NU:
                    S1(t)
                if 0 <= t - 2 < NU:
                    S2b(t - 2)
                if 0 <= t - 3 < NU:
                    S3a(t - 3)
                if 0 <= t - 4 < NU:
                    S3b(t - 4)
            kb.store(YT[6 + hp], yd[:], [byd], q='pool')


def phase_C(kb, l, PT, YT):
    P = kb.P
    S = kb.S
    c = kb.c
    SEG = min(512, S)
    NC_ = SEG // 64
    NCH = NC_ * 2
    with P.scope():
        pv = P.sb([128, 2, 16])
        bpv = Buf()
        kb.load(pv[:, :, 0:11], kb.d['pvC'][l].rearrange("h p n -> p h n"), [bpv])
        for hp in range(2):
            kb.ts(pv[:, hp, 11:15], pv[:, hp, 0:4], -1.0, 1.0, ALU.mult, ALU.add, [bpv], [bpv])
            kb.ts(pv[:, hp, 15:16], pv[:, hp, 7:8], -1.0, 1.0, ALU.mult, ALU.add, [bpv], [bpv])
        wag = P.sb([128, 2, 128])
        bwag = Buf()
        kb.load(wag[:], kb.d['wC'][l].rearrange("h p n -> p h n"), [bwag])
        rmask = P.sb([128, SEG])
        brm = Buf()
        kb.memset(rmask[:], 1.0, [brm])
        kb.memset(rmask[:].rearrange("p (n c) -> p n c", c=64)[:, :, 0:1], 0.0, [brm])

        def T():
            return P.sb([128, SEG]), Buf()
        raw = [(P.sb([128, SEG + 1]), Buf()) for _ in range(4)]
        sh = [T() for _ in range(4)]
        (ld, bld), (al, bal), (gf, bgf), (kk, bkk), (km, bkm), (bon, bbon) = T(), T(), T(), T(), T(), T()
        (Lc, bLc), (t1, bt1), (t2, bt2), (bv_, bbv) = T(), T(), T(), T()
        AR = P.sb([128, NC_, 128])
        bAR = Buf()
        (btl, bbt), (ktl, bkt), (bh, bbh), (kh, bkh) = T(), T(), T(), T()
        BHt = P.sb([64, NC_, 128], BF16)
        KHt = P.sb([64, NC_, 128], BF16)
        Vt = P.sb([64, NC_, 128], BF16)
        ARb = P.sb([128, NC_, 128], BF16)
        bARb = Buf()
        AMb = P.sb([64, NCH, 192], BF16)
        bAMb = [Buf() for _ in range(NCH // 8)]
        Mtb = P.sb([64, NCH, 64], BF16)
        bMtb = [Buf() for _ in range(NCH // 8)]
        Tb = [P.sb([64, 2, 64], BF16) for _ in range(2)]
        bTb = [Buf(), Buf()]
        bBHt, bKHt, bVt = Buf(), Buf(), Buf()
        AM = P.sb([64, NCH, 320])
        bAM = [Buf() for _ in range(NCH)]
        Bm = [P.sb([64, NCH, 64]) for _ in range(2)]
        Btm = [P.sb([64, NCH, 64]) for _ in range(2)]
        Mt = [P.sb([64, NCH, 64]) for _ in range(2)]
        bBm = [[Buf() for _ in range(NCH // 8)] for _ in range(2)]
        bBtm = [[Buf() for _ in range(NCH // 8)] for _ in range(2)]
        bMt = [[Buf() for _ in range(NCH // 8)] for _ in range(2)]
        TS = [P.sb([64, 2, 64]) for _ in range(2)]
        AR1 = P.sb([64, NC_, 128], BF16)
        bAR1 = Buf()
        GCc = P.sb([128, NC_])
        bGCc = Buf()
        G1 = P.sb([64, NC_])
        bG1 = Buf()
        bTS = [Buf(), Buf()]
        Wsb = P.sb([64, 128], BF16)
        bWsb = Buf()
        Usb = P.sb([64, 128], BF16)
        bUsb = Buf()
        ysb = P.sb([64, 8, 64])
        bysb = Buf()
        ysq = P.sb([64, 8, 64])
        bysq = Buf()
        st1 = P.sb([64, 8])
        st2 = P.sb([64, 8])
        bst1, bst2 = Buf(), Buf()
        yc = P.sb([128, SEG], BF16)
        byc = Buf()
        yfm = P.sb([128, 256])
        byfm = Buf()
        pg = [P.ps([128, 512]) for _ in range(2)]
        bpg = [Buf(), Buf()]
        pinv = [P.ps([64, 512]) for _ in range(3)]
        bpinv = [Buf() for _ in range(3)]
        pW = P.ps([64, 128])
        bpW = Buf()
        pY = P.ps([64, 512])
        bpY = Buf()
        pT = P.ps([64, 128])
        bpT = Buf()
        ig = [0]

        def gbank():
            i = ig[0] % 2
            ig[0] += 1
            return pg[i], bpg[i]

        for hp in range(2):
            chans = [10 + hp, 12 + hp, 14 + hp, 16]
            kb.memset(TS[0][:], 0.0, [bTS[0]])
            kb.memset(Tb[0][:], 0.0, [bTb[0]])
            tsi = 0
            for sg in range(S // SEG):
                t0 = sg * SEG
                for i in range(4):
                    rt_, br_ = raw[i]
                    if t0 == 0:
                        kb.memset(rt_[:, 0:1], 0.0, [br_])
                        kb.load(rt_[:, 1:SEG + 1], PT[chans[i]][:, 0:SEG], [br_])
                    else:
                        kb.load(rt_[:, :], PT[chans[i]][:, t0 - 1:t0 + SEG], [br_])
                    s_, bs_ = sh[i]
                    mcol = 3 if i == 3 else i
                    kb.ts(s_[:], rt_[:, 1:SEG + 1], pv[:, hp, 11 + mcol:12 + mcol], None, ALU.mult, None, [br_, bpv], [bs_])
                    kb.stt(s_[:], rt_[:, 0:SEG], pv[:, hp, mcol:mcol + 1], s_[:], ALU.mult, ALU.add, [br_, bpv, bs_], [bs_])
                (rs, brs), (ks, bks), (vs, bvs), (xs, bxs) = sh
                kb.act(t1[0:32, :], xs[0:32, :], AF.Tanh, [bxs], [bt1])
                kb.act(t2[64:128, :], xs[64:128, :], AF.Sigmoid, [bxs], [bt2])
                for ti in range(SEG // 512):
                    sl = slice(ti * 512, (ti + 1) * 512)
                    p_, bp_ = gbank()
                    kb.mm(p_[:], wag[0:32, hp, :], t1[0:32, sl], [bwag, bt1], [bp_])
                    kb.act(ld[:, sl], p_[:], AF.Sigmoid, [bp_, bpv], [bld], bias=pv[:, hp, 4:5])
                    p_, bp_ = gbank()
                    kb.mm(p_[:], wag[32:64, hp, :], xs[32:64, sl], [bwag, bxs], [bp_])
                    kb.act(al[:, sl], p_[:], AF.Sigmoid, [bp_, bpv], [bal], bias=pv[:, hp, 5:6])
                    p_, bp_ = gbank()
                    kb.mm(p_[:], wag[64:128, hp, :], t2[64:128, sl], [bwag, bt2], [bp_])
                    kb.cp(gf[:, sl], p_[:], [bp_], [bgf])
                kb.ts(ld[:], ld[:], -0.6065306597126334, None, ALU.mult, None, [bld], [bld])
                kb.ts(kk[:], ks[:], pv[:, hp, 6:7], None, ALU.mult, None, [bks, bpv], [bkk])
                kb.tt(t1[:], kk[:], kk[:], ALU.mult, [bkk], [bt1])
                for ti in range(SEG // 512):
                    sl = slice(ti * 512, (ti + 1) * 512)
                    p_, bp_ = gbank()
                    kb.mm(p_[:], c['bd'], t1[:, sl], [c['buf'], bt1], [bp_])
                    kb.act(t2[:, sl], p_[:], AF.Sqrt, [bp_], [bt2])
                kb.ts(t2[:], t2[:], 1e-12, None, ALU.max, None, [bt2], [bt2])
                kb.recip(t2[:], t2[:], [bt2], [bt2])
                kb.tt(kk[:], kk[:], t2[:], ALU.mult, [bkk, bt2], [bkk])
                kb.ts(km[:], al[:], pv[:, hp, 7:8], pv[:, hp, 15:16], ALU.mult, ALU.add, [bal, bpv], [bkm])
                kb.tt(km[:], km[:], ks[:], ALU.mult, [bkm, bks], [bkm])
                kb.stt(t1[:], rs[:], pv[:, hp, 8:9], km[:], ALU.mult, ALU.mult, [brs, bpv, bkm], [bt1])
                for ti in range(SEG // 512):
                    sl = slice(ti * 512, (ti + 1) * 512)
                    p_, bp_ = gbank()
                    kb.mm(p_[:], c['bd'], t1[:, sl], [c['buf'], bt1], [bp_])
                    kb.tt(bon[:, sl], p_[:], vs[:, sl], ALU.mult, [bp_, bvs], [bbon])
                kb.P.op('dve', lambda e: e.tensor_tensor_scan(out=Lc[:], data0=rmask[:], data1=ld[:], initial=0.0,
                                                              op0=ALU.mult, op1=ALU.add), reads=[brm, bld], writes=[bLc])
                AR4 = AR[:].rearrange("p n (two c) -> p n two c", two=2)
                v3 = lambda t: t[:].rearrange("p (n c) -> p n c", c=64)
                kb.act(t1[:], Lc[:], AF.Exp, [bLc], [bt1])
                kb.tt(AR4[:, :, 1, :], v3(rs), v3(t1), ALU.mult, [brs, bt1], [bAR])
                kb.tt(t2[:], Lc[:], ld[:], ALU.subtract, [bLc, bld], [bt2])
                kb.act(t2[:], t2[:], AF.Exp, [bt2], [bt2])
                kb.stt(AR4[:, :, 0, :], v3(kk), -1.0, v3(t2), ALU.mult, ALU.mult, [bkk, bt2], [bAR])
                kb.tt(bv_[:], kk[:], al[:], ALU.mult, [bkk, bal], [bbv])
                kb.act(t2[:], Lc[:], AF.Exp, [bLc], [bt2], scale=-1.0)
                kb.tt(btl[:], bv_[:], t2[:], ALU.mult, [bbv, bt2], [bbt])
                kb.tt(ktl[:], km[:], t2[:], ALU.mult, [bkm, bt2], [bkt])
                LCb = v3(Lc)[:, :, 63:64].broadcast_to([128, NC_, 64])
                kb.tt(v3(t2), LCb, v3(Lc), ALU.subtract, [bLc], [bt2])
                kb.act(t2[:], t2[:], AF.Exp, [bt2], [bt2])
                kb.tt(bh[:], bv_[:], t2[:], ALU.mult, [bbv, bt2], [bbh])
                kb.tt(kh[:], km[:], t2[:], ALU.mult, [bkm, bt2], [bkh])
                kb.cp(GCc[:], v3(t1)[:, :, 63], [bt1], [bGCc])
                kb.load(G1[:], GCc[64:128, :], [bG1], r=[bGCc])
                kb.cp(ARb[:], AR[:], [bAR], [bARb])
                kb.load(AR1[:], ARb[64:128, :, :], [bAR1], r=[bARb])
                ARh = [ARb, AR1]
                bARh = [bARb, bAR1]
                GCh = [GCc, G1]
                bGCh = [bGCc, bG1]
                for (src, bsrc, dst, bdst) in ((bh, bbh, BHt, bBHt), (kh, bkh, KHt, bKHt), (vs, bvs, Vt, bVt)):
                    for g in range(NC_ // 4):
                        p_, bp_ = gbank()
                        for j in range(4):
                            n = g * 4 + j
                            kb.tr(p_[0:64, j * 128:(j + 1) * 128], src[:, n * 64:(n + 1) * 64], c['ident'], [bsrc, c['buf']], [bp_])
                        kb.cp(dst[:, g * 4:(g + 1) * 4, :], p_[0:64, :].rearrange("p (j c) -> p j c", c=128), [bp_], [bdst])
                for n in range(NC_):
                    for h in range(2):
                        hb = 64 * h
                        ci = n * 2 + h
                        p_, bp_ = gbank()
                        csl = slice(n * 64, (n + 1) * 64)
                        kb.mm(p_[0:64, 0:128], btl[hb:hb + 64, csl], AR[hb:hb + 64, n, :], [bbt, bAR], [bp_])
                        kb.mm(p_[0:64, 128:256], ktl[hb:hb + 64, csl], AR[hb:hb + 64, n, :], [bkt, bAR], [bp_])
                        kb.mm(p_[0:64, 256:320], AR[hb:hb + 64, n, 0:64], btl[hb:hb + 64, csl], [bbt, bAR], [bp_])
                        kb.tt(AM[:, ci, :], p_[0:64, 0:320], c['m320'], ALU.mult, [bp_, c['buf']], [bAM[ci]])
                for g in range(NCH // 8):
                    gs_ = slice(g * 8, (g + 1) * 8)
                    rb = [bAM[ci] for ci in range(g * 8, (g + 1) * 8)]
                    kb.cp(Bm[0][:, gs_, :], AM[:, gs_, 256:320], rb, [bBm[0][g]])
                    kb.act(Btm[0][:, gs_, :], AM[:, gs_, 0:64], AF.Copy, rb, [bBtm[0][g]])
                    kb.tt(Mt[0][:, gs_, :], AM[:, gs_, 0:64], c['ident64'].unsqueeze(1).broadcast_to([64, 8, 64]), ALU.add,
                          rb + [c['buf']], [bMt[0][g]])
                cur = 0
                for step in range(5):
                    nxt = 1 - cur
                    for g in range(NCH // 8):
                        gs_ = slice(g * 8, (g + 1) * 8)
                        pB, pBt, pM = pinv
                        for j in range(8):
                            ci = g * 8 + j
                            kb.mm(pB[:, j * 64:(j + 1) * 64], Btm[cur][:, ci, :], Bm[cur][:, ci, :], [bBtm[cur][g], bBm[cur][g]], [bpinv[0]])
                        if step < 4:
                            for j in range(8):
                                ci = g * 8 + j
                                kb.mm(pBt[:, j * 64:(j + 1) * 64], Bm[cur][:, ci, :], Btm[cur][:, ci, :], [bBtm[cur][g], bBm[cur][g]], [bpinv[1]])
                        kb.cp(Bm[nxt][:, gs_, :], pB[:].rearrange("p (j c) -> p j c", c=64), [bpinv[0]], [bBm[nxt][g]])
                        if step < 4:
                            kb.act(Btm[nxt][:, gs_, :], pBt[:].rearrange("p (j c) -> p j c", c=64), AF.Copy, [bpinv[1]], [bBtm[nxt][g]])
                        for j in range(8):
                            ci = g * 8 + j
                            kb.mm(pM[:, j * 64:(j + 1) * 64], Bm[nxt][:, ci, :], Mt[cur][:, ci, :], [bBm[nxt][g], bMt[cur][g]], [bpinv[2]])
                        kb.tt(Mt[nxt][:, gs_, :], Mt[cur][:, gs_, :], pM[:].rearrange("p (j c) -> p j c", c=64), ALU.add,
                              [bpinv[2], bMt[cur][g]], [bMt[nxt][g]])
                    cur = nxt
                MtF = Mtb
                bMtF = bMtb
                for g in range(NCH // 8):
                    gs_ = slice(g * 8, (g + 1) * 8)
                    kb.cp(Mtb[:, gs_, :], Mt[cur][:, gs_, :], [bMt[cur][g]], [bMtb[g]])
                    kb.act(AMb[:, gs_, :], AM[:, gs_, 64:256], AF.Copy, [bAM[ci] for ci in range(g * 8, (g + 1) * 8)], [bAMb[g]])
                for n in range(NC_):
                    Tc = TS[tsi % 2]
                    bTc = bTS[tsi % 2]
                    Tn = TS[(tsi + 1) % 2]
                    bTn = bTS[(tsi + 1) % 2]
                    Tcb, bTcb = Tb[tsi % 2], bTb[tsi % 2]
                    Tnb, bTnb = Tb[(tsi + 1) % 2], bTb[(tsi + 1) % 2]
                    tsi += 1
                    for h in range(2):
                        hb = 64 * h
                        ci = n * 2 + h
                        kb.mm(pW[:, h * 64:(h + 1) * 64], ARh[h][0:64, n, 0:64], Tcb[:, h, :], [bARh[h], bTcb], [bpW], start=True, stop=False)
                        kb.mm(pW[:, h * 64:(h + 1) * 64], AMb[:, ci, 64:128], Vt[:, n, hb:hb + 64], [bAMb[ci // 8], bVt], [bpW], start=False, stop=True)
                    kb.cp(Wsb[:], pW[:], [bpW], [bWsb])
                    for h in range(2):
                        ci = n * 2 + h
                        kb.mm(pW[:, h * 64:(h + 1) * 64], MtF[:, ci, :], Wsb[:, h * 64:(h + 1) * 64], [bMtF[ci // 8], bWsb], [bpW])
                    kb.cp(Usb[:], pW[:], [bpW], [bUsb])
                    q4 = n % 4
                    for h in range(2):
                        hb = 64 * h
                        ci = n * 2 + h
                        ysl = slice((q4 * 2 + h) * 64, (q4 * 2 + h + 1) * 64)
                        kb.mm(pY[:, ysl], ARh[h][0:64, n, 64:128], Tcb[:, h, :], [bARh[h], bTcb], [bpY], start=True, stop=False)
                        kb.mm(pY[:, ysl], AMb[:, ci, 0:64], Usb[:, h * 64:(h + 1) * 64], [bAMb[ci // 8], bUsb], [bpY], start=False, stop=False)
                        kb.mm(pY[:, ysl], AMb[:, ci, 128:192], Vt[:, n, hb:hb + 64], [bAMb[ci // 8], bVt], [bpY], start=False, stop=True)
                    for h in range(2):
                        hb = 64 * h
                        kb.mm(pT[:, h * 64:(h + 1) * 64], BHt[:, n, hb:hb + 64], Usb[:, h * 64:(h + 1) * 64], [bBHt, bUsb], [bpT], start=True, stop=False)
                        kb.mm(pT[:, h * 64:(h + 1) * 64], KHt[:, n, hb:hb + 64], Vt[:, n, hb:hb + 64], [bKHt, bVt], [bpT], start=False, stop=True)
                    for h in range(2):
                        kb.stt(Tnb[:, h, :], Tc[:, h, :], GCh[h][0:64, n:n + 1], pT[:, h * 64:(h + 1) * 64], ALU.mult, ALU.add,
                               [bTc, bGCh[h], bpT], [bTnb])
                    for h in range(2):
                        kb.stt(Tn[:, h, :], Tc[:, h, :], GCh[h][0:64, n:n + 1], pT[:, h * 64:(h + 1) * 64], ALU.mult, ALU.add,
                               [bTc, bGCh[h], bpT], [bTn])
                    if q4 == 3:
                        kb.act(ysb[:], pY[:].rearrange("p (g v) -> p g v", v=64), AF.Copy, [bpY], [bysb])
                        kb.P.op('dve', lambda e: e.tensor_reduce(out=st1[:], in_=ysb[:], axis=AX.X, op=ALU.add), reads=[bysb], writes=[bst1])
                        kb.ts(st1[:], st1[:], 1.0 / 64, None, ALU.mult, None, [bst1], [bst1])
                        kb.tt(ysb[:], ysb[:], st1[:].unsqueeze(2).broadcast_to([64, 8, 64]), ALU.subtract, [bysb, bst1], [bysb])
                        kb.act(ysq[:], ysb[:], AF.Square, [bysb], [bysq])
                        kb.P.op('dve', lambda e: e.tensor_reduce(out=st2[:], in_=ysq[:], axis=AX.X, op=ALU.add), reads=[bysq], writes=[bst2])
                        kb.ts(st2[:], st2[:], 1.0 / 64, RWKV_LN_EPS, ALU.mult, ALU.add, [bst2], [bst2])
                        kb.act(st2[:], st2[:], AF.Sqrt, [bst2], [bst2])
                        kb.recip(st2[:], st2[:], [bst2], [bst2])
                        kb.tt(ysb[:], ysb[:], st2[:].unsqueeze(2).broadcast_to([64, 8, 64]), ALU.mult, [bysb, bst2], [bysb])
                        p_, bp_ = gbank()
                        for j in range(4):
                            kb.tr(p_[:, j * 64:(j + 1) * 64], ysb[:, 2 * j:2 * j + 2, :].rearrange("p g v -> p (g v)"), c['ident64'],
                                  [bysb, c['buf']], [bp_])
                        tsl = slice((n - 3) * 64, (n + 1) * 64)
                        kb.ts(yfm[:], p_[:, 0:256], pv[:, hp, 9:10], pv[:, hp, 10:11], ALU.mult, ALU.add, [bp_, bpv], [byfm])
                        kb.tt(yfm[:], yfm[:], bon[:, tsl], ALU.add, [byfm, bbon], [byfm])
                        kb.tt(yc[:, tsl], yfm[:], gf[:, tsl], ALU.mult, [byfm, bgf], [byc])
                kb.store(YT[4 + hp][:, t0:t0 + SEG], yc[:], [byc], q='pool')


def gen_D(kb, l, PT, YT):
    P = kb.P
    S = kb.S
    c = kb.c
    NQ = S // 128
    WMAX = 1024
    NB = 3
    PC = min(512, S)
    with contextlib.nullcontext():
        stg = P.sb([128, PC])
        bstg = Buf()
        qb = P.sb([128, S], BF16)
        kbf = P.sb([128, S], BF16)
        bqb, bkb = Buf(), Buf()
        vtm = P.sb([128, NQ, 128], BF16)
        bvtm = Buf()
        yd = P.sb([128, S], BF16)
        byd = Buf()
        pz = [P.ps([128, WMAX])]
        bpz = [Buf()]
        pt = [P.ps([128, WMAX], BF16)]
        bpt = [Buf()]
        po = [P.ps([128, 512])]
        bpo = [Buf()]
        kp = [P.sb([128, WMAX]) for _ in range(NB)]
        bkp = [Buf() for _ in range(NB)]
        Pb = [P.sb([128, WMAX + 1]) for _ in range(NB)]
        bPb = [Buf() for _ in range(NB)]
        wt = [P.sb([128, WMAX], BF16) for _ in range(NB)]
        bwt = [Buf() for _ in range(NB)]
        wT = [P.sb([128, WMAX], BF16) for _ in range(2)]
        bwT = [Buf(), Buf()]
        for hp in range(2):
            for t0 in range(0, S, PC):
                sl = slice(t0, t0 + PC)
                kb.load(stg[:], PT[17 + hp][:, sl], [bstg])
                kb.act(qb[:, sl], stg[:], AF.Copy, [bstg], [bqb], scale=0.125)
                kb.load(stg[:], PT[19 + hp][:, sl], [bstg])
                kb.cp(kbf[:, sl], stg[:], [bstg], [bkb])
            it = 0
            for t0 in range(0, S, PC):
                kb.load(stg[:], PT[21 + hp][:, t0:t0 + PC], [bstg])
                for g in range(PC // 512):
                    pa = pz[0]
                    bpa = bpz[0]
                    it += 1
                    for j in range(4):
                        kb.tr(pa[:, j * 128:(j + 1) * 128], stg[:, (g * 4 + j) * 128:(g * 4 + j + 1) * 128], c['ident'], [bstg, c['buf']], [bpa])
                    n0 = t0 // 128 + g * 4
                    kb.cp(vtm[:, n0:n0 + 4, :], pa[:, 0:512].rearrange("p (j c) -> p j c", c=128), [bpa], [bvtm])
            units = []
            io = 0
            for h in range(2):
                for qi in range(NQ):
                    kend = (qi + 1) * 128
                    nsb = (kend + WMAX - 1) // WMAX
                    for s in range(nsb - 1, -1, -1):
                        k0 = s * WMAX
                        k1 = min(kend, k0 + WMAX)
                        units.append(dict(h=h, qi=qi, k0=k0, W=k1 - k0, diag=(s == nsb - 1), first=(s == nsb - 1), last=(s == 0), io=io))
                    io += 1
            NU = len(units)

            def S1(i):
                u = units[i]
                hb = 64 * u['h']
                W = u['W']
                z, bz = pz[0], bpz[0]
                k_, bk_ = kp[i % NB], bkp[i % NB]
                for b0 in range(0, W, 512):
                    b1 = min(W, b0 + 512)
                    kb.mm(z[:, b0:b1], qb[hb:hb + 64, u['qi'] * 128:(u['qi'] + 1) * 128], kbf[hb:hb + 64, u['k0'] + b0:u['k0'] + b1],
                          [bqb, bkb], [bz])
                kb.act(k_[:, 0:W], z[:, 0:W], AF.Sigmoid, [bz], [bk_], scale=-1.0)
                if u['diag']:
                    kb.tt(k_[:, W - 128:W], k_[:, W - 128:W], c['mdiag_inv'], ALU.max, [bk_, c['buf']], [bk_])

            def S2(i):
                u = units[i]
                W = u['W']
                k_, bk_ = kp[i % NB], bkp[i % NB]
                p_, bp_ = Pb[i % NB], bPb[i % NB]
                if u['first']:
                    init = 1.0
                    rds = [bk_]
                    kb.memset(p_[:, W:W + 1], 1.0, [bp_], eng='pool')
                else:
                    pp_, bpp_ = Pb[(i - 1) % NB], bPb[(i - 1) % NB]
                    init = pp_[:, 0:1]
                    rds = [bk_, bpp_]
                    kb.act(p_[:, W:W + 1], pp_[:, 0:1], AF.Copy, [bpp_], [bp_])
                kb.P.op('dve', lambda e, p_=p_, k_=k_, W=W, init=init: e.tensor_tensor_scan(
                    out=p_[:, 0:W][:, ::-1], data0=k_[:, 0:W][:, ::-1], data1=c['ones_col'].broadcast_to([128, W]),
                    initial=init, op0=ALU.mult, op1=ALU.mult), reads=rds + [c['buf']], writes=[bp_])

            def S2b(i):
                u = units[i]
                W = u['W']
                p_, bp_ = Pb[i % NB], bPb[i % NB]
                w_, bw_ = wt[i % NB], bwt[i % NB]
                hW = 128 if W >= 256 else 0
                if hW > 0:
                    kb.tt(w_[:, 0:hW], p_[:, 1:hW + 1], p_[:, 0:hW], ALU.subtract, [bp_], [bw_])
                kb.tt(w_[:, hW:W], p_[:, hW + 1:W + 1], p_[:, hW:W], ALU.subtract, [bp_], [bw_], eng='pool')

            def S3(i):
                u = units[i]
                hb = 64 * u['h']
                W = u['W']
                w_, bw_ = wt[i % NB], bwt[i % NB]
                T_, bT_ = pt[0], bpt[0]
                wT_, bwT_ = wT[i % 2], bwT[i % 2]
                pot, bpot = po[0], bpo[0]
                nblk = W // 128
                for bi in range(nblk):
                    kb.tr(T_[:, bi * 128:(bi + 1) * 128], w_[:, bi * 128:(bi + 1) * 128], c['ident_b'], [bw_, c['bufb']], [bT_])
                kb.act(wT_[:, 0:W], T_[:, 0:W], AF.Copy, [bT_], [bwT_])
                for bi in range(nblk):
                    kb.mm(pot[hb:hb + 64, 0:128], vtm[:, u['k0'] // 128 + bi, hb:hb + 64], wT_[:, bi * 128:(bi + 1) * 128],
                          [bvtm, bwT_], [bpot], start=(u['first'] and bi == 0), stop=(u['last'] and bi == nblk - 1))
                if u['last']:
                    kb.act(yd[hb:hb + 64, u['qi'] * 128:(u['qi'] + 1) * 128], pot[hb:hb + 64, 0:128], AF.Copy, [bpot], [byd])

            for t in range(NU + 3):
                if 0 <= t - 1 < NU:
                    S2(t - 1)
                if t < NU:
                    S1(t)
                if 0 <= t - 2 < NU:
                    S2b(t - 2)
                if 0 <= t - 3 < NU:
                    S3(t - 3)
                yield
            kb.store(YT[6 + hp], yd[:], [byd], q='pool')


def gen_C(kb, l, PT, YT):
    P = kb.P
    S = kb.S
    c = kb.c
    SEG = min(256, S)
    TW = min(512, SEG)
    NC_ = SEG // 64
    NCH = NC_ * 2
    with contextlib.nullcontext():
        pv = P.sb([128, 2, 16])
        bpv = Buf()
        kb.load(pv[:, :, 0:11], kb.d['pvC'][l].rearrange("h p n -> p h n"), [bpv])
        for hp in range(2):
            kb.ts(pv[:, hp, 11:15], pv[:, hp, 0:4], -1.0, 1.0, ALU.mult, ALU.add, [bpv], [bpv])
            kb.ts(pv[:, hp, 15:16], pv[:, hp, 7:8], -1.0, 1.0, ALU.mult, ALU.add, [bpv], [bpv])
        wag = P.sb([128, 2, 128])
        bwag = Buf()
        kb.load(wag[:], kb.d['wC'][l].rearrange("h p n -> p h n"), [bwag])
        rmask = P.sb([128, SEG])
        brm = Buf()
        kb.memset(rmask[:], 1.0, [brm])
        kb.memset(rmask[:].rearrange("p (n c) -> p n c", c=64)[:, :, 0:1], 0.0, [brm])

        def T():
            return P.sb([128, SEG]), Buf()
        raw = [(P.sb([128, SEG + 1]), Buf()) for _ in range(4)]
        sh = [T() for _ in range(4)]
        (ld, bld), (al, bal), (gf, bgf), (kk, bkk), (km, bkm), (bon, bbon) = T(), T(), T(), T(), T(), T()
        (Lc, bLc), (t1, bt1), (t2, bt2), (bv_, bbv) = T(), T(), T(), T()
        AR = P.sb([128, NC_, 128])
        bAR = Buf()
        (btl, bbt), (ktl, bkt), (bh, bbh), (kh, bkh) = T(), T(), T(), T()
        BHt = P.sb([64, NC_, 128])
        KHt = P.sb([64, NC_, 128])
        Vt = P.sb([64, NC_, 128])
        bBHt, bKHt, bVt = Buf(), Buf(), Buf()
        AM = P.sb([64, NCH, 320])
        bAM = [Buf() for _ in range(NCH)]
        Bm = [P.sb([64, NCH, 64]) for _ in range(2)]
        Btm = [P.sb([64, NCH, 64]) for _ in range(2)]
        Mt = [P.sb([64, NCH, 64]) for _ in range(2)]
        bBm = [[Buf() for _ in range(NCH // 4)] for _ in range(2)]
        bBtm = [[Buf() for _ in range(NCH // 4)] for _ in range(2)]
        bMt = [[Buf() for _ in range(NCH // 4)] for _ in range(2)]
        TS = [P.sb([64, 2, 64]) for _ in range(2)]
        AR1 = P.sb([64, NC_, 128])
        bAR1 = Buf()
        GCc = P.sb([128, NC_])
        bGCc = Buf()
        G1 = P.sb([64, NC_])
        bG1 = Buf()
        bTS = [Buf(), Buf()]
        Wsb = P.sb([64, 128])
        bWsb = Buf()
        Usb = P.sb([64, 128])
        bUsb = Buf()
        ysb = P.sb([64, 8, 64])
        bysb = Buf()
        ysq = P.sb([64, 8, 64])
        bysq = Buf()
        st1 = P.sb([64, 8])
        st2 = P.sb([64, 8])
        bst1, bst2 = Buf(), Buf()
        yc = P.sb([128, SEG], BF16)
        byc = Buf()
        yfm = P.sb([128, 256])
        byfm = Buf()
        pg = [P.ps([128, 512])]
        bpg = [Buf()]
        bankX = P.ps([64, 512])
        bankY = P.ps([64, 512])
        pinv = [bankX[:, 0:256], bankX[:, 256:512], bankY[:, 0:256]]
        bbX = Buf()
        bbY = Buf()
        bpinv = [bbX, bbX, bbY]
        pW = bankY[:, 256:384]
        bpW = bbY
        pT = bankY[:, 384:512]
        bpT = bbY
        pY = P.ps([64, 512])
        bpY = Buf()
        ig = [0]

        def gbank():
            i = 0
            ig[0] += 1
            return pg[i], bpg[i]

        for hp in range(2):
            chans = [10 + hp, 12 + hp, 14 + hp, 16]
            kb.memset(TS[0][:], 0.0, [bTS[0]])
            tsi = 0
            for sg in range(S // SEG):
                t0 = sg * SEG
                for i in range(4):
                    rt_, br_ = raw[i]
                    if t0 == 0:
                        kb.memset(rt_[:, 0:1], 0.0, [br_])
                        kb.load(rt_[:, 1:SEG + 1], PT[chans[i]][:, 0:SEG], [br_])
                    else:
                        kb.load(rt_[:, :], PT[chans[i]][:, t0 - 1:t0 + SEG], [br_])
                    s_, bs_ = sh[i]
                    mcol = 3 if i == 3 else i
                    kb.ts(s_[:], rt_[:, 1:SEG + 1], pv[:, hp, 11 + mcol:12 + mcol], None, ALU.mult, None, [br_, bpv], [bs_])
                    kb.stt(s_[:], rt_[:, 0:SEG], pv[:, hp, mcol:mcol + 1], s_[:], ALU.mult, ALU.add, [br_, bpv, bs_], [bs_])
                (rs, brs), (ks, bks), (vs, bvs), (xs, bxs) = sh
                yield
                kb.act(t1[0:32, :], xs[0:32, :], AF.Tanh, [bxs], [bt1])
                kb.act(t2[64:128, :], xs[64:128, :], AF.Sigmoid, [bxs], [bt2])
                for ti in range(SEG // TW):
                    sl = slice(ti * TW, (ti + 1) * TW)
                    p_, bp_ = gbank()
                    kb.mm(p_[:, 0:TW], wag[0:32, hp, :], t1[0:32, sl], [bwag, bt1], [bp_])
                    kb.act(ld[:, sl], p_[:, 0:TW], AF.Sigmoid, [bp_, bpv], [bld], bias=pv[:, hp, 4:5])
                    p_, bp_ = gbank()
                    kb.mm(p_[:, 0:TW], wag[32:64, hp, :], xs[32:64, sl], [bwag, bxs], [bp_])
                    kb.act(al[:, sl], p_[:, 0:TW], AF.Sigmoid, [bp_, bpv], [bal], bias=pv[:, hp, 5:6])
                    p_, bp_ = gbank()
                    kb.mm(p_[:, 0:TW], wag[64:128, hp, :], t2[64:128, sl], [bwag, bt2], [bp_])
                    kb.cp(gf[:, sl], p_[:, 0:TW], [bp_], [bgf])
                kb.ts(ld[:], ld[:], -0.6065306597126334, None, ALU.mult, None, [bld], [bld])
                kb.ts(kk[:], ks[:], pv[:, hp, 6:7], None, ALU.mult, None, [bks, bpv], [bkk])
                kb.tt(t1[:], kk[:], kk[:], ALU.mult, [bkk], [bt1])
                for ti in range(SEG // TW):
                    sl = slice(ti * TW, (ti + 1) * TW)
                    p_, bp_ = gbank()
                    kb.mm(p_[:, 0:TW], c['bd'], t1[:, sl], [c['buf'], bt1], [bp_])
                    kb.act(t2[:, sl], p_[:, 0:TW], AF.Sqrt, [bp_], [bt2])
                kb.ts(t2[:], t2[:], 1e-12, None, ALU.max, None, [bt2], [bt2])
                kb.recip(t2[:], t2[:], [bt2], [bt2])
                kb.tt(kk[:], kk[:], t2[:], ALU.mult, [bkk, bt2], [bkk])
                kb.ts(km[:], al[:], pv[:, hp, 7:8], pv[:, hp, 15:16], ALU.mult, ALU.add, [bal, bpv], [bkm])
                kb.tt(km[:], km[:], ks[:], ALU.mult, [bkm, bks], [bkm])
                kb.stt(t1[:], rs[:], pv[:, hp, 8:9], km[:], ALU.mult, ALU.mult, [brs, bpv, bkm], [bt1])
                for ti in range(SEG // TW):
                    sl = slice(ti * TW, (ti + 1) * TW)
                    p_, bp_ = gbank()
                    kb.mm(p_[:, 0:TW], c['bd'], t1[:, sl], [c['buf'], bt1], [bp_])
                    kb.tt(bon[:, sl], p_[:, 0:TW], vs[:, sl], ALU.mult, [bp_, bvs], [bbon])
                kb.P.op('dve', lambda e: e.tensor_tensor_scan(out=Lc[:], data0=rmask[:], data1=ld[:], initial=0.0,
                                                              op0=ALU.mult, op1=ALU.add), reads=[brm, bld], writes=[bLc])
                yield
                AR4 = AR[:].rearrange("p n (two c) -> p n two c", two=2)
                v3 = lambda t: t[:].rearrange("p (n c) -> p n c", c=64)
                kb.act(t1[:], Lc[:], AF.Exp, [bLc], [bt1])
                kb.tt(AR4[:, :, 1, :], v3(rs), v3(t1), ALU.mult, [brs, bt1], [bAR])
                kb.tt(t2[:], Lc[:], ld[:], ALU.subtract, [bLc, bld], [bt2])
                kb.act(t2[:], t2[:], AF.Exp, [bt2], [bt2])
                kb.stt(AR4[:, :, 0, :], v3(kk), -1.0, v3(t2), ALU.mult, ALU.mult, [bkk, bt2], [bAR])
                kb.tt(bv_[:], kk[:], al[:], ALU.mult, [bkk, bal], [bbv])
                kb.act(t2[:], Lc[:], AF.Exp, [bLc], [bt2], scale=-1.0)
                kb.tt(btl[:], bv_[:], t2[:], ALU.mult, [bbv, bt2], [bbt])
                kb.tt(ktl[:], km[:], t2[:], ALU.mult, [bkm, bt2], [bkt])
                LCb = v3(Lc)[:, :, 63:64].broadcast_to([128, NC_, 64])
                kb.tt(v3(t2), LCb, v3(Lc), ALU.subtract, [bLc], [bt2])
                kb.act(t2[:], t2[:], AF.Exp, [bt2], [bt2])
                kb.tt(bh[:], bv_[:], t2[:], ALU.mult, [bbv, bt2], [bbh])
                kb.tt(kh[:], km[:], t2[:], ALU.mult, [bkm, bt2], [bkh])
                kb.cp(GCc[:], v3(t1)[:, :, 63], [bt1], [bGCc])
                kb.load(G1[:], GCc[64:128, :], [bG1], r=[bGCc])
                kb.load(AR1[:], AR[64:128, :, :], [bAR1], r=[bAR])
                ARh = [AR, AR1]
                bARh = [bAR, bAR1]
                GCh = [GCc, G1]
                bGCh = [bGCc, bG1]
                yield
                for (src, bsrc, dst, bdst) in ((bh, bbh, BHt, bBHt), (kh, bkh, KHt, bKHt), (vs, bvs, Vt, bVt)):
                    for g in range(NC_ // 4):
                        p_, bp_ = gbank()
                        for j in range(4):
                            n = g * 4 + j
                            kb.tr(p_[0:64, j * 128:(j + 1) * 128], src[:, n * 64:(n + 1) * 64], c['ident'], [bsrc, c['buf']], [bp_])
                        kb.cp(dst[:, g * 4:(g + 1) * 4, :], p_[0:64, :].rearrange("p (j c) -> p j c", c=128), [bp_], [bdst])
                for n in range(NC_):
                    for h in range(2):
                        hb = 64 * h
                        ci = n * 2 + h
                        p_, bp_ = gbank()
                        csl = slice(n * 64, (n + 1) * 64)
                        kb.mm(p_[0:64, 0:128], btl[hb:hb + 64, csl], AR[hb:hb + 64, n, :], [bbt, bAR], [bp_])
                        kb.mm(p_[0:64, 128:256], ktl[hb:hb + 64, csl], AR[hb:hb + 64, n, :], [bkt, bAR], [bp_])
                        kb.mm(p_[0:64, 256:320], AR[hb:hb + 64, n, 0:64], btl[hb:hb + 64, csl], [bbt, bAR], [bp_])
                        kb.tt(AM[:, ci, :], p_[0:64, 0:320], c['m320'], ALU.mult, [bp_, c['buf']], [bAM[ci]])
                        yield
                for g in range(NCH // 4):
                    gs_ = slice(g * 4, (g + 1) * 4)
                    rb = [bAM[ci] for ci in range(g * 4, (g + 1) * 4)]
                    kb.cp(Bm[0][:, gs_, :], AM[:, gs_, 256:320], rb, [bBm[0][g]])
                    kb.cp(Btm[0][:, gs_, :], AM[:, gs_, 0:64], rb, [bBtm[0][g]], eng='pool')
                    kb.tt(Mt[0][:, gs_, :], AM[:, gs_, 0:64], c['ident64'].unsqueeze(1).broadcast_to([64, 4, 64]), ALU.add,
                          rb + [c['buf']], [bMt[0][g]])
                cur = 0
                for step in range(5):
                    nxt = 1 - cur
                    for g in range(NCH // 4):
                        gs_ = slice(g * 4, (g + 1) * 4)
                        pB, pBt, pM = pinv
                        for j in range(4):
                            ci = g * 4 + j
                            kb.mm(pB[:, j * 64:(j + 1) * 64], Btm[cur][:, ci, :], Bm[cur][:, ci, :], [bBtm[cur][g], bBm[cur][g]], [bpinv[0]])
                        if step < 4:
                            for j in range(4):
                                ci = g * 4 + j
                                kb.mm(pBt[:, j * 64:(j + 1) * 64], Bm[cur][:, ci, :], Btm[cur][:, ci, :], [bBtm[cur][g], bBm[cur][g]], [bpinv[1]])
                        kb.cp(Bm[nxt][:, gs_, :], pB[:].rearrange("p (j c) -> p j c", c=64), [bpinv[0]], [bBm[nxt][g]])
                        if step < 4:
                            kb.cp(Btm[nxt][:, gs_, :], pBt[:].rearrange("p (j c) -> p j c", c=64), [bpinv[1]], [bBtm[nxt][g]])
                        for j in range(4):
                            ci = g * 4 + j
                            kb.mm(pM[:, j * 64:(j + 1) * 64], Bm[nxt][:, ci, :], Mt[cur][:, ci, :], [bBm[nxt][g], bMt[cur][g]], [bpinv[2]])
                        kb.tt(Mt[nxt][:, gs_, :], Mt[cur][:, gs_, :], pM[:].rearrange("p (j c) -> p j c", c=64), ALU.add,
                              [bpinv[2], bMt[cur][g]], [bMt[nxt][g]])
                        yield
                    cur = nxt
                MtF = Mt[cur]
                bMtF = bMt[cur]
                for n in range(NC_):
                    Tc = TS[tsi % 2]
                    bTc = bTS[tsi % 2]
                    Tn = TS[(tsi + 1) % 2]
                    bTn = bTS[(tsi + 1) % 2]
                    tsi += 1
                    yield
                    for h in range(2):
                        hb = 64 * h
                        ci = n * 2 + h
                        kb.mm(pW[:, h * 64:(h + 1) * 64], ARh[h][0:64, n, 0:64], Tc[:, h, :], [bARh[h], bTc], [bpW], start=True, stop=False)
                        kb.mm(pW[:, h * 64:(h + 1) * 64], AM[:, ci, 128:192], Vt[:, n, hb:hb + 64], [bAM[ci], bVt], [bpW], start=False, stop=True)
                    kb.cp(Wsb[:], pW[:], [bpW], [bWsb])
                    for h in range(2):
                        ci = n * 2 + h
                        kb.mm(pW[:, h * 64:(h + 1) * 64], MtF[:, ci, :], Wsb[:, h * 64:(h + 1) * 64], [bMtF[ci // 4], bWsb], [bpW])
                    kb.cp(Usb[:], pW[:], [bpW], [bUsb])
                    q4 = n % 4
                    for h in range(2):
                        hb = 64 * h
                        ci = n * 2 + h
                        ysl = slice((q4 * 2 + h) * 64, (q4 * 2 + h + 1) * 64)
                        kb.mm(pY[:, ysl], ARh[h][0:64, n, 64:128], Tc[:, h, :], [bARh[h], bTc], [bpY], start=True, stop=False)
                        kb.mm(pY[:, ysl], AM[:, ci, 64:128], Usb[:, h * 64:(h + 1) * 64], [bAM[ci], bUsb], [bpY], start=False, stop=False)
                        kb.mm(pY[:, ysl], AM[:, ci, 192:256], Vt[:, n, hb:hb + 64], [bAM[ci], bVt], [bpY], start=False, stop=True)
                    for h in range(2):
                        hb = 64 * h
                        kb.mm(pT[:, h * 64:(h + 1) * 64], BHt[:, n, hb:hb + 64], Usb[:, h * 64:(h + 1) * 64], [bBHt, bUsb], [bpT], start=True, stop=False)
                        kb.mm(pT[:, h * 64:(h + 1) * 64], KHt[:, n, hb:hb + 64], Vt[:, n, hb:hb + 64], [bKHt, bVt], [bpT], start=False, stop=True)
                    for h in range(2):
                        kb.stt(Tn[:, h, :], Tc[:, h, :], GCh[h][0:64, n:n + 1], pT[:, h * 64:(h + 1) * 64], ALU.mult, ALU.add,
                               [bTc, bGCh[h], bpT], [bTn])
                    if q4 == 3:
                        kb.act(ysb[:], pY[:].rearrange("p (g v) -> p g v", v=64), AF.Copy, [bpY], [bysb])
                        kb.P.op('dve', lambda e: e.tensor_reduce(out=st1[:], in_=ysb[:], axis=AX.X, op=ALU.add), reads=[bysb], writes=[bst1])
                        kb.ts(st1[:], st1[:], 1.0 / 64, None, ALU.mult, None, [bst1], [bst1])
                        kb.tt(ysb[:], ysb[:], st1[:].unsqueeze(2).broadcast_to([64, 8, 64]), ALU.subtract, [bysb, bst1], [bysb])
                        kb.act(ysq[:], ysb[:], AF.Square, [bysb], [bysq])
                        kb.P.op('dve', lambda e: e.tensor_reduce(out=st2[:], in_=ysq[:], axis=AX.X, op=ALU.add), reads=[bysq], writes=[bst2])
                        kb.ts(st2[:], st2[:], 1.0 / 64, RWKV_LN_EPS, ALU.mult, ALU.add, [bst2], [bst2])
                        kb.act(st2[:], st2[:], AF.Sqrt, [bst2], [bst2])
                        kb.recip(st2[:], st2[:], [bst2], [bst2])
                        kb.tt(ysb[:], ysb[:], st2[:].unsqueeze(2).broadcast_to([64, 8, 64]), ALU.mult, [bysb, bst2], [bysb])
                        p_, bp_ = gbank()
                        for j in range(4):
                            kb.tr(p_[:, j * 64:(j + 1) * 64], ysb[:, 2 * j:2 * j + 2, :].rearrange("p g v -> p (g v)"), c['ident64'],
                                  [bysb, c['buf']], [bp_])
                        tsl = slice((n - 3) * 64, (n + 1) * 64)
                        kb.ts(yfm[:], p_[:, 0:256], pv[:, hp, 9:10], pv[:, hp, 10:11], ALU.mult, ALU.add, [bp_, bpv], [byfm])
                        kb.tt(yfm[:], yfm[:], bon[:, tsl], ALU.add, [byfm, bbon], [byfm])
                        kb.tt(yc[:, tsl], yfm[:], gf[:, tsl], ALU.mult, [byfm, bgf], [byc])
                kb.store(YT[4 + hp][:, t0:t0 + SEG], yc[:], [byc], q='pool')


def phase_CD(kb, l, PT, YT):
    P = kb.P
    S = kb.S
    with P.scope():
        gC = gen_C(kb, l, PT, YT)
        gD = gen_D(kb, l, PT, YT)
        SEG = min(256, S)
        nseg = S // SEG
        nc_ = SEG // 64
        totC = 2 * nseg * (3 + 2 * nc_ + 5 * (2 * nc_ // 4) + nc_)
        NQ = S // 128
        nu = sum((qi + 1 + 7) // 8 for qi in range(NQ)) * 2
        totD = 2 * (nu + 3)
        ratio = totC / float(totD)
        acc = 0.0
        doneC = doneD = False
        while not (doneC and doneD):
            if not doneD:
                try:
                    next(gD)
                except StopIteration:
                    doneD = True
            acc += ratio
            while (acc >= 1.0 or doneD) and not doneC:
                acc -= 1.0
                try:
                    next(gC)
                except StopIteration:
                    doneC = True


def phase_T2a(kb, l, XT, YT, H2, modT, XTsrc=None):
    P = kb.P
    S = kb.S
    TN = 512
    with P.scope():
        Wb = P.sb([128, 8, D_MODEL], BF16)
        bW = Buf()
        stages = [P.sb([128, 2048]) for _ in range(3)]
        bst = [Buf() for _ in range(3)]
        load_cast_weight(kb, Wb, bW, lambda k: kb.d['w_out'][l, k * 128:(k + 1) * 128, :], D_MODEL, 8, stages, bst)
        modt = P.sb([128, 48])
        g2 = P.sb([128, 8])
        gs = P.sb([128, 8])
        bmod = Buf()
        kb.load(modt[:], modT[l], [bmod])
        kb.load(g2[:], kb.d['norm2_gT'][l], [bmod])
        kb.stt(gs[:], modt[:, 32:40], 1.0, g2[:], ALU.add, ALU.mult, [bmod], [bmod])
        sh = modt[:, 24:32]
        gt = modt[:, 16:24]
        xts = [P.sb([128, 8, TN]) for _ in range(2)]
        bxs = [Buf() for _ in range(2)]
        yts = [P.sb([128, 8, TN], BF16) for _ in range(2)]
        bys = [Buf() for _ in range(2)]
        sq = P.sb([128, 8, TN], BF16)
        bsq = Buf()
        pss = P.ps([128, TN])
        bpss = Buf()
        rstd = P.sb([128, TN])
        brstd = Buf()
        tmp = P.sb([128, 8, TN])
        btmp = [Buf(), Buf()]
        hs = [P.sb([128, 8, TN], BF16) for _ in range(2)]
        bhs = [Buf() for _ in range(2)]
        pps = [P.ps([128, TN]) for _ in range(4)]
        bpp = [Buf() for _ in range(4)]
        io = 0
        for ti in range(S // TN):
            xt, bx = xts[ti % 2], bxs[ti % 2]
            yt, by = yts[ti % 2], bys[ti % 2]
            h, bh = hs[ti % 2], bhs[ti % 2]
            tsl = slice(ti * TN, (ti + 1) * TN)
            kb.load(xt[:], (XT if XTsrc is None else XTsrc)[:, :, tsl], [bx])
            kb.load(yt[:], YT[:, :, tsl].rearrange("k p t -> p k t"), [by])
            for j in range(8):
                pp, bp = pps[io % 4], bpp[io % 4]
                io += 1
                for k in range(8):
                    kb.mm(pp[:], Wb[:, k, j * 128:(j + 1) * 128], yt[:, k, :], [bW, by], [bp], start=(k == 0), stop=(k == 7))
                kb.stt(xt[:, j, :], pp[:], gt[:, j:j + 1], xt[:, j, :], ALU.mult, ALU.add, [bp, bmod, bx], [bx])
            kb.store(XT[:, :, tsl], xt[:], [bx], q='pool')
            emit_norm(kb, xt, bx, TN, gs, sh, bmod, sq, bsq, pss, bpss, rstd, brstd, tmp, btmp, h, bh)
            kb.store(H2[:, :, tsl], h[:], [bh], q='pool')


def phase_T2b(kb, l, XT, H2, modT):
    P = kb.P
    S = kb.S
    TN = 512
    NF = 2 * D_FF // 128
    NV = D_FF // 128
    with P.scope():
        Wu = P.sb([128, 8, 2 * D_FF], BF16)
        bWu = Buf()
        Wd = P.sb([128, NV, D_MODEL], BF16)
        bWd = Buf()
        with P.scope():
            stages = [P.sb([128, 1024]) for _ in range(3)]
            bst = [Buf() for _ in range(3)]
            load_cast_weight(kb, Wu, bWu, lambda k: kb.d['ffn_up'][l, k * 128:(k + 1) * 128, :], 2 * D_FF, 8, stages, bst)
            load_cast_weight(kb, Wd, bWd, lambda k: kb.d['ffn_down'][l, k * 128:(k + 1) * 128, :], D_MODEL, NV, stages, bst)
        modt = P.sb([128, 48])
        bmod = Buf()
        kb.load(modt[:], modT[l], [bmod])
        gt = modt[:, 40:48]
        cw = P.sb([128, 4, NF])
        bcw = Buf()
        kb.load(cw[:], kb.d['ffn_cwT'][l], [bcw])
        halo = P.sb([128, NF, 2])
        bhalo = Buf()
        kb.memset(halo[:], 0.0, [bhalo])
        xt = P.sb([128, 8, TN])
        bx = Buf()
        h = P.sb([128, 8, TN], BF16)
        bh = Buf()
        pps = [P.ps([128, 512]) for _ in range(4)]
        bpp = [Buf() for _ in range(4)]
        us = [P.sb([128, TN + 2]) for _ in range(3)]
        bus = [Buf() for _ in range(3)]
        uc = [P.sb([128, TN]) for _ in range(3)]
        buc = [Buf() for _ in range(3)]
        ov = [P.sb([128, TN]) for _ in range(2)]
        bov = [Buf() for _ in range(2)]
        g_ = P.sb([128, NV, TN], BF16)
        bg_ = Buf()
        io = 0
        iu = 0
        for ti in range(S // TN):
            tsl = slice(ti * TN, (ti + 1) * TN)
            kb.load(xt[:], XT[:, :, tsl], [bx])
            kb.load(h[:], H2[:, :, tsl], [bh])
            for i in range(NV):
                v_, bv_ = ov[i % 2], bov[i % 2]
                for f in (i, i + NV):
                    pp, bp = pps[io % 4], bpp[io % 4]
                    io += 1
                    u, bu = us[iu % 3], bus[iu % 3]
                    o, bo_ = uc[iu % 3], buc[iu % 3]
                    iu += 1
                    for k in range(8):
                        kb.mm(pp[:, 0:TN], Wu[:, k, f * 128:(f + 1) * 128], h[:, k, :], [bWu, bh], [bp], start=(k == 0), stop=(k == 7))
                    kb.cp(u[:, 0:2], halo[:, f, :], [bhalo], [bu])
                    kb.act(u[:, 2:TN + 2], pp[:, 0:TN], AF.Copy, [bp], [bu])
                    kb.act(halo[:, f, :], u[:, TN:TN + 2], AF.Copy, [bu], [bhalo])
                    kb.ts(o[:], u[:, 2:TN + 2], cw[:, 2, f:f + 1], cw[:, 3, f:f + 1], ALU.mult, ALU.add, [bu, bcw], [bo_])
                    kb.stt(o[:], u[:, 1:TN + 1], cw[:, 1, f:f + 1], o[:], ALU.mult, ALU.add, [bu, bcw, bo_], [bo_])
                    if f < NV:
                        kb.stt(v_[:], u[:, 0:TN], cw[:, 0, f:f + 1], o[:], ALU.mult, ALU.add, [bu, bcw, bo_], [bv_])
                    else:
                        kb.stt(o[:], u[:, 0:TN], cw[:, 0, f:f + 1], o[:], ALU.mult, ALU.add, [bu, bcw, bo_], [bo_])
                        kb.act(o[:], o[:], AF.Gelu, [bo_], [bo_])
                        kb.tt(g_[:, i, :], v_[:], o[:], ALU.mult, [bv_, bo_], [bg_])
            for j in range(8):
                pp, bp = pps[io % 4], bpp[io % 4]
                io += 1
                for k in range(NV):
                    kb.mm(pp[:, 0:TN], Wd[:, k, j * 128:(j + 1) * 128], g_[:, k, :], [bWd, bg_], [bp], start=(k == 0), stop=(k == NV - 1))
                kb.stt(xt[:, j, :], pp[:, 0:TN], gt[:, j:j + 1], xt[:, j, :], ALU.mult, ALU.add, [bp, bmod, bx], [bx])
            kb.store(XT[:, :, tsl], xt[:], [bx], q='pool')


PARAM_SPECS = None


def host_params(inp, L):
    f = lambda a: np.ascontiguousarray(np.asarray(a, dtype=np.float32))
    o = {}
    o['ada_w'] = f(inp['ada_w'][:L])
    o['ada_bT'] = f(np.asarray(inp['ada_b'])[:L].reshape(L, 48, 128).transpose(0, 2, 1))
    o['norm1_gT'] = f(np.asarray(inp['norm1_g'])[:L].reshape(L, 8, 128).transpose(0, 2, 1))
    o['norm2_gT'] = f(np.asarray(inp['norm2_g'])[:L].reshape(L, 8, 128).transpose(0, 2, 1))
    o['w_in'] = f(inp['w_in'][:L])
    o['w_out'] = f(inp['w_out'][:L])
    o['ffn_up'] = f(inp['ffn_up'][:L])
    o['ffn_down'] = f(inp['ffn_down'][:L])
    cw = np.asarray(inp['ffn_conv_w'])[:L]
    cb = np.asarray(inp['ffn_conv_b'])[:L]
    cwb = np.concatenate([cw, cb[:, None, :]], axis=1)
    o['ffn_cwT'] = f(cwb.reshape(L, 4, 44, 128).transpose(0, 3, 1, 2))
    qg = np.asarray(inp['attn_q_gain'])[:L]
    kg = np.asarray(inp['attn_k_gain'])[:L]
    o['pvA'] = f(np.stack([np.tile(qg, (1, 2)), np.tile(kg, (1, 2))], axis=-1))
    rb = np.asarray(inp['attn_rel_bias'])[:L]
    qi = np.arange(64)[None, None, :]
    jj = np.arange(9)[None, :, None]
    kj = np.arange(64)[:, None, None]
    dist = 512 + qi - (jj * 64 + kj)
    rel = np.clip(dist, -128, 128) + 128
    o['biasA'] = f(rb[:, :, rel])
    cwB = np.asarray(inp['lru_conv_w'])[:L]
    cols = [cwB[:, 0], cwB[:, 1], cwB[:, 2], cwB[:, 3], np.asarray(inp['lru_conv_b'])[:L], np.asarray(inp['lru_ra_b'])[:L],
            np.asarray(inp['lru_ri_b'])[:L], np.asarray(inp['lru_lambda'])[:L]]
    pvB = np.stack(cols, axis=-1)
    o['pvB'] = f(pvB.reshape(L, 2, 128, 8))
    ra = np.asarray(inp['lru_ra_w'])[:L]
    ri = np.asarray(inp['lru_ri_w'])[:L]
    wB = np.zeros((L, 2, 2, 128, 128), np.float32)
    for hp in range(2):
        for h in range(2):
            wB[:, hp, 0, h * 64:(h + 1) * 64, h * 64:(h + 1) * 64] = ra[:, hp * 2 + h]
            wB[:, hp, 1, h * 64:(h + 1) * 64, h * 64:(h + 1) * 64] = ri[:, hp * 2 + h]
    o['wB'] = wB
    mu = np.asarray(inp['rwkv_mu'])[:L]
    pvC = np.zeros((L, 2, 128, 11), np.float32)
    for hp in range(2):
        cs = slice(hp * 128, (hp + 1) * 128)
        pvC[:, hp, :, 0] = mu[:, 0:256][:, cs]
        pvC[:, hp, :, 1] = mu[:, 256:512][:, cs]
        pvC[:, hp, :, 2] = mu[:, 512:768][:, cs]
        pvC[:, hp, :, 3] = mu[:, 768:896]
        for i, nm in enumerate(['rwkv_w0', 'rwkv_a0', 'rwkv_k_k', 'rwkv_k_a', 'rwkv_r_k', 'rwkv_ln_w', 'rwkv_ln_b']):
            pvC[:, hp, :, 4 + i] = np.asarray(inp[nm])[:L][:, cs]
    o['pvC'] = pvC
    wC = np.zeros((L, 2, 128, 128), np.float32)
    for hp in range(2):
        cs = slice(hp * 128, (hp + 1) * 128)
        wC[:, hp, 0:32] = np.asarray(inp['rwkv_w2'])[:L][:, :, cs]
        wC[:, hp, 32:64] = np.asarray(inp['rwkv_a2'])[:L][:, :, cs]
        wC[:, hp, 64:128] = np.asarray(inp['rwkv_g2'])[:L][:, :, cs]
    o['wC'] = wC
    o['consts'] = make_consts()
    return o


def declare_params(kb, hp):
    for k, v in hp.items():
        kb.din(k, v.shape)


def build_full(S, L, shapes):
    nc = bass.Bass("TRN2", target_bir_lowering=False)
    kb = KB(nc, S, L)
    for k, shp in shapes.items():
        kb.din(k, shp)
    kb.din('cT', [128, 8])
    kb.din('xT_in', [128, 8, S])
    XT = kb.dout('xT', [128, 8, S])
    PT = kb.dscr('PT', [23, 128, S])
    YT = kb.dscr('YT', [8, 128, S], BF16)
    H2 = kb.dscr('H2', [128, 8, S], BF16)
    PTb = kb.dscr('PTb', [6, 128, S], BF16)
    modT = kb.dscr('modT', [L, 128, 48])
    load_consts(kb)
    P = kb.P
    phase_ada(kb, modT)
    for l in range(L):
        phase_T1(kb, l, XT if l > 0 else kb.d['xT_in'], PT, modT, PTb=PTb)
        phase_A(kb, l, PT, YT)
        phase_B(kb, l, PT, YT)
        phase_C(kb, l, PT, YT)
        phase_D(kb, l, PT, YT, PTb=PTb)
        phase_T2a(kb, l, XT, YT, H2, modT, XTsrc=(None if l > 0 else kb.d['xT_in']))
        phase_T2b(kb, l, XT, H2, modT)
    kb.P.emit()
    return nc


def kernel(**inputs):
    x = np.asarray(inputs['x'], dtype=np.float32)
    B, S, D = x.shape
    L = DEPTH
    hp = host_params(inputs, L)
    nc = build_full(S, L, {k: v.shape for k, v in hp.items()})
    c = np.asarray(inputs['c'], dtype=np.float32)
    in_maps = []
    ncores = 4
    for i in range(ncores):
        b = i % B
        m = dict(hp)
        m['cT'] = np.ascontiguousarray(c[b].reshape(8, 128).T)
        m['xT_in'] = np.ascontiguousarray(x[b].reshape(S, 8, 128).transpose(2, 1, 0))
        in_maps.append(m)
    res = run_bass_kernel_spmd(nc, in_maps, core_ids=list(range(ncores)))
    out = np.empty((B, S, D), np.float32)
    for b in range(B):
        xT = res.results[b]['xT']
        out[b] = xT.transpose(2, 1, 0).reshape(S, D)
    return out
```
